# Optimizing a Trainium2 kernel written in Bass

```python
import math
import jax, jax.numpy as jnp
from jax import lax
import numpy as np

D_MODEL = 2048
BATCH = 4
SEQ = 4096
DEPTH = 2

GRID_W = 64
CTX_LEN = 256
N_EVEN = (DEPTH + 1) // 2
N_ODD = DEPTH // 2
EPS = 1e-6

HEAD_DIM = 128
ATTN_HEADS = D_MODEL // (2 * HEAD_DIM)
ATTN_KV_HEADS = max(1, ATTN_HEADS // 4)
ATTN_GROUP = ATTN_HEADS // ATTN_KV_HEADS
ATTN_WIDTH = ATTN_HEADS * HEAD_DIM
KV_WIDTH = ATTN_KV_HEADS * HEAD_DIM
Q_BLOCK = 128
ROPE_THETA = 10000.0
AXIS_FREQS = HEAD_DIM // 4

S5_WIDTH = D_MODEL // 2
S5_GROUP = 16
S5_GROUPS = S5_WIDTH // S5_GROUP
S5_STATE = 64
AB_IN = ATTN_WIDTH + 2 * KV_WIDTH + S5_WIDTH
AB_OUT = ATTN_WIDTH + S5_WIDTH

HGRN_DK = 128
HGRN_HEADS = D_MODEL // HGRN_DK
HGRN_DV = D_MODEL // HGRN_HEADS
HGRN_KEY = HGRN_HEADS * HGRN_DK
HGRN_VAL = HGRN_HEADS * HGRN_DV
HGRN_CHUNK = 64
C_IN = 3 * HGRN_KEY + 2 * HGRN_VAL

D_FF = ((8 * D_MODEL) // 3 + 255) // 256 * 256

kernel_name = "hybrid_flow_backbone_attn_s5_hgrn2"

F32 = jnp.float32


def rmsnorm(x, g):
    xf = x.astype(F32)
    y = xf * lax.rsqrt(jnp.mean(xf * xf, axis=-1, keepdims=True) + EPS)
    return y * g.astype(F32)


def modulate(h, shift, scale):
    return h * (1.0 + scale) + shift


def axial_rope_tables(n_tokens):
    rows = n_tokens // GRID_W
    row = jnp.repeat(jnp.arange(rows, dtype=F32), GRID_W)
    col = jnp.tile(jnp.arange(GRID_W, dtype=F32), rows)
    inv = ROPE_THETA ** (-jnp.arange(AXIS_FREQS, dtype=F32) / AXIS_FREQS)
    ang = jnp.stack([row[:, None] * inv, col[:, None] * inv], axis=1)
    return jnp.cos(ang), jnp.sin(ang)


def apply_axial_rope(x, cos, sin):
    bsz, n, h, _ = x.shape
    xs = x.reshape(bsz, n, h, 2, 2, AXIS_FREQS)
    x1, x2 = xs[..., 0, :], xs[..., 1, :]
    c = cos[None, :, None]
    s = sin[None, :, None]
    out = jnp.stack([x1 * c - x2 * s, x2 * c + x1 * s], axis=-2)
    return out.reshape(bsz, n, h, HEAD_DIM)


def attend(q, k, v):
    s = jnp.einsum('bqhgd,bkhd->bhgqk', q, k, preferred_element_type=F32) * (HEAD_DIM ** -0.5)
    p = jax.nn.softmax(s, axis=-1)
    return jnp.einsum('bhgqk,bkhd->bqhgd', p, v.astype(F32))


def block_attention(q, k, v):
    bsz, n = q.shape[:2]
    qb = q.reshape(bsz, n // Q_BLOCK, Q_BLOCK, ATTN_KV_HEADS, ATTN_GROUP, HEAD_DIM).swapaxes(0, 1)
    ob = lax.map(lambda qi: attend(qi, k, v), qb)
    return ob.swapaxes(0, 1).reshape(bsz, n, ATTN_WIDTH)


def s5_discretise(lam_re, lam_im, log_dt, b_re, b_im):
    lam = lax.complex(jnp.minimum(lam_re.astype(F32), -1e-4), lam_im.astype(F32))
    dt = jnp.exp(log_dt.astype(F32))[:, None]
    lam_bar = jnp.exp(lam * dt)
    bmat = lax.complex(b_re.astype(F32), b_im.astype(F32))
    b_bar = ((lam_bar - 1.0) / lam)[..., None] * bmat
    return lam_bar, b_bar


def _linear_combine(e1, e2):
    a1, b1 = e1
    a2, b2 = e2
    return a1 * a2, a2 * b1 + b2


def s5_states(u, lam_bar, b_bar, h0, reverse):
    if reverse:
        u = jnp.flip(u, axis=1)
    bu = jnp.einsum('blgc,gpc->blgp', u.astype(jnp.complex64), b_bar)
    bu = bu.at[:, 0].add(lam_bar * h0)
    a = jnp.broadcast_to(lam_bar, (1, u.shape[1]) + lam_bar.shape)
    _, h = lax.associative_scan(_linear_combine, (a, bu), axis=1)
    h_last = h[:, -1]
    if reverse:
        h = jnp.flip(h, axis=1)
    return h, h_last


def s5_readout(h, cmat):
    return jnp.real(jnp.einsum('blgp,gcp->blgc', h, cmat))


def gla_chunk_scan(q, k, v, logf, s0, reverse):
    if reverse:
        q, k, v, logf = (jnp.flip(t, axis=1) for t in (q, k, v, logf))
    bsz, n, h, _ = q.shape
    nc = n // HGRN_CHUNK

    def chunks(t):
        return t.astype(F32).reshape(bsz, nc, HGRN_CHUNK, h, t.shape[-1]).transpose(1, 0, 3, 2, 4)

    mask = jnp.tril(jnp.ones((HGRN_CHUNK, HGRN_CHUNK), bool))

    def step(state, inp):
        qc, kc, vc, gc = inp
        b = jnp.cumsum(gc, axis=2)
        o = jnp.einsum('bhtk,bhkv->bhtv', qc * jnp.exp(b), state)
        diff = b[:, :, :, None, :] - b[:, :, None, :, :]
        decay = jnp.exp(jnp.where(mask[:, :, None], diff, -jnp.inf))
        scores = jnp.einsum('bhtk,bhsk,bhtsk->bhts', qc, kc, decay)
        o = o + jnp.einsum('bhts,bhsv->bhtv', scores, vc)
        b_last = b[:, :, -1:, :]
        state = jnp.exp(b_last[:, :, 0])[..., None] * state + jnp.einsum(
            'bhsk,bhsv->bhkv', kc * jnp.exp(b_last - b), vc)
        return state, o

    s_last, o = lax.scan(step, s0.astype(F32), (chunks(q), chunks(k), chunks(v), chunks(logf)))
    o = o.transpose(1, 0, 3, 2, 4).reshape(bsz, n, h, v.shape[-1])
    if reverse:
        o = jnp.flip(o, axis=1)
    return o, s_last


def mixer_ab(h_ctx, h_lat, cos, sin, w_in, w_out, q_norm_g, k_norm_g, lam_re, lam_im, log_dt,
             b_re, b_im, c_re, c_im, d_skip, glu_w, glu_b, need_ctx):
    split_at = [ATTN_WIDTH, ATTN_WIDTH + KV_WIDTH, ATTN_WIDTH + 2 * KV_WIDTH]

    def project(h):
        bsz, n = h.shape[:2]
        q, k, v, u = jnp.split(h @ w_in, split_at, axis=-1)
        q = rmsnorm(q.reshape(bsz, n, ATTN_HEADS, HEAD_DIM), q_norm_g)
        k = rmsnorm(k.reshape(bsz, n, ATTN_KV_HEADS, HEAD_DIM), k_norm_g)
        v = v.reshape(bsz, n, ATTN_KV_HEADS, HEAD_DIM)
        u = u.astype(F32).reshape(bsz, n, S5_GROUPS, S5_GROUP)
        return q, k, v, u

    def grp(t):
        return t.reshape(t.shape[:2] + (ATTN_KV_HEADS, ATTN_GROUP, HEAD_DIM))

    q_c, k_c, v_c, u_c = project(h_ctx)
    q_l, k_l, v_l, u_l = project(h_lat)
    q_l = apply_axial_rope(q_l, cos, sin)
    k_l = apply_axial_rope(k_l, cos, sin)
    bsz = h_lat.shape[0]

    k_all = jnp.concatenate([k_c, k_l], axis=1)
    v_all = jnp.concatenate([v_c.astype(F32), v_l.astype(F32)], axis=1)
    a_lat = block_attention(grp(q_l), k_all, v_all)

    d = d_skip.astype(F32).reshape(S5_GROUPS, S5_GROUP)
    y_l = d * u_l
    y_c = d * u_c if need_ctx else None
    h0 = jnp.zeros((bsz, S5_GROUPS, S5_STATE), jnp.complex64)
    for di, rev in enumerate((False, True)):
        lam_bar, b_bar = s5_discretise(lam_re[di], lam_im[di], log_dt[di], b_re[di], b_im[di])
        cmat = lax.complex(c_re[di].astype(F32), c_im[di].astype(F32))
        hc, hc_last = s5_states(u_c, lam_bar, b_bar, h0, rev)
        hl, _ = s5_states(u_l, lam_bar, b_bar, hc_last, rev)
        y_l = y_l + s5_readout(hl, cmat)
        if need_ctx:
            y_c = y_c + s5_readout(hc, cmat)

    def glu(y):
        z = jax.nn.gelu(y.reshape(y.shape[:2] + (S5_WIDTH,))) @ glu_w + glu_b
        a, g = jnp.split(z, 2, axis=-1)
        return a * jax.nn.sigmoid(g)

    out_l = jnp.concatenate([a_lat, glu(y_l)], axis=-1) @ w_out
    out_c = None
    if need_ctx:
        a_ctx = attend(grp(q_c), k_c, v_c).reshape(bsz, h_ctx.shape[1], ATTN_WIDTH)
        out_c = jnp.concatenate([a_ctx, glu(y_c)], axis=-1) @ w_out
    return out_c, out_l


def mixer_c(h_ctx, h_lat, w_in, w_out, lb, norm_g, need_ctx):
    split_at = [HGRN_KEY, 2 * HGRN_KEY, 3 * HGRN_KEY, 3 * HGRN_KEY + HGRN_VAL]

    def project(h):
        bsz, n = h.shape[:2]
        heads = lambda t, dim: t.reshape(bsz, n, HGRN_HEADS, dim)
        q, zf_f, zf_b, i, g = jnp.split((h @ w_in).astype(F32), split_at, axis=-1)
        q = heads(jax.nn.silu(q), HGRN_DK)
        v = heads(i, HGRN_DV)
        gates = []
        for di, zf in enumerate((zf_f, zf_b)):
            f = lb[di] + (1.0 - lb[di]) * jax.nn.sigmoid(zf)
            gates.append((heads(1.0 - f, HGRN_DK), heads(jnp.log(f), HGRN_DK)))
        return q, v, gates, g

    def readout(o_f, o_b, g):
        bsz, n = g.shape[:2]
        o = rmsnorm(o_f + o_b, norm_g) * jax.nn.silu(g).reshape(bsz, n, HGRN_HEADS, HGRN_DV)
        return o.reshape(bsz, n, HGRN_VAL) @ w_out

    q_c, v_c, gates_c, g_c = project(h_ctx)
    q_l, v_l, gates_l, g_l = project(h_lat)
    s0 = jnp.zeros((h_lat.shape[0], HGRN_HEADS, HGRN_DK, HGRN_DV), F32)
    outs_c, outs_l = [], []
    for di, rev in enumerate((False, True)):
        k_c, lf_c = gates_c[di]
        k_l, lf_l = gates_l[di]
        o_c, s_c = gla_chunk_scan(q_c, k_c, v_c, lf_c, s0, rev)
        o_l, _ = gla_chunk_scan(q_l, k_l, v_l, lf_l, s_c, rev)
        outs_c.append(o_c)
        outs_l.append(o_l)
    out_l = readout(outs_l[0], outs_l[1], g_l)
    out_c = readout(outs_c[0], outs_c[1], g_c) if need_ctx else None
    return out_c, out_l


def dwconv3(h, w, b):
    hp = jnp.pad(h, ((0, 0), (1, 1), (0, 0)))
    return hp[:, :-2] * w[0] + hp[:, 1:-1] * w[1] + hp[:, 2:] * w[2] + b


def conv_ffn(h, w_up, conv_w, conv_b, w_down):
    u = dwconv3(h @ w_up, conv_w, conv_b)
    a, v = jnp.split(u, 2, axis=-1)
    return (jax.nn.silu(a) * v) @ w_down


def setup_inputs(seed: int = 0) -> dict:
    key = jax.random.key(seed)
    ks = iter(jax.random.split(key, 40))

    def nrm(shape, scale):
        return scale * jax.random.normal(next(ks), shape, F32)

    G, P, C = S5_GROUPS, S5_STATE, S5_GROUP
    n_idx = jnp.arange(P, dtype=F32)
    return {
        "x": nrm((BATCH, SEQ, D_MODEL), 1.0),
        "c": nrm((BATCH, D_MODEL), 1.0),
        "ctx": nrm((BATCH, CTX_LEN, D_MODEL), 1.0),
        "c_ctx": nrm((D_MODEL,), 1.0),
        "mod_w": nrm((DEPTH, D_MODEL, 6 * D_MODEL), D_MODEL ** -0.5),
        "mod_b": nrm((DEPTH, 6 * D_MODEL), 0.01),
        "norm_g": 1.0 + nrm((DEPTH, 4, D_MODEL), 0.1),
        "ab_w_in": nrm((N_EVEN, D_MODEL, AB_IN), D_MODEL ** -0.5),
        "ab_w_out": nrm((N_EVEN, AB_OUT, D_MODEL), AB_OUT ** -0.5),
        "attn_q_norm": 1.0 + nrm((N_EVEN, HEAD_DIM), 0.1),
        "attn_k_norm": 1.0 + nrm((N_EVEN, HEAD_DIM), 0.1),
        "s5_lam_re": -0.5 + nrm((N_EVEN, 2, G, P), 0.01),
        "s5_lam_im": math.pi * n_idx + nrm((N_EVEN, 2, G, P), 0.01),
        "s5_log_dt": jax.random.uniform(next(ks), (N_EVEN, 2, G), F32, math.log(1e-3), math.log(1e-1)),
        "s5_b_re": nrm((N_EVEN, 2, G, P, C), (2.0 * C) ** -0.5),
        "s5_b_im": nrm((N_EVEN, 2, G, P, C), (2.0 * C) ** -0.5),
        "s5_c_re": nrm((N_EVEN, 2, G, C, P), 0.5),
        "s5_c_im": nrm((N_EVEN, 2, G, C, P), 0.5),
        "s5_d": nrm((N_EVEN, S5_WIDTH), 1.0),
        "s5_glu_w": nrm((N_EVEN, S5_WIDTH, 2 * S5_WIDTH), S5_WIDTH ** -0.5),
        "s5_glu_b": nrm((N_EVEN, 2 * S5_WIDTH), 0.01),
        "c_w_in": nrm((N_ODD, D_MODEL, C_IN), D_MODEL ** -0.5),
        "c_w_out": nrm((N_ODD, HGRN_VAL, D_MODEL), HGRN_VAL ** -0.5),
        "hgrn_lb_logits": nrm((DEPTH, 2, HGRN_KEY), 0.1),
        "hgrn_norm": 1.0 + nrm((N_ODD, HGRN_DV), 0.1),
        "ffn_w_up": nrm((DEPTH, D_MODEL, 2 * D_FF), D_MODEL ** -0.5),
        "ffn_conv_w": nrm((DEPTH, 3, 2 * D_FF), 3.0 ** -0.5),
        "ffn_conv_b": nrm((DEPTH, 2 * D_FF), 0.01),
        "ffn_w_down": nrm((DEPTH, D_FF, D_MODEL), D_FF ** -0.5),
    }


def reference(x, c, ctx, c_ctx, mod_w, mod_b, norm_g, ab_w_in, ab_w_out, attn_q_norm, attn_k_norm,
              s5_lam_re, s5_lam_im, s5_log_dt, s5_b_re, s5_b_im, s5_c_re, s5_c_im, s5_d,
              s5_glu_w, s5_glu_b, c_w_in, c_w_out, hgrn_lb_logits, hgrn_norm,
              ffn_w_up, ffn_conv_w, ffn_conv_b, ffn_w_down):
    n_tokens = x.shape[1]
    cos, sin = axial_rope_tables(n_tokens)
    lb_all = jnp.cumsum(jax.nn.softmax(hgrn_lb_logits.astype(F32), axis=0), axis=0)
    lb_all = lb_all - lb_all[:1]
    sc = jax.nn.silu(c)
    sc_ctx = jax.nn.silu(c_ctx)
    for l in range(DEPTH):
        last = l == DEPTH - 1
        m_lat = jnp.split((sc @ mod_w[l] + mod_b[l])[:, None, :], 6, axis=-1)
        m_ctx = jnp.split(sc_ctx @ mod_w[l] + mod_b[l], 6, axis=-1)
        g = norm_g[l]
        h_lat = modulate(rmsnorm(x, g[0]), m_lat[0], m_lat[1])
        h_ctx = modulate(rmsnorm(ctx, g[0]), m_ctx[0], m_ctx[1])
        if l % 2 == 0:
            e = l // 2
            y_ctx, y_lat = mixer_ab(h_ctx, h_lat, cos, sin, ab_w_in[e], ab_w_out[e], attn_q_norm[e],
                                    attn_k_norm[e], s5_lam_re[e], s5_lam_im[e], s5_log_dt[e],
                                    s5_b_re[e], s5_b_im[e], s5_c_re[e], s5_c_im[e], s5_d[e],
                                    s5_glu_w[e], s5_glu_b[e], not last)
        else:
            o = l // 2
            y_ctx, y_lat = mixer_c(h_ctx, h_lat, c_w_in[o], c_w_out[o], lb_all[l], hgrn_norm[o], not last)
        x = x + m_lat[2] * rmsnorm(y_lat, g[1])
        h_lat = modulate(rmsnorm(x, g[2]), m_lat[3], m_lat[4])
        x = x + m_lat[5] * rmsnorm(conv_ffn(h_lat, ffn_w_up[l], ffn_conv_w[l], ffn_conv_b[l], ffn_w_down[l]), g[3])
        if not last:
            ctx = ctx + m_ctx[2] * rmsnorm(y_ctx, g[1])
            h_ctx = modulate(rmsnorm(ctx, g[2]), m_ctx[3], m_ctx[4])
            ctx = ctx + m_ctx[5] * rmsnorm(conv_ffn(h_ctx, ffn_w_up[l], ffn_conv_w[l], ffn_conv_b[l], ffn_w_down[l]), g[3])
    return x
```

```python
import math
import numpy as np
import ml_dtypes
from contextlib import ExitStack
import concourse.bass as bass
import concourse.mybir as mybir
from concourse.bass_utils import run_bass_kernel_spmd

F32 = mybir.dt.float32
BF16 = mybir.dt.bfloat16
I32 = mybir.dt.int32
AF = mybir.ActivationFunctionType
ALU = mybir.AluOpType
NPBF = ml_dtypes.bfloat16
NCORES = 8
VW = 4
EPS = 1e-6


class Cfg:
    def __init__(self, D_MODEL=2048, SEQ=4096, CTX_LEN=256):
        self.D = D_MODEL
        self.SEQ = SEQ
        self.NC = CTX_LEN
        self.BATCH = 4
        self.FT = D_MODEL // 128
        self.NL = SEQ // 2
        self.NT = self.NC + self.NL
        self.LS = self.NC + SEQ
        self.AH = D_MODEL // 256
        self.KVH = max(1, self.AH // 4)
        self.AG = self.AH // self.KVH
        self.AW = self.AH * 128
        self.KVW = self.KVH * 128
        self.S5W = D_MODEL // 2
        self.S5G = self.S5W // 16
        self.AB_IN = self.AW + 2 * self.KVW + self.S5W
        self.HH = D_MODEL // 128
        self.DFF = ((8 * D_MODEL) // 3 + 255) // 256 * 256
        self.FC = self.DFF // 128


class Prog:
    ENG = ("pe", "act", "dve", "pool", "sp")
    NDMA = 12

    def __init__(self, nc):
        self.nc = nc
        self.es = ExitStack()
        self.ops = {e: [] for e in self.ENG}
        self.count = {e: 0 for e in self.ENG}
        self.sem = {e: self.es.enter_context(nc.semaphore("prg_" + e)) for e in self.ENG}
        self.dma_sems = {}
        self.dma_cnt = {}
        self.dma_rr = {}
        for e in ("sp", "act", "pool"):
            self.dma_sems[e] = [self.es.enter_context(nc.semaphore("dq_%s_%d" % (e, i)))
                                for i in range(self.NDMA)]
            self.dma_cnt[e] = [0] * self.NDMA
            self.dma_rr[e] = 0
        self.waited = {e: {} for e in self.ENG}
        self.last_w = {}
        self.readers = {}
        self.semid = {}
        self.ntiles = 0
        self.psb = [self.psum([128, 512], F32, "psb%d" % i) for i in range(8)]
        self.ps_rr = 0
        self.V = 1
        self.v = 0
        self.dcache = {}

    def sbuf(self, shape, dtype, name=None):
        self.ntiles += 1
        name = (name or "t") + "_%d" % self.ntiles
        return self.es.enter_context(self.nc.sbuf_tensor(name, list(shape), dtype))

    def psum(self, shape, dtype, name=None):
        self.ntiles += 1
        name = (name or "p") + "_%d" % self.ntiles
        return self.es.enter_context(self.nc.psum_tensor(name, list(shape), dtype))

    def dram(self, name, shape, dtype, kind, shared=False):
        if name not in self.dcache:
            if name in getattr(self, "internal", ()):
                kind = "Internal"
            shp = list(shape) if shared else [self.V] + list(shape)
            self.dcache[name] = (self.nc.dram_tensor(name, shp, dtype, kind=kind).ap(), shared)
        ap, sh = self.dcache[name]
        return ap if sh else ap[self.v]

    def op(self, eng, fn, reads=(), writes=(), dma=False):
        writes = list(writes) + [k for k in reads if isinstance(k, tuple) and k[0] == "ps"]
        deps = {}

        def add(tok):
            s, v = tok
            k = id(s)
            self.semid[k] = s
            if deps.get(k, 0) < v:
                deps[k] = v

        for k in reads:
            w = self.last_w.get(k)
            if w is not None:
                add(w)
        for k in writes:
            w = self.last_w.get(k)
            if w is not None:
                add(w)
            for sk, v in self.readers.get(k, {}).items():
                add((self.semid[sk], v))
        if dma:
            slot = self.dma_rr[eng]
            self.dma_rr[eng] = (slot + 1) % self.NDMA
            sem = self.dma_sems[eng][slot]
            prev = self.dma_cnt[eng][slot]
            if prev > 0:
                add((sem, prev))
            val = prev + 16
            self.dma_cnt[eng][slot] = val
            inc = 16
        else:
            self.count[eng] += 1
            sem = self.sem[eng]
            val = self.count[eng]
            inc = 1
        tok = (sem, val)
        self.semid[id(sem)] = sem
        own = id(self.sem[eng])
        waits = []
        for sk, v in deps.items():
            if sk == own and eng == "pe":
                continue
            if self.waited[eng].get(sk, 0) >= v:
                continue
            self.waited[eng][sk] = v
            waits.append((self.semid[sk], v))
        self.ops[eng].append((waits, fn, sem, inc))
        for k in writes:
            self.last_w[k] = tok
            self.readers[k] = {}
        for k in reads:
            d = self.readers.setdefault(k, {})
            if d.get(id(sem), 0) < val:
                d[id(sem)] = val
        return tok

    def barrier(self):
        toks = []
        for e in self.ENG:
            if self.count[e] > 0:
                toks.append((self.sem[e], self.count[e]))
        for e in self.dma_sems:
            for s, c in zip(self.dma_sems[e], self.dma_cnt[e]):
                if c > 0:
                    toks.append((s, c))
        for e in self.ENG:
            waits = []
            for s, v in toks:
                if self.waited[e].get(id(s), 0) >= v:
                    continue
                if e == "pe" and s is self.sem["pe"]:
                    continue
                self.waited[e][id(s)] = v
                waits.append((s, v))
            if waits:
                self.ops[e].append((waits, None, None, 0))

    def finish(self):
        self.flush()
        self.es.close()

    def flush(self):
        self.barrier()
        nc = self.nc
        with nc.Block() as block:
            @block.tensor
            def _(e):
                self._emit("pe", e)

            @block.scalar
            def _(e):
                self._emit("act", e)

            @block.vector
            def _(e):
                self._emit("dve", e)

            @block.gpsimd
            def _(e):
                self._emit("pool", e)

            @block.sync
            def _(e):
                self._emit("sp", e)
        for e in self.ENG:
            self.ops[e] = []

    def _emit(self, name, e):
        for waits, fn, sem, inc in self.ops[name]:
            for (s, v) in waits:
                e.wait_ge(s, v)
            if fn is None:
                continue
            ins = fn(e)
            ins.then_inc(sem, inc)

    def dma(self, out, in_, reads, writes, eng="sp"):
        return self.op(eng, lambda e: e.dma_start(out=out, in_=in_), reads=reads, writes=writes, dma=True)

    def mm(self, out, lhsT, rhs, start, stop, reads, writes, **kw):
        return self.op("pe", lambda e: e.matmul(out, lhsT=lhsT, rhs=rhs, start=start, stop=stop, **kw),
                       reads=reads, writes=writes)

    def act(self, out, in_, func, reads, writes, scale=1.0, bias=None, eng="act"):
        if bias is None:
            return self.op(eng, lambda e: e.activation(out=out, in_=in_, func=func, scale=scale),
                           reads=reads, writes=writes)
        return self.op(eng, lambda e: e.activation(out=out, in_=in_, func=func, scale=scale, bias=bias),
                       reads=reads, writes=writes)

    def tt(self, out, in0, in1, op, reads, writes, eng="dve"):
        return self.op(eng, lambda e: e.tensor_tensor(out=out, in0=in0, in1=in1, op=op), reads=reads, writes=writes)

    def ts(self, out, in0, s1, op0, reads, writes, s2=None, op1=None, eng="dve"):
        if op1 is None:
            return self.op(eng, lambda e: e.tensor_scalar(out=out, in0=in0, scalar1=s1, scalar2=None, op0=op0),
                           reads=reads, writes=writes)
        return self.op(eng, lambda e: e.tensor_scalar(out=out, in0=in0, scalar1=s1, scalar2=s2, op0=op0, op1=op1),
                       reads=reads, writes=writes)

    def stt(self, out, in0, scalar, in1, op0, op1, reads, writes):
        return self.op("dve", lambda e: e.scalar_tensor_tensor(out=out, in0=in0, scalar=scalar, in1=in1,
                                                               op0=op0, op1=op1), reads=reads, writes=writes)

    def copy(self, out, in_, reads, writes, eng="dve"):
        if eng == "act":
            return self.op("act", lambda e: e.copy(out=out, in_=in_), reads=reads, writes=writes)
        return self.op(eng, lambda e: e.tensor_copy(out=out, in_=in_), reads=reads, writes=writes)

    def memset(self, ap, val, writes, eng="pool"):
        return self.op(eng, lambda e: e.memset(ap, val), writes=writes)

    def bank(self):
        i = self.ps_rr
        self.ps_rr = (i + 1) % 8
        return i


def colkeys(name, s, T):
    return [(name, c) for c in range(s // 128, (s + T + 127) // 128)]


def build_prog(body, V=1, internal=()):
    nc = bass.Bass("TRN2", target_bir_lowering=False)
    P = Prog(nc)
    P.V = V
    P.internal = set(internal)
    outer = P.es
    for v in range(V):
        P.v = v
        sub = ExitStack()
        P.es = sub
        body(P)
        P.flush()
        sub.close()
    P.es = outer
    P.es.close()
    return nc


def launch(body, maps, shared, V, internal=()):
    nc = build_prog(body, V, internal)
    npc = NCORES // V
    pmaps = []
    for pc in range(npc):
        m = {}
        for k in maps[0]:
            if k in shared:
                m[k] = maps[pc * V][k]
            else:
                m[k] = np.ascontiguousarray(np.stack([maps[pc * V + v][k] for v in range(V)], 0))
        pmaps.append(m)
    res = run_bass_kernel_spmd(nc, pmaps, core_ids=list(range(npc)))
    out = []
    for pc in range(npc):
        for v in range(V):
            out.append({k: a[v] for k, a in res.results[pc].items()})
    return out


def col_blocks(cfg, maxw=512):
    blks = []
    s = 0
    while s < cfg.NC:
        w = min(maxw, cfg.NC - s)
        blks.append((s, w, 0))
        s += w
    while s < cfg.NT:
        w = min(maxw, cfg.NT - s)
        blks.append((s, w, 1))
        s += w
    return blks


def tile_w(w, kc):
    K, N = w.shape
    return np.ascontiguousarray(w.reshape(K // 128, 128, N // 128, 128).transpose(2, 1, 0, 3))


def fm(v):
    return np.ascontiguousarray(v.reshape(-1, 128).T)


class WStream:
    def __init__(self, P, name, KC, nbuf=2, cast_eng="pool"):
        self.P = P
        self.name = name
        self.KC = KC
        self.nbuf = nbuf
        self.st = [P.sbuf([128, KC, 128], F32, name + "_st%d" % i) for i in range(nbuf)]
        self.wb = [P.sbuf([128, KC, 128], BF16, name + "_wb%d" % i) for i in range(nbuf)]
        self.i = 0
        self.cast_eng = cast_eng

    def load(self, src):
        P = self.P
        b = self.i % self.nbuf
        self.i += 1
        ks, kb = (self.name, "st", b), (self.name, "wb", b)
        P.dma(self.st[b][:], src, reads=[], writes=[ks])
        P.copy(self.wb[b][:], self.st[b][:], reads=[ks], writes=[kb], eng=self.cast_eng)
        return self.wb[b], kb


def body_L0(P, cfg):
    FT = cfg.FT
    NM = 6 * FT // NCORES
    c5 = P.dram("c5", [128, FT, 5], F32, "ExternalInput")
    mw = P.dram("mw", [2, NM, 128, FT, 128], F32, "ExternalInput")
    mb = P.dram("mb", [128, 2, NM], F32, "ExternalInput")
    mo = P.dram("mo", [128, 2, NM, 5], F32, "ExternalOutput")
    c5s = P.sbuf([128, FT, 5], F32, "c5s")
    scs = P.sbuf([128, FT, 5], F32, "scs")
    mbs = P.sbuf([128, 2, NM], F32, "mbs")
    mos = P.sbuf([128, 2, NM, 5], F32, "mos")
    wt = [P.sbuf([128, FT, 128], F32, "wt%d" % i) for i in range(2)]
    P.dma(c5s[:], c5, [], ["c5s"])
    P.dma(mbs[:], mb, [], ["mbs"])
    P.act(scs[:], c5s[:], AF.Silu, ["c5s"], ["scs"])
    it = 0
    for l in range(2):
        for j in range(NM):
            b = it % 2
            it += 1
            P.dma(wt[b][:], mw[l, j], [], [("wt", b)])
            pb = P.bank()
            for kc in range(FT):
                P.mm(P.psb[pb][:, 0:5], wt[b][:, kc, :], scs[:, kc, :], kc == 0, kc == FT - 1,
                     [("wt", b), "scs"], [("ps", pb)])
            P.ts(mos[:, l, j, :], P.psb[pb][:, 0:5], mbs[:, l, j:j + 1], ALU.add, [("ps", pb), "mbs"], ["mos"])
    P.dma(mo, mos[:], ["mos"], ["mo"])


def run_L0(cfg, inp):
    FT, D = cfg.FT, cfg.D
    NM = 6 * FT // NCORES
    c5 = np.concatenate([inp["c_ctx"][None, :], inp["c"]], 0)
    c5T = np.ascontiguousarray(c5.T.reshape(FT, 128, 5).transpose(1, 0, 2))
    maps = []
    for core in range(NCORES):
        mwl, mbl = [], []
        for l in range(2):
            cols = slice(core * NM * 128, (core + 1) * NM * 128)
            mwl.append(tile_w(inp["mod_w"][l][:, cols], FT))
            mbl.append(fm(inp["mod_b"][l][cols]))
        maps.append({"c5": c5T, "mw": np.stack(mwl, 0), "mb": np.ascontiguousarray(np.stack(mbl, 1))})
    res = launch(lambda P: body_L0(P, cfg), maps, [], 1)
    mod = np.zeros((2, 6 * D, 5), np.float32)
    for core in range(NCORES):
        mo = res[core]["mo"]
        for l in range(2):
            mod[l, core * NM * 128:(core + 1) * NM * 128] = mo[:, l].transpose(1, 0, 2).reshape(NM * 128, 5)
    return mod


def mods_for(cfg, mod, l, b):
    m = mod[l][:, [0, 1 + b]]
    return np.ascontiguousarray(m.reshape(6, cfg.FT, 128, 2).transpose(2, 0, 1, 3))


class Rot:
    def __init__(self, P, name, shape, dtype, n=2):
        self.name = name
        self.t = [P.sbuf(shape, dtype, "%s%d" % (name, i)) for i in range(n)]
        self.i = 0

    def next(self):
        b = self.i % len(self.t)
        self.i += 1
        return self.t[b], (self.name, b)


def mod_scalars(P, mods, gS, cfg):
    FT = cfg.FT
    sc = P.sbuf([128, 4, FT, 2], F32, "modsc")
    for c in range(2):
        for j, (mi, gi, plus1) in enumerate(((1, 0, True), (2, 1, False), (4, 2, True), (5, 3, False))):
            if plus1:
                P.ts(sc[:, j, :, c], mods[:, mi, :, c], 1.0, ALU.add, ["mods"], [("modsc", j, c)])
                P.tt(sc[:, j, :, c], sc[:, j, :, c], gS[:, gi, :], ALU.mult, [("modsc", j, c), "gS"], [("modsc", j, c)])
            else:
                P.tt(sc[:, j, :, c], mods[:, mi, :, c], gS[:, gi, :], ALU.mult, ["mods", "gS"], [("modsc", j, c)])
    return sc


def norm_block(P, cfg, xs, kx, T, ones, rot_sq, rs_t, rstd_t, nfeat_tiles, nfeat):
    pb = P.bank()
    for ft in range(nfeat_tiles):
        sq, ksq = rot_sq.next()
        P.act(sq[:, :T], xs[:, ft, :T], AF.Square, [kx], [ksq])
        P.mm(P.psb[pb][:, :T], ones[:], sq[:, :T], ft == 0, ft == nfeat_tiles - 1, [ksq, "ones"], [("ps", pb)])
    P.act(rs_t[:, :T], P.psb[pb][:, :T], AF.Sqrt, [("ps", pb)], ["rs_t"], scale=1.0 / nfeat, bias=EPS)
    P.op("dve", lambda e: e.reciprocal(out=rstd_t[:, :T], in_=rs_t[:, :T]), reads=["rs_t"], writes=["rstd_t"])


def body_L1(P, cfg):
    FT, NT, NL, NC = cfg.FT, cfg.NT, cfg.NL, cfg.NC
    NTI = cfg.AB_IN // 128
    NU = cfg.S5W // 128
    xT = P.dram("xT", [128, FT, NT], F32, "ExternalInput")
    mods_d = P.dram("mods", [128, 6, FT, 2], F32, "ExternalInput")
    gS_d = P.dram("gS", [128, 4, FT], F32, "ExternalInput", shared=True)
    w_in = P.dram("w_in", [NTI, 128, FT, 128], F32, "ExternalInput", shared=True)
    qkg_d = P.dram("qkg", [128, 2], F32, "ExternalInput", shared=True)
    cos_d = P.dram("cosT", [128, NL], F32, "ExternalInput")
    sin_d = P.dram("sinT", [128, NL], F32, "ExternalInput")
    prot_d = P.dram("prot", [128, 128], F32, "ExternalInput", shared=True)
    qT = P.dram("qT", [cfg.AH, 128, NT], BF16, "ExternalOutput")
    kT = P.dram("kT", [cfg.KVH, 128, NT], BF16, "ExternalOutput")
    vT = P.dram("vT", [cfg.KVH, 128, NT], BF16, "ExternalOutput")
    uT = P.dram("uT", [NU, 128, NT], F32, "ExternalOutput")

    mods = P.sbuf([128, 6, FT, 2], F32, "mods")
    gS = P.sbuf([128, 4, FT], F32, "gS")
    qkg = P.sbuf([128, 2], F32, "qkg")
    cosS = P.sbuf([128, NL], F32, "cosS")
    sinS = P.sbuf([128, NL], F32, "sinS")
    prot = P.sbuf([128, 128], F32, "prot")
    ones = P.sbuf([128, 128], F32, "ones")
    hT = P.sbuf([128, FT, NT], BF16, "hT")
    for t, d, k in ((mods, mods_d, "mods"), (gS, gS_d, "gS"), (qkg, qkg_d, "qkg"), (cosS, cos_d, "cos"),
                    (sinS, sin_d, "sin"), (prot, prot_d, "prot")):
        P.dma(t[:], d, [], [k])
    P.memset(ones[:], 1.0, ["ones"])
    sc = mod_scalars(P, mods, gS, cfg)
    blks = col_blocks(cfg)
    TM = max(w for _, w, _ in blks)
    blksA = col_blocks(cfg, 256)
    xs_r = Rot(P, "xs", [128, FT, 256], F32, 2)
    sq_r = Rot(P, "sq", [128, TM], F32, 2)
    tmp_r = Rot(P, "tmp", [128, TM], F32, 2)
    rs_t = P.sbuf([128, TM], F32, "rs_t")
    rstd_t = P.sbuf([128, TM], F32, "rstd_t")
    for (s, T, lat) in blksA:
        xs, kx = xs_r.next()
        P.dma(xs[:, :, :T], xT[:, :, s:s + T], [], [kx])
        norm_block(P, cfg, xs, kx, T, ones, sq_r, rs_t, rstd_t, FT, cfg.D)
        for ft in range(FT):
            tmp, kt = tmp_r.next()
            P.tt(tmp[:, :T], xs[:, ft, :T], rstd_t[:, :T], ALU.mult, [kx, "rstd_t"], [kt])
            P.act(hT[:, ft, s:s + T], tmp[:, :T], AF.Identity, [kt, ("modsc", 0, lat), "mods"], colkeys("hT", s, T),
                  scale=sc[:, 0, ft, lat:lat + 1], bias=mods[:, 0, ft, lat:lat + 1])
    ws = WStream(P, "win", FT)
    qraw_r = Rot(P, "qraw", [128, TM], F32, 2)
    qn_r = Rot(P, "qn", [128, TM], F32, 2)
    t1_r = Rot(P, "t1", [128, TM], F32, 2)
    t2_r = Rot(P, "t2", [128, TM], F32, 2)
    qo_r = Rot(P, "qo", [128, TM], BF16, 3)
    uo_r = Rot(P, "uo", [128, TM], F32, 3)
    rs2 = P.sbuf([128, TM], F32, "rs2")
    rstd2 = P.sbuf([128, TM], F32, "rstd2")
    for nt in range(NTI):
        wb, kw = ws.load(w_in[nt])
        for (s, T, lat) in blks:
            pb = P.bank()
            ps = P.psb[pb]
            for kc in range(FT):
                P.mm(ps[:, :T], wb[:, kc, :], hT[:, kc, s:s + T], kc == 0, kc == FT - 1, [kw] + colkeys("hT", s, T), [("ps", pb)])
            if nt < cfg.AH + cfg.KVH:
                isq = nt < cfg.AH
                dst = qT[nt] if isq else kT[nt - cfg.AH]
                gcol = qkg[:, 0:1] if isq else qkg[:, 1:2]
                sq, ksq = sq_r.next()
                P.act(sq[:, :T], ps[:, :T], AF.Square, [("ps", pb)], [ksq])
                qraw, kq = qraw_r.next()
                P.copy(qraw[:, :T], ps[:, :T], [("ps", pb)], [kq])
                pb2 = P.bank()
                P.mm(P.psb[pb2][:, :T], ones[:], sq[:, :T], True, True, [ksq, "ones"], [("ps", pb2)])
                P.act(rs2[:, :T], P.psb[pb2][:, :T], AF.Sqrt, [("ps", pb2)], ["rs2"], scale=1.0 / 128, bias=EPS)
                P.op("dve", lambda e, T=T: e.reciprocal(out=rstd2[:, :T], in_=rs2[:, :T]), reads=["rs2"], writes=["rstd2"])
                qn, kn = qn_r.next()
                P.stt(qn[:, :T], qraw[:, :T], gcol, rstd2[:, :T], ALU.mult, ALU.mult, [kq, "rstd2", "qkg"], [kn])
                qo, ko = qo_r.next()
                if lat:
                    lc = s - NC
                    pb3 = P.bank()
                    P.mm(P.psb[pb3][:, :T], prot[:], qn[:, :T], True, True, [kn, "prot"], [("ps", pb3)])
                    t1, k1 = t1_r.next()
                    t2, k2 = t2_r.next()
                    P.tt(t1[:, :T], qn[:, :T], cosS[:, lc:lc + T], ALU.mult, [kn, "cos"], [k1])
                    P.tt(t2[:, :T], P.psb[pb3][:, :T], sinS[:, lc:lc + T], ALU.mult, [("ps", pb3), "sin"], [k2])
                    P.tt(qo[:, :T], t1[:, :T], t2[:, :T], ALU.add, [k1, k2], [ko], eng="pool")
                else:
                    P.copy(qo[:, :T], qn[:, :T], [kn], [ko], eng="act")
                P.dma(dst[:, s:s + T], qo[:, :T], [ko], [("out", nt, s)])
            elif nt < cfg.AH + 2 * cfg.KVH:
                qo, ko = qo_r.next()
                P.copy(qo[:, :T], ps[:, :T], [("ps", pb)], [ko], eng="act")
                P.dma(vT[nt - cfg.AH - cfg.KVH][:, s:s + T], qo[:, :T], [ko], [("out", nt, s)])
            else:
                uo, ko = uo_r.next()
                P.copy(uo[:, :T], ps[:, :T], [("ps", pb)], [ko])
                P.dma(uT[nt - cfg.AH - 2 * cfg.KVH][:, s:s + T], uo[:, :T], [ko], [("out", nt, s)])


def rope_tables(cfg, pos):
    pos = np.asarray(pos)
    inv = (10000.0 ** (-np.arange(32, dtype=np.float32) / 32)).astype(np.float32)
    row = (pos // 64).astype(np.float32)
    col = (pos % 64).astype(np.float32)
    cosT = np.zeros((128, len(pos)), np.float32)
    sinT = np.zeros((128, len(pos)), np.float32)
    for d in range(128):
        axis, j = d // 64, d % 64
        f = j % 32
        ang = (row if axis == 0 else col) * inv[f]
        cosT[d] = np.cos(ang)
        sinT[d] = np.sin(ang)
    prot = np.zeros((128, 128), np.float32)
    for d in range(128):
        j = d % 64
        if j < 32:
            prot[d + 32, d] = -1.0
        else:
            prot[d - 32, d] = 1.0
    return cosT, sinT, prot


def tok_fm(X, FT):
    T = X.shape[0]
    return np.ascontiguousarray(X.T.reshape(-1, 128, T).transpose(1, 0, 2))


def run_L1(cfg, inp, mod):
    FT = cfg.FT
    w_in_t = tile_w(inp["ab_w_in"][0], FT)
    gS = np.ascontiguousarray(inp["norm_g"][0].reshape(4, FT, 128).transpose(2, 0, 1))
    qkg = np.ascontiguousarray(np.stack([inp["attn_q_norm"][0], inp["attn_k_norm"][0]], 1))
    maps = []
    for core in range(NCORES):
        b, half = core // 2, core % 2
        lat = inp["x"][b, half * cfg.NL:(half + 1) * cfg.NL]
        X = np.concatenate([inp["ctx"][b], lat], 0)
        cosT, sinT, prot = rope_tables(cfg, np.arange(half * cfg.NL, (half + 1) * cfg.NL))
        maps.append({"xT": tok_fm(X, FT), "mods": mods_for(cfg, mod, 0, b), "gS": gS, "w_in": w_in_t,
                     "qkg": qkg, "cosT": cosT, "sinT": sinT, "prot": prot})
    res = launch(lambda P: body_L1(P, cfg), maps, ['w_in', 'qkg', 'prot', 'gS'], VW)
    return [res[c] for c in range(NCORES)]


def attention_phase(P, cfg, qS, KTs, Vs, onesb, aT):
    NC, NT = cfg.NC, cfg.NT
    NK = cfg.LS
    blks = col_blocks(cfg)
    TM = max(w for _, w, _ in blks)
    pT_r = Rot(P, "pT", [128, TM], BF16, 3)
    rl = P.sbuf([128, TM], F32, "rl")
    ao_r = Rot(P, "ao", [128, TM], BF16, 2)
    scale = 1.0 / math.sqrt(128.0)
    it = 0
    for h in range(cfg.AH):
        kvh = h // cfg.AG
        for (s, T, lat) in blks:
            nkt = (NK if lat else NC) // 128
            bo, bl = 3 + it % 2, 5 + it % 2
            it += 1
            pend = None
            for kt in range(nkt + 1):
                cur = None
                if kt < nkt:
                    bs = kt % 3
                    P.mm(P.psb[bs][:, :T], KTs[:, kvh, kt * 128:(kt + 1) * 128], qS[:, h, s:s + T], True, True,
                         ["KT", "qS"], [("ps", bs)])
                    pT, kp = pT_r.next()
                    P.act(pT[:, :T], P.psb[bs][:, :T], AF.Exp, [("ps", bs)], [kp], scale=scale)
                    cur = (pT, kp, kt)
                if pend is not None:
                    pT0, kp0, k0 = pend
                    P.mm(P.psb[bo][:, :T], Vs[:, kvh, k0, :], pT0[:, :T], k0 == 0, k0 == nkt - 1, ["V", kp0], [("ps", bo)])
                    P.mm(P.psb[bl][:, :T], onesb[:], pT0[:, :T], k0 == 0, k0 == nkt - 1, ["onesb", kp0], [("ps", bl)])
                pend = cur
            P.op("dve", lambda e, T=T, bl=bl: e.reciprocal(out=rl[:, :T], in_=P.psb[bl][:, :T]), reads=[("ps", bl)], writes=["rl"])
            ao, ka = ao_r.next()
            P.tt(ao[:, :T], P.psb[bo][:, :T], rl[:, :T], ALU.mult, [("ps", bo), "rl"], [ka])
            P.dma(aT[h][:, s:s + T], ao[:, :T], [ka], [("aT", h, s)])


def s5_phase(P, cfg, uF, uB, prm, yF, yB, identb):
    LS = cfg.LS
    NJ = LS // 8
    G2 = cfg.S5G // 2
    NP = G2 // 2
    NQT = max(1, NP // 4)
    RPT = min(4, NP)
    M2 = 2 * NP
    TWO_PI = 2.0 * math.pi
    lam_re, lam_im, logdt, bre, bim, cre, cim, dsk, K9, JJ = prm
    sb = lambda shape, name, dt=F32: P.sbuf(shape, dt, name)
    lr = sb([128, M2], "lr"); dt_ = sb([128, M2], "dt"); a_ = sb([128, M2], "a_"); thn = sb([128, M2], "thn")
    P.ts(lr[:], lam_re[:], -1e-4, ALU.min, ["s5prm"], ["lr"])
    P.act(dt_[:], logdt[:], AF.Exp, ["s5prm"], ["dt"])
    P.tt(a_[:], lr[:], dt_[:], ALU.mult, ["lr", "dt"], ["a_"])
    P.tt(thn[:], lam_im[:], dt_[:], ALU.mult, ["s5prm", "dt"], ["thn"])
    P.ts(thn[:], thn[:], 1.0 / TWO_PI, ALU.mult, ["thn"], ["thn"])
    X = sb([128, M2, 9], "X"); Xi = sb([128, M2, 9], "Xi", I32); Xf = sb([128, M2, 9], "Xf")
    mag = sb([128, M2, 9], "mag"); sn = sb([128, M2, 9], "sn"); s2 = sb([128, M2, 9], "s2")
    LRE = sb([128, M2, 9], "LRE"); LIM = sb([128, M2, 9], "LIM"); NLIM = sb([128, M2, 9], "NLIM")
    for k in range(9):
        P.ts(X[:, :, k], thn[:], float(k), ALU.mult, ["thn"], ["X"])
        P.act(mag[:, :, k], a_[:], AF.Exp, ["a_"], ["mag"], scale=float(k))
    P.copy(Xi[:], X[:], ["X"], ["Xi"])
    P.copy(Xf[:], Xi[:], ["Xi"], ["Xf"])
    P.tt(X[:], X[:], Xf[:], ALU.subtract, ["X", "Xf"], ["X"])
    P.act(sn[:], X[:], AF.Sin, ["X"], ["sn"], scale=TWO_PI)
    P.act(s2[:], X[:], AF.Sin, ["X"], ["s2"], scale=math.pi)
    P.tt(s2[:], s2[:], s2[:], ALU.mult, ["s2"], ["s2"])
    P.ts(s2[:], s2[:], -2.0, ALU.mult, ["s2"], ["s2"], s2=1.0, op1=ALU.add)
    P.tt(LRE[:], mag[:], s2[:], ALU.mult, ["mag", "s2"], ["LRE"])
    P.tt(LIM[:], mag[:], sn[:], ALU.mult, ["mag", "sn"], ["LIM"])
    P.ts(NLIM[:], LIM[:], -1.0, ALU.mult, ["LIM"], ["NLIM"])
    den = sb([128, M2], "den"); t0 = sb([128, M2], "t0"); t1 = sb([128, M2], "t1")
    kr = sb([128, M2], "kr"); ki = sb([128, M2], "ki"); lm1 = sb([128, M2], "lm1")
    P.tt(den[:], lr[:], lr[:], ALU.mult, ["lr"], ["den"])
    P.tt(t0[:], lam_im[:], lam_im[:], ALU.mult, ["s5prm"], ["t0"])
    P.tt(den[:], den[:], t0[:], ALU.add, ["den", "t0"], ["den"])
    P.op("dve", lambda e: e.reciprocal(out=den[:], in_=den[:]), reads=["den"], writes=["den"])
    P.ts(lm1[:], LRE[:, :, 1], -1.0, ALU.add, ["LRE"], ["lm1"])
    P.tt(t0[:], lm1[:], lr[:], ALU.mult, ["lm1", "lr"], ["t0"])
    P.tt(t1[:], LIM[:, :, 1], lam_im[:], ALU.mult, ["LIM", "s5prm"], ["t1"])
    P.tt(t0[:], t0[:], t1[:], ALU.add, ["t0", "t1"], ["t0"])
    P.tt(kr[:], t0[:], den[:], ALU.mult, ["t0", "den"], ["kr"])
    P.tt(t0[:], LIM[:, :, 1], lr[:], ALU.mult, ["LIM", "lr"], ["t0"])
    P.tt(t1[:], lm1[:], lam_im[:], ALU.mult, ["lm1", "s5prm"], ["t1"])
    P.tt(t0[:], t0[:], t1[:], ALU.subtract, ["t0", "t1"], ["t0"])
    P.tt(ki[:], t0[:], den[:], ALU.mult, ["t0", "den"], ["ki"])
    BBre = sb([128, M2, 16], "BBre"); BBim = sb([128, M2, 16], "BBim"); tb = sb([128, M2, 16], "tb")
    NCim = sb([128, M2, 16], "NCim")
    krb = kr[:].unsqueeze(2).broadcast_to([128, M2, 16])
    kib = ki[:].unsqueeze(2).broadcast_to([128, M2, 16])
    P.tt(BBre[:], bre[:], krb, ALU.mult, ["s5prm", "kr"], ["BBre"])
    P.tt(tb[:], bim[:], kib, ALU.mult, ["s5prm", "ki"], ["tb"])
    P.tt(BBre[:], BBre[:], tb[:], ALU.subtract, ["BBre", "tb"], ["BBre"])
    P.tt(BBim[:], bim[:], krb, ALU.mult, ["s5prm", "kr"], ["BBim"])
    P.tt(tb[:], bre[:], kib, ALU.mult, ["s5prm", "ki"], ["tb"])
    P.tt(BBim[:], BBim[:], tb[:], ALU.add, ["BBim", "tb"], ["BBim"])
    P.ts(NCim[:], cim[:], -1.0, ALU.mult, ["s5prm"], ["NCim"])
    E4 = [sb([128, 8, 2, 128], "E4_%d" % r, BF16) for r in range(RPT)]
    R4E = [sb([128, 8, 2, 128], "R4E_%d" % r, BF16) for r in range(RPT)]
    CE = [sb([128, 2, 32], "CE_%d" % r, BF16) for r in range(2)]
    KernE = sb([128, 8, 128], "KernE", BF16)
    WS = sb([128, 8, 2, 128], "WS", BF16)
    for r in range(RPT):
        P.memset(E4[r][:], 0.0, [("E4", r)])
        P.memset(R4E[r][:], 0.0, [("R4E", r)])
    for r in range(2):
        P.memset(CE[r][:], 0.0, [("CE", r)])
    P.memset(KernE[:], 0.0, [("KernE", r) for r in range(RPT)])
    W8 = sb([128, 8, 2, 16], "W8"); W8b = sb([128, 8, 2, 16], "W8b")
    R8 = sb([128, 8, 2, 16], "R8"); R8b = sb([128, 8, 2, 16], "R8b")
    Hfull = [sb([128, 2, 257], "Hfull%d" % r, BF16) for r in range(RPT)]
    hcar = sb([128, RPT, 2], "hcar")
    uf = sb([128, LS], "uf"); ub = sb([128, LS], "ub", BF16)
    NB = 256
    xj = sb([128, NB], "xj"); xji = sb([128, NB], "xji", I32); xjf = sb([128, NB], "xjf")
    snj = sb([128, NB], "snj"); csj = sb([128, NB], "csj")
    va = sb([128, 2, NB], "va"); vb = sb([128, 2, NB], "vb"); vv = sb([128, 2, NB], "vv")
    hh = sb([128, 2, NB], "hh"); r8t = sb([128, NB], "r8t"); onesf = sb([128, NB], "onesf")
    ha = sb([128, 2, NB], "ha"); hb = sb([128, 2, NB], "hb")
    ysb_r = Rot(P, "ysb", [128, 8 * NB], F32, 2)
    P.memset(onesf[:], 1.0, ["onesf"])
    jblocks = []
    j = 0
    while j < NJ:
        n = min(NB, NJ - j)
        jblocks.append((j, n))
        j += n
    b16 = lambda ap: ap.unsqueeze(1).broadcast_to([128, 8, 16])
    for d in range(2):
        usrc, ydst = (uF, yF) if d == 0 else (uB, yB)
        for qt in range(NQT):
            P.dma(uf[:], usrc[qt], [], ["uf"])
            P.copy(ub[:], uf[:], ["uf"], ["ub"], eng="pool")
            for r in range(RPT):
                q = qt * RPT + r
                m = d * NP + q
                R0 = 32 * r
                lre8 = LRE[:, m, 0:8].unsqueeze(2).broadcast_to([128, 8, 16])
                lim8 = LIM[:, m, 0:8].unsqueeze(2).broadcast_to([128, 8, 16])
                nlim8 = NLIM[:, m, 0:8].unsqueeze(2).broadcast_to([128, 8, 16])
                P.tt(W8[:, :, 0, :], b16(BBre[:, m, :]), lre8, ALU.mult, ["BBre", "LRE"], ["W8"])
                P.tt(W8b[:, :, 0, :], b16(BBim[:, m, :]), nlim8, ALU.mult, ["BBim", "NLIM"], ["W8b"])
                P.tt(W8[:, :, 1, :], b16(BBim[:, m, :]), lre8, ALU.mult, ["BBim", "LRE"], ["W8"])
                P.tt(W8b[:, :, 1, :], b16(BBre[:, m, :]), lim8, ALU.mult, ["BBre", "LIM"], ["W8b"])
                P.tt(W8[:], W8[:], W8b[:], ALU.add, ["W8", "W8b"], ["W8"])
                for gi in range(2):
                    P.copy(E4[r][64 * gi:64 * gi + 64, :, :, R0 + 16 * gi:R0 + 16 * gi + 16], W8[64 * gi:64 * gi + 64],
                           ["W8"], [("E4", r)], eng="act" if gi else "dve")
                for half in range(2):
                    pbk = 5 + half
                    pv = P.psb[pbk][:].bitcast(BF16)
                    for kk in range(4):
                        for ri in range(2):
                            k = half * 4 + kk
                            idx = kk * 2 + ri
                            P.op("pe", lambda e, pv=pv, idx=idx, k=k, ri=ri, r=r: e.transpose(
                                out=pv[:, idx * 128:(idx + 1) * 128], in_=E4[r][:, k, ri, :], identity=identb[:]),
                                reads=[("E4", r), "identb"], writes=[("ps", pbk)])
                    P.copy(WS[R0:R0 + 32, half * 4:half * 4 + 4, :, :],
                           pv[R0:R0 + 32, :].rearrange("p (k r c) -> p k r c", k=4, r=2), [("ps", pbk)], [("WS", r)],
                           eng="act" if half else "dve")
                ce = CE[q % 2]
                for gi in range(2):
                    P.copy(ce[64 * gi:64 * gi + 64, 0, 16 * gi:16 * gi + 16], cre[64 * gi:64 * gi + 64, m, :],
                           ["s5prm"], [("CE", q % 2)])
                    P.copy(ce[64 * gi:64 * gi + 64, 1, 16 * gi:16 * gi + 16], NCim[64 * gi:64 * gi + 64, m, :],
                           ["NCim"], [("CE", q % 2)])
                for tau in range(8):
                    P.mm(P.psb[7][:, tau * 32:(tau + 1) * 32], E4[r][:, tau, 0, :], ce[:, 0, :], True, False,
                         [("E4", r), ("CE", q % 2)], [("ps", 7)])
                    P.mm(P.psb[7][:, tau * 32:(tau + 1) * 32], E4[r][:, tau, 1, :], ce[:, 1, :], False, True,
                         [("E4", r), ("CE", q % 2)], [("ps", 7)])
                P.copy(KernE[R0:R0 + 32, :, R0:R0 + 32], P.psb[7][R0:R0 + 32, 0:256].rearrange("p (t c) -> p t c", t=8),
                       [("ps", 7)], [("KernE", r)])
                lre1 = LRE[:, m, 1:9].unsqueeze(2).broadcast_to([128, 8, 16])
                lim1 = LIM[:, m, 1:9].unsqueeze(2).broadcast_to([128, 8, 16])
                nlim1 = NLIM[:, m, 1:9].unsqueeze(2).broadcast_to([128, 8, 16])
                P.tt(R8[:, :, 0, :], b16(cre[:, m, :]), lre1, ALU.mult, ["s5prm", "LRE"], ["R8"])
                P.tt(R8b[:, :, 0, :], b16(NCim[:, m, :]), lim1, ALU.mult, ["NCim", "LIM"], ["R8b"])
                P.tt(R8[:, :, 1, :], b16(cre[:, m, :]), nlim1, ALU.mult, ["s5prm", "NLIM"], ["R8"])
                P.tt(R8b[:, :, 1, :], b16(NCim[:, m, :]), lre1, ALU.mult, ["NCim", "LRE"], ["R8b"])
                P.tt(R8[:], R8[:], R8b[:], ALU.add, ["R8", "R8b"], ["R8"])
                for gi in range(2):
                    P.copy(R4E[r][64 * gi:64 * gi + 64, :, :, R0 + 16 * gi:R0 + 16 * gi + 16], R8[64 * gi:64 * gi + 64],
                           ["R8"], [("R4E", r)], eng="act" if gi else "dve")
                P.memset(Hfull[r][:, :, 0:1], 0.0, [("Hfull", r)])
                P.memset(hcar[:, r, :], 0.0, [("hcar", r)])
            for (j0, n) in jblocks:
                first_in_bank = [True] * 4
                for r in range(RPT):
                    q = qt * RPT + r
                    m = d * NP + q
                    R0 = 32 * r
                    tp = (R0, 0)
                    for ri in range(2):
                        for s_ in range(8):
                            c0 = 8 * j0 + s_
                            P.mm(P.psb[4][:, ri * NB:ri * NB + n], WS[R0:R0 + 32, 7 - s_, ri, :],
                                 ub[R0:R0 + 32, c0:c0 + 8 * (n - 1) + 1:8], s_ == 0, s_ == 7,
                                 [("WS", r), "ub"], [("ps", 4)], tile_position=tp)
                    P.ts(xj[:, :n], JJ[:, j0:j0 + n], X[:, m, 8:9], ALU.mult, ["s5prm", "X"], ["xj"])
                    P.copy(xji[:, :n], xj[:, :n], ["xj"], ["xji"])
                    P.copy(xjf[:, :n], xji[:, :n], ["xji"], ["xjf"])
                    P.tt(xj[:, :n], xj[:, :n], xjf[:, :n], ALU.subtract, ["xj", "xjf"], ["xj"])
                    P.act(snj[:, :n], xj[:, :n], AF.Sin, ["xj"], ["snj"], scale=TWO_PI)
                    P.act(csj[:, :n], xj[:, :n], AF.Sin, ["xj"], ["csj"], scale=math.pi)
                    P.act(csj[:, :n], csj[:, :n], AF.Square, ["csj"], ["csj"])
                    P.ts(csj[:, :n], csj[:, :n], -2.0, ALU.mult, ["csj"], ["csj"], s2=1.0, op1=ALU.add)
                    S2 = P.psb[4][:, 0:2 * NB].rearrange("p (r n) -> p r n", r=2)
                    csb = csj[:, :n].unsqueeze(1).broadcast_to([128, 2, n])
                    snb = snj[:, :n].unsqueeze(1).broadcast_to([128, 2, n])
                    P.tt(va[:, :, :n], S2[:, :, :n], csb, ALU.mult, [("ps", 4), "csj"], ["va"])
                    P.tt(vb[:, :, :n], S2[:, :, :n], snb, ALU.mult, [("ps", 4), "snj"], ["vb"])
                    P.tt(vv[:, 0, :n], va[:, 0, :n], vb[:, 1, :n], ALU.add, ["va", "vb"], ["vv"])
                    P.tt(vv[:, 1, :n], va[:, 1, :n], vb[:, 0, :n], ALU.subtract, ["va", "vb"], ["vv"])
                    P.ts(r8t[:, :n], onesf[:, :n], mag[:, m, 8:9], ALU.mult, ["onesf", "mag"], ["r8t"])
                    for ri in range(2):
                        P.op("dve", lambda e, ri=ri, r=r, n=n: e.tensor_tensor_scan(
                            out=hh[:, ri, :n], data0=r8t[:, :n], data1=vv[:, ri, :n], initial=hcar[:, r, ri:ri + 1],
                            op0=ALU.mult, op1=ALU.add), reads=["r8t", "vv", ("hcar", r)], writes=["hh"])
                    P.copy(hcar[:, r, :], hh[:, :, n - 1], ["hh"], [("hcar", r)])
                    P.tt(ha[:, :, :n], hh[:, :, :n], csb, ALU.mult, ["hh", "csj"], ["ha"])
                    P.tt(hb[:, :, :n], hh[:, :, :n], snb, ALU.mult, ["hh", "snj"], ["hb"])
                    P.tt(Hfull[r][:, 0, 1:n + 1], ha[:, 0, :n], hb[:, 1, :n], ALU.subtract, ["ha", "hb"], [("Hfull", r)])
                    P.tt(Hfull[r][:, 1, 1:n + 1], ha[:, 1, :n], hb[:, 0, :n], ALU.add, ["ha", "hb"], [("Hfull", r)])
                    for t in range(8):
                        bk = t // 2
                        yreg = P.psb[bk][:, (t % 2) * NB:(t % 2) * NB + n]
                        for ri in range(2):
                            P.mm(yreg, R4E[r][:, t, ri, :], Hfull[r][:, ri, 0:n], first_in_bank[bk], False,
                                 [("R4E", r), ("Hfull", r)], [("ps", bk)], skip_group_check=True)
                            first_in_bank[bk] = False
                        for s_ in range(t + 1):
                            c0 = 8 * j0 + s_
                            P.mm(yreg, KernE[R0:R0 + 32, t - s_, :], ub[R0:R0 + 32, c0:c0 + 8 * (n - 1) + 1:8], False,
                                 (r == RPT - 1 and s_ == t), [("KernE", r), "ub"], [("ps", bk)], tile_position=tp,
                                 skip_group_check=True)
                    P.copy(Hfull[r][:, :, 0:1], Hfull[r][:, :, n:n + 1], [("Hfull", r)], [("Hfull", r)], eng="pool")
                ysb, ky = ysb_r.next()
                for t in range(8):
                    bk = t // 2
                    yreg = P.psb[bk][:, (t % 2) * NB:(t % 2) * NB + n]
                    c0 = 8 * j0 + t
                    if d == 0:
                        P.stt(ysb[:, t:t + 8 * (n - 1) + 1:8], uf[:, c0:c0 + 8 * (n - 1) + 1:8], dsk[:, qt:qt + 1], yreg,
                              ALU.mult, ALU.add, ["uf", "s5prm", ("ps", bk)], [ky])
                    else:
                        P.copy(ysb[:, t:t + 8 * (n - 1) + 1:8], yreg, [("ps", bk)], [ky], eng="act")
                P.dma(ydst[qt][:, 8 * j0:8 * (j0 + n)], ysb[:, :8 * n], [ky], [("y", d, qt, j0)])


def body_L2(P, cfg):
    NT, LS = cfg.NT, cfg.LS
    NK = LS
    KT = NK // 128
    G2 = cfg.S5G // 2
    NP = G2 // 2
    NQT = max(1, NP // 4)
    M2 = 2 * NP
    NJ = LS // 8

    def part_attn(P):
        qT_d = P.dram("qT", [cfg.AH, 128, NT], BF16, "ExternalInput")
        KT_d = P.dram("KT", [cfg.KVH, 128, NK], BF16, "ExternalInput")
        V_d = P.dram("V", [128, cfg.KVH, KT, 128], BF16, "ExternalInput")
        aT = P.dram("aT", [cfg.AH, 128, NT], BF16, "ExternalOutput")
        onesb = P.sbuf([128, 128], BF16, "onesb")
        P.memset(onesb[:], 1.0, ["onesb"])
        qS = P.sbuf([128, cfg.AH, NT], BF16, "qS")
        KTs = P.sbuf([128, cfg.KVH, NK], BF16, "KTs")
        Vs = P.sbuf([128, cfg.KVH, KT, 128], BF16, "Vs")
        P.dma(qS[:], qT_d.rearrange("h p n -> p h n"), [], ["qS"])
        P.dma(KTs[:], KT_d.rearrange("h p n -> p h n"), [], ["KT"])
        P.dma(Vs[:], V_d, [], ["V"])
        attention_phase(P, cfg, qS, KTs, Vs, onesb, aT)

    def part_s5(P):
        uF = P.dram("uF", [NQT, 128, LS], F32, "ExternalInput")
        uB = P.dram("uB", [NQT, 128, LS], F32, "ExternalInput")
        yF = P.dram("yF", [NQT, 128, LS], F32, "ExternalOutput")
        yB = P.dram("yB", [NQT, 128, LS], F32, "ExternalOutput")
        names = [("lam_re", [128, M2]), ("lam_im", [128, M2]), ("logdt", [128, M2]), ("bre", [128, M2, 16]),
                 ("bim", [128, M2, 16]), ("cre", [128, M2, 16]), ("cim", [128, M2, 16]), ("dsk", [128, NQT]),
                 ("K9", [128, 9]), ("JJ", [128, NJ])]
        prm = []
        for nm, shp in names:
            dd = P.dram(nm, shp, F32, "ExternalInput", shared=nm in ("K9", "JJ"))
            t = P.sbuf(shp, F32, "p_" + nm)
            P.dma(t[:], dd, [], ["s5prm"])
            prm.append(t)
        ident_d = P.dram("identb", [128, 128], BF16, "ExternalInput", shared=True)
        identb = P.sbuf([128, 128], BF16, "identb")
        P.dma(identb[:], ident_d, [], ["identb"])
        s5_phase(P, cfg, uF, uB, prm, yF, yB, identb)
    fused_body(P, [part_attn, part_s5])


def seg_rev(a, NC):
    return np.concatenate([a[..., :NC][..., ::-1], a[..., NC:][..., ::-1]], -1)


def run_L2(cfg, inp, r1):
    NC, NL, LS = cfg.NC, cfg.NL, cfg.LS
    G2 = cfg.S5G // 2
    NP = G2 // 2
    NQT = max(1, NP // 4)
    CH = cfg.S5W // 2
    NJ = LS // 8
    maps = []
    for core in range(NCORES):
        b, half = core // 2, core % 2
        ra, rb = r1[2 * b], r1[2 * b + 1]
        KTf = np.concatenate([ra["kT"][:, :, :NC], ra["kT"][:, :, NC:], rb["kT"][:, :, NC:]], 2)
        vTf = np.concatenate([ra["vT"][:, :, :NC], ra["vT"][:, :, NC:], rb["vT"][:, :, NC:]], 2)
        Vtok = vTf.transpose(0, 2, 1).reshape(cfg.KVH, LS // 128, 128, 128).transpose(2, 0, 1, 3)
        uTf = np.concatenate([ra["uT"][:, :, :NC], ra["uT"][:, :, NC:], rb["uT"][:, :, NC:]], 2)
        uT2 = uTf.reshape(cfg.S5W, LS)[half * CH:(half + 1) * CH].reshape(NQT, -1, LS)
        if uT2.shape[1] != 128:
            raise ValueError("S5 channel tile must be 128")
        gs = slice(half * G2, (half + 1) * G2)

        def pp(a):
            return np.ascontiguousarray(a.reshape(2, NP, 2, 64).transpose(2, 3, 0, 1).reshape(128, 2 * NP))

        def pp16(a):
            return np.ascontiguousarray(a.reshape(2, NP, 2, 64, 16).transpose(2, 3, 0, 1, 4).reshape(128, 2 * NP, 16))
        ldt = np.broadcast_to(inp["s5_log_dt"][0][:, gs, None], (2, G2, 64))
        m = {"qT": r1[core]["qT"], "KT": np.ascontiguousarray(KTf), "V": np.ascontiguousarray(Vtok),
             "uF": np.ascontiguousarray(uT2), "uB": np.ascontiguousarray(seg_rev(uT2, NC)),
             "lam_re": pp(inp["s5_lam_re"][0][:, gs]), "lam_im": pp(inp["s5_lam_im"][0][:, gs]), "logdt": pp(ldt),
             "bre": pp16(inp["s5_b_re"][0][:, gs]), "bim": pp16(inp["s5_b_im"][0][:, gs]),
             "cre": pp16(inp["s5_c_re"][0][:, gs].transpose(0, 1, 3, 2)),
             "cim": pp16(inp["s5_c_im"][0][:, gs].transpose(0, 1, 3, 2)),
             "dsk": fm(inp["s5_d"][0][half * CH:(half + 1) * CH]),
             "K9": np.broadcast_to(np.arange(9, dtype=np.float32), (128, 9)).copy(),
             "JJ": np.broadcast_to(np.arange(1, NJ + 1, dtype=np.float32), (128, NJ)).copy(),
             "identb": np.eye(128, dtype=np.float32).astype(NPBF)}
        maps.append(m)
    res = launch(lambda P: body_L2(P, cfg), maps, ['K9', 'JJ', 'identb'], 1)
    return [res[c] for c in range(NCORES)]


def gemm_ws(P, ws, w_dram, nts, KC, inT, in_key, blks, evac):
    for nt in nts:
        wb, kw = ws.load(w_dram[nt])
        for blk in blks:
            s, T, lat = blk
            pb = P.bank()
            for kc in range(KC):
                P.mm(P.psb[pb][:, :T], wb[:, kc, :], inT[:, kc, s:s + T], kc == 0, kc == KC - 1,
                     [kw] + colkeys(in_key, s, T), [("ps", pb)])
            evac(nt, blk, pb)


def body_L3a(P, cfg):
    FT, NT, AH = cfg.FT, cfg.NT, cfg.AH
    NU = cfg.S5W // 128
    aT = P.dram("aT", [AH, 128, NT], BF16, "ExternalInput")
    yFd = P.dram("yF", [NU, 128, NT], F32, "ExternalInput")
    yBd = P.dram("yB", [NU, 128, NT], F32, "ExternalInput")
    gw = P.dram("glu_w", [2 * NU, 128, NU, 128], F32, "ExternalInput", shared=True)
    gb_d = P.dram("glu_b", [128, 2 * NU], F32, "ExternalInput", shared=True)
    wo = P.dram("w_out", [FT, 128, FT, 128], F32, "ExternalInput", shared=True)
    yT = P.dram("yT", [FT, 128, NT], F32, "ExternalOutput")
    gb = P.sbuf([128, 2 * NU], F32, "gb")
    P.dma(gb[:], gb_d, [], ["gb"])
    mixT = P.sbuf([128, FT, NT], BF16, "mixT")
    gyT = P.sbuf([128, NU, NT], BF16, "gyT")
    blks = col_blocks(cfg)
    TM = max(w for _, w, _ in blks)
    for h in range(AH):
        P.dma(mixT[:, h, :], aT[h], [], colkeys("mixT", 0, NT))
    ya_r = Rot(P, "ya", [128, TM], F32, 2)
    yb_r = Rot(P, "yb", [128, TM], F32, 2)
    t_r = Rot(P, "tg", [128, TM], F32, 2)
    for (s, T, lat) in blks:
        for j in range(NU):
            ya, ka = ya_r.next()
            yb, kb = yb_r.next()
            tg, kt = t_r.next()
            P.dma(ya[:, :T], yFd[j][:, s:s + T], [], [ka])
            P.dma(yb[:, :T], yBd[j][:, s:s + T], [], [kb])
            P.tt(ya[:, :T], ya[:, :T], yb[:, :T], ALU.add, [ka, kb], [ka], eng="pool")
            P.act(tg[:, :T], ya[:, :T], AF.Square, [ka], [kt])
            P.ts(tg[:, :T], tg[:, :T], 0.044715, ALU.mult, [kt], [kt], s2=1.0, op1=ALU.add)
            P.tt(tg[:, :T], tg[:, :T], ya[:, :T], ALU.mult, [kt, ka], [kt])
            P.act(tg[:, :T], tg[:, :T], AF.Sigmoid, [kt], [kt], scale=2.0 * math.sqrt(2.0 / math.pi))
            P.tt(gyT[:, j, s:s + T], tg[:, :T], ya[:, :T], ALU.mult, [kt, ka], colkeys("gyT", s, T))
    ws = WStream(P, "w3", FT)
    sg_r = Rot(P, "sg", [128, TM], F32, 2)
    for j in range(NU):
        wa, kwa = ws.load_kc(gw[j], NU)
        wg, kwg = ws.load_kc(gw[NU + j], NU)
        for (s, T, lat) in blks:
            pa, pg = P.bank(), P.bank()
            for kc in range(NU):
                P.mm(P.psb[pa][:, :T], wa[:, kc, :], gyT[:, kc, s:s + T], kc == 0, kc == NU - 1, [kwa] + colkeys("gyT", s, T), [("ps", pa)])
            for kc in range(NU):
                P.mm(P.psb[pg][:, :T], wg[:, kc, :], gyT[:, kc, s:s + T], kc == 0, kc == NU - 1, [kwg] + colkeys("gyT", s, T), [("ps", pg)])
            sg, ks = sg_r.next()
            P.act(sg[:, :T], P.psb[pg][:, :T], AF.Sigmoid, [("ps", pg), "gb"], [ks], bias=gb[:, NU + j:NU + j + 1])
            P.stt(mixT[:, AH + j, s:s + T], P.psb[pa][:, :T], gb[:, j:j + 1], sg[:, :T], ALU.add, ALU.mult,
                  [("ps", pa), "gb", ks], colkeys("mixT", s, T))
    yo_r = Rot(P, "yo", [128, TM], F32, 3)

    def evac(nt, blk, pb):
        s, T, lat = blk
        yo, ko = yo_r.next()
        P.copy(yo[:, :T], P.psb[pb][:, :T], [("ps", pb)], [ko], eng="act" if (nt + s // 128) % 2 else "dve")
        P.dma(yT[nt][:, s:s + T], yo[:, :T], [ko], [("yT", nt, s)])
    gemm_ws(P, ws, wo, range(FT), FT, mixT, "mixT", blks, evac)


def _ws_load_kc(self, src, kc):
    P = self.P
    b = self.i % self.nbuf
    self.i += 1
    ks, kb = (self.name, "st", b), (self.name, "wb", b)
    P.dma(self.st[b][:, :kc, :], src, reads=[], writes=[ks])
    P.copy(self.wb[b][:, :kc, :], self.st[b][:, :kc, :], reads=[ks], writes=[kb], eng=self.cast_eng)
    return self.wb[b], kb


WStream.load_kc = _ws_load_kc


def pair_tokens(cfg, arrs, half, axis=-1):
    NC, NL = cfg.NC, cfg.NL
    a = arrs
    idx = np.r_[0:NC, NC + half * NL:NC + (half + 1) * NL]
    return np.take(a, idx, axis=axis)


def run_L3a(cfg, inp, r2):
    FT = cfg.FT
    NU = cfg.S5W // 128
    gw = tile_w(inp["s5_glu_w"][0], NU)
    gb = fm(inp["s5_glu_b"][0])
    wo = tile_w(inp["ab_w_out"][0], FT)
    maps = []
    for core in range(NCORES):
        b, half = core // 2, core % 2
        ra, rb = r2[2 * b], r2[2 * b + 1]
        yFf = np.concatenate([ra["yF"], rb["yF"]], 0).reshape(NU, 128, cfg.LS)
        yBf = np.concatenate([seg_rev(ra["yB"], cfg.NC), seg_rev(rb["yB"], cfg.NC)], 0).reshape(NU, 128, cfg.LS)
        maps.append({"aT": r2[core]["aT"], "yF": np.ascontiguousarray(pair_tokens(cfg, yFf, half)),
                     "yB": np.ascontiguousarray(pair_tokens(cfg, yBf, half)), "glu_w": gw, "glu_b": gb, "w_out": wo})
    res = launch(lambda P: body_L3a(P, cfg), maps, ['glu_w', 'glu_b', 'w_out'], VW)
    return [res[c]["yT"] for c in range(NCORES)]


def resid_phase(P, cfg, xT, yT, xoT, hoT, mods_g, gS_g, gate_j, mods_h, gS_h, h_sel, ones, hT_res=None):
    FT = cfg.FT
    sc_g = mod_scalars(P, mods_g, gS_g, cfg)
    sc_h = sc_g if (mods_h is mods_g) else (mod_scalars(P, mods_h, gS_h, cfg) if h_sel is not None else None)
    blks = col_blocks(cfg, 256)
    TM = max(w for _, w, _ in blks)
    ys_r = Rot(P, "ys", [128, FT, TM], F32, 2)
    xs_r = Rot(P, "xr", [128, FT, TM], F32, 2)
    hb_r = Rot(P, "hb", [128, FT, TM], BF16, 2) if (h_sel is not None and hT_res is None) else None
    sq_r = Rot(P, "sqr", [128, TM], F32, 2)
    tmp_r = Rot(P, "tmr", [128, TM], F32, 2)
    rs_t = P.sbuf([128, TM], F32, "rs_t")
    rstd_t = P.sbuf([128, TM], F32, "rstd_t")
    for (s, T, lat) in blks:
        ys, ky = ys_r.next()
        xs, kx = xs_r.next()
        P.dma(ys[:, :, :T], yT.rearrange("f p n -> p f n")[:, :, s:s + T], [], [ky])
        P.dma(xs[:, :, :T], xT[:, :, s:s + T], [], [kx])
        norm_block(P, cfg, ys, ky, T, ones, sq_r, rs_t, rstd_t, FT, cfg.D)
        for ft in range(FT):
            tmp, kt = tmp_r.next()
            P.tt(tmp[:, :T], ys[:, ft, :T], rstd_t[:, :T], ALU.mult, [ky, "rstd_t"], [kt])
            P.stt(xs[:, ft, :T], tmp[:, :T], sc_g[:, gate_j, ft, lat:lat + 1], xs[:, ft, :T], ALU.mult, ALU.add,
                  [kt, ("modsc", gate_j, lat), kx], [kx])
        P.dma(xoT[:, :, s:s + T], xs[:, :, :T], [kx], [("xo", s)])
        if h_sel is not None:
            norm_block(P, cfg, xs, kx, T, ones, sq_r, rs_t, rstd_t, FT, cfg.D)
            bidx = 0 if h_sel == 0 else 3
            if hT_res is None:
                hb, kh = hb_r.next()
            for ft in range(FT):
                tmp, kt = tmp_r.next()
                P.tt(tmp[:, :T], xs[:, ft, :T], rstd_t[:, :T], ALU.mult, [kx, "rstd_t"], [kt])
                if hT_res is None:
                    P.act(hb[:, ft, :T], tmp[:, :T], AF.Identity, [kt, ("modsc", h_sel, lat), "mods"], [kh],
                          scale=sc_h[:, h_sel, ft, lat:lat + 1], bias=mods_h[:, bidx, ft, lat:lat + 1])
                else:
                    P.act(hT_res[:, ft, s:s + T], tmp[:, :T], AF.Identity, [kt, ("modsc", h_sel, lat), "mods"],
                          colkeys("hT", s, T), scale=sc_h[:, h_sel, ft, lat:lat + 1], bias=mods_h[:, bidx, ft, lat:lat + 1])
            if hT_res is None:
                P.dma(hoT[:, :, s:s + T], hb[:, :, :T], [kh], [("ho", s)])


def body_R(P, cfg, gate_j, h_sel, yname="yT"):
    FT, NT = cfg.FT, cfg.NT
    xT = P.dram("xT", [128, FT, NT], F32, "ExternalInput")
    yT = P.dram(yname, [FT, 128, NT], F32, "ExternalInput")
    mods_d = P.dram("mods", [128, 6, FT, 2], F32, "ExternalInput")
    gS_d = P.dram("gS", [128, 4, FT], F32, "ExternalInput", shared=True)
    xoT = P.dram("xoT", [128, FT, NT], F32, "ExternalOutput")
    hoT = P.dram("hoT", [128, FT, NT], BF16, "ExternalOutput") if h_sel is not None else None
    mods = P.sbuf([128, 6, FT, 2], F32, "mods")
    gS = P.sbuf([128, 4, FT], F32, "gS")
    ones = P.sbuf([128, 128], F32, "ones")
    P.dma(mods[:], mods_d, [], ["mods"])
    P.dma(gS[:], gS_d, [], ["gS"])
    P.memset(ones[:], 1.0, ["ones"])
    resid_phase(P, cfg, xT, yT, xoT, hoT, mods, gS, gate_j, mods, gS, h_sel, ones)


def gS_of(cfg, inp, l):
    return np.ascontiguousarray(inp["norm_g"][l].reshape(4, cfg.FT, 128).transpose(2, 0, 1))


def run_R(cfg, inp, mod, l, xTs, yTs, gate_j, h_sel):
    maps = []
    for core in range(NCORES):
        maps.append({"xT": xTs[core], "yT": yTs[core], "mods": mods_for(cfg, mod, l, core // 2), "gS": gS_of(cfg, inp, l)})
    res = launch(lambda P: body_R(P, cfg, gate_j, h_sel), maps, ['gS'], 1)
    return [res[c] for c in range(NCORES)]


def arena_view(arena, off, shape, dtype):
    n = 1
    for d in shape[1:]:
        n *= d
    nbytes = n * (2 if dtype == BF16 else 4)
    assert off % 4 == 0 and nbytes % 4 == 0
    a = arena[:, off // 4:(off + nbytes) // 4]
    if dtype == BF16:
        a = a.bitcast(BF16)
    if len(shape) == 3:
        a = a.rearrange("p (a b) -> p a b", a=shape[1])
    return a, off + nbytes


def ffn_layout(cfg):
    NC, NL = cfg.NC, cfg.NL
    segs = []
    hl = 0
    if NC:
        segs.append(("c", 0, NC, 0))
        hl = NC + 2
    NLa = (NL - NC) // 2
    mbs = [[], []]

    def pieces(h0, w, o0):
        npc = -(-w // 510)
        base = -(-w // npc)
        out = []
        a = 0
        while a < w:
            ww = min(base, w - a)
            out.append((h0 + a, ww, o0 + a))
            a += ww
        return out
    if NC:
        mbs[0] += pieces(0, NC, 0)
    mbs[0] += pieces(hl, NLa, NC)
    mbs[1] += pieces(hl + NLa, NL - NLa, NC + NLa)
    return mbs


def body_F(P, cfg):
    FT, NT, FC = cfg.FT, cfg.NT, cfg.FC
    WH = NT + (4 if cfg.NC else 2)
    h2h = P.dram("h2h", [128, FT, WH], BF16, "ExternalInput")
    wup = P.dram("w_up", [2 * FC, 128, FT, 128], F32, "ExternalInput", shared=True)
    cw_d = P.dram("cw", [128, 2 * FC, 4], F32, "ExternalInput", shared=True)
    wdn = P.dram("w_dn", [FT, 128, FC, 128], F32, "ExternalInput", shared=True)
    fT = P.dram("fT", [FT, 128, NT], F32, "ExternalOutput")
    cw = P.sbuf([128, 2 * FC, 4], F32, "cw")
    P.dma(cw[:], cw_d, [], ["cw"])
    mbs = ffn_layout(cfg)
    VM = max(sum(w for _, w, _ in mb) for mb in mbs)
    WM = max(sum(w + 2 for _, w, _ in mb) for mb in mbs)
    PW = max(w for mb in mbs for _, w, _ in mb) + 2
    hid = P.sbuf([128, FC, VM], BF16, "hid")
    up_bytes = FT * WM * 2 + 3 * FT * 128 * 4 + 3 * FT * 128 * 2 + 8 * PW * 4
    dn_bytes = 2 * FC * 128 * 4 + 2 * FC * 128 * 2 + 3 * 512 * 4
    arena = P.sbuf([128, (max(up_bytes, dn_bytes) + 3) // 4], F32, "arena")
    off = 0
    h2s, off = arena_view(arena, off, [128, FT, WM], BF16)
    ust, uwb = [], []
    for i in range(3):
        v, off = arena_view(arena, off, [128, FT, 128], F32)
        ust.append(v)
    for i in range(3):
        v, off = arena_view(arena, off, [128, FT, 128], BF16)
        uwb.append(v)
    ctile = []
    for i in range(8):
        v, off = arena_view(arena, off, [128, PW], F32)
        ctile.append(v)
    off = 0
    dst_, dwb = [], []
    for i in range(2):
        v, off = arena_view(arena, off, [128, FC, 128], F32)
        dst_.append(v)
    for i in range(2):
        v, off = arena_view(arena, off, [128, FC, 128], BF16)
        dwb.append(v)
    etile = []
    for i in range(3):
        v, off = arena_view(arena, off, [128, 512], F32)
        etile.append(v)
    uit = [0]
    dit = [0]
    eit = [0]
    for mi, mb in enumerate(mbs):
        lcs = []
        lc = 0
        for (h0, w, o0) in mb:
            P.dma(h2s[:, :, lc:lc + w + 2], h2h[:, :, h0:h0 + w + 2], [], [("h2s", mi, lc)])
            lcs.append(lc)
            lc += w + 2
        for c in range(FC):
            wt = []
            for nt in (c, FC + c):
                b = uit[0] % 3
                uit[0] += 1
                P.dma(ust[b], wup[nt], [], [("ust", mi, b)])
                P.copy(uwb[b], ust[b], [("ust", mi, b)], [("uwb", mi, b)], eng="pool")
                wt.append((uwb[b], ("uwb", mi, b)))
            vo = 0
            for pi, (h0, w, o0) in enumerate(mb):
                lc = lcs[pi]
                N = w + 2
                pa, pv = P.bank(), P.bank()
                for (wb_, kw), pbk in zip(wt, (pa, pv)):
                    for kc in range(FT):
                        P.mm(P.psb[pbk][:, :N], wb_[:, kc, :], h2s[:, kc, lc:lc + N], kc == 0, kc == FT - 1,
                             [kw, ("h2s", mi, lc)], [("ps", pbk)])
                par = (c * len(mb) + pi) % 2
                Ua, Uv, ca, cv = (ctile[par * 4 + i] for i in range(4))
                ka, kv, kca, kcv = (("ct", mi, par, i) for i in range(4))
                P.copy(Ua[:, :N], P.psb[pa][:, :N], [("ps", pa)], [ka], eng="act")
                P.copy(Uv[:, :N], P.psb[pv][:, :N], [("ps", pv)], [kv], eng="dve")
                for (U, ku, cc, kc_, ch) in ((Ua, ka, ca, kca, c), (Uv, kv, cv, kcv, FC + c)):
                    P.act(cc[:, :w], U[:, 1:w + 1], AF.Identity, [ku, "cw"], [kc_], scale=cw[:, ch, 1:2], bias=cw[:, ch, 3:4])
                    P.stt(cc[:, :w], U[:, 0:w], cw[:, ch, 0:1], cc[:, :w], ALU.mult, ALU.add, [ku, "cw", kc_], [kc_])
                    P.stt(cc[:, :w], U[:, 2:w + 2], cw[:, ch, 2:3], cc[:, :w], ALU.mult, ALU.add, [ku, "cw", kc_], [kc_])
                P.act(ca[:, :w], ca[:, :w], AF.Silu, [kca], [kca])
                P.tt(hid[:, c, vo:vo + w], ca[:, :w], cv[:, :w], ALU.mult, [kca, kcv], [("hid", vo)], eng="pool")
                vo += w
        P.barrier()
        for nt in range(FT):
            b = dit[0] % 2
            dit[0] += 1
            P.dma(dst_[b], wdn[nt], [], [("dst", mi, b)])
            P.copy(dwb[b], dst_[b], [("dst", mi, b)], [("dwb", mi, b)], eng="pool")
            vo = 0
            for (h0, w, o0) in mb:
                pbk = P.bank()
                for kc in range(FC):
                    P.mm(P.psb[pbk][:, :w], dwb[b][:, kc, :], hid[:, kc, vo:vo + w], kc == 0, kc == FC - 1,
                         [("dwb", mi, b), ("hid", vo)], [("ps", pbk)])
                e = eit[0] % 3
                eit[0] += 1
                P.copy(etile[e][:, :w], P.psb[pbk][:, :w], [("ps", pbk)], [("et", mi, e)], eng="act" if e % 2 else "dve")
                P.dma(fT[nt][:, o0:o0 + w], etile[e][:, :w], [("et", mi, e)], [("fT", nt, o0)])
                vo += w
        P.barrier()


def halo_h2(cfg, hs, b, half):
    NC, NL = cfg.NC, cfg.NL
    me, other = hs[2 * b + half], hs[2 * b + 1 - half]
    z = np.zeros(me.shape[:2] + (1,), me.dtype)
    parts = []
    if NC:
        parts += [z, me[:, :, :NC], z]
    left = z if half == 0 else other[:, :, NT_last(cfg)]
    right = other[:, :, NC:NC + 1] if half == 0 else z
    parts += [left, me[:, :, NC:], right]
    return np.ascontiguousarray(np.concatenate(parts, 2))


def NT_last(cfg):
    return slice(cfg.NT - 1, cfg.NT)


def run_F(cfg, inp, l, hs):
    FT, FC = cfg.FT, cfg.FC
    wup = tile_w(inp["ffn_w_up"][l], FT)
    wdn = tile_w(inp["ffn_w_down"][l], FC)
    cwb = np.concatenate([inp["ffn_conv_w"][l], inp["ffn_conv_b"][l][None]], 0)
    cw = np.ascontiguousarray(cwb.T.reshape(2 * FC, 128, 4).transpose(1, 0, 2))
    maps = []
    for core in range(NCORES):
        maps.append({"h2h": halo_h2(cfg, hs, core // 2, core % 2), "w_up": wup, "cw": cw, "w_dn": wdn})
    res = launch(lambda P: body_F(P, cfg), maps, ['w_up', 'cw', 'w_dn'], VW)
    return [res[c]["fT"] for c in range(NCORES)]


def body_L4b(P, cfg):
    FT, NT = cfg.FT, cfg.NT
    xT = P.dram("xT", [128, FT, NT], F32, "ExternalInput")
    yT = P.dram("fT", [FT, 128, NT], F32, "ExternalInput")
    md0 = P.dram("mods0", [128, 6, FT, 2], F32, "ExternalInput")
    g0 = P.dram("gS0", [128, 4, FT], F32, "ExternalInput", shared=True)
    md1 = P.dram("mods1", [128, 6, FT, 2], F32, "ExternalInput")
    g1 = P.dram("gS1", [128, 4, FT], F32, "ExternalInput", shared=True)
    lbl_d = P.dram("lbl", [128, 2, 2, FT], F32, "ExternalInput", shared=True)
    wci = P.dram("c_w_in", [5 * FT, 128, FT, 128], F32, "ExternalInput", shared=True)
    xoT = P.dram("xoT", [128, FT, NT], F32, "ExternalOutput")
    qh = P.dram("qh", [FT, 128, NT], F32, "ExternalOutput")
    fF = P.dram("fF", [FT, 128, NT], F32, "ExternalOutput")
    fB = P.dram("fB", [FT, 128, NT], F32, "ExternalOutput")
    vh = P.dram("vh", [FT, 128, NT], BF16, "ExternalOutput")
    gh = P.dram("gh", [FT, 128, NT], F32, "ExternalOutput")
    mods0 = P.sbuf([128, 6, FT, 2], F32, "mods0"); gS0 = P.sbuf([128, 4, FT], F32, "gS0")
    mods1 = P.sbuf([128, 6, FT, 2], F32, "mods1"); gS1 = P.sbuf([128, 4, FT], F32, "gS1")
    lbl = P.sbuf([128, 2, 2, FT], F32, "lbl"); lb = P.sbuf([128, 2, FT], F32, "lb"); oml = P.sbuf([128, 2, FT], F32, "oml")
    ones = P.sbuf([128, 128], F32, "ones")
    P.dma(mods0[:], md0, [], ["mods"]); P.dma(gS0[:], g0, [], ["gS"])
    P.dma(mods1[:], md1, [], ["mods"]); P.dma(gS1[:], g1, [], ["gS"])
    P.dma(lbl[:], lbl_d, [], ["lbl"])
    P.memset(ones[:], 1.0, ["ones"])
    P.tt(lb[:], lbl[:, 1], lbl[:, 0], ALU.subtract, ["lbl"], ["lb"])
    P.act(lb[:], lb[:], AF.Sigmoid, ["lb"], ["lb"])
    P.ts(oml[:], lb[:], -1.0, ALU.mult, ["lb"], ["oml"], s2=1.0, op1=ALU.add)
    hT = P.sbuf([128, FT, NT], BF16, "hT")
    resid_phase(P, cfg, xT, yT, xoT, None, mods0, gS0, 3, mods1, gS1, 0, ones, hT_res=hT)
    blks = col_blocks(cfg)
    TM = max(w for _, w, _ in blks)
    ws = WStream(P, "wci", FT)
    eo_r = Rot(P, "eo", [128, TM], F32, 3)
    eb_r = Rot(P, "eb", [128, TM], BF16, 2)

    def evac(nt, blk, pb):
        s, T, lat = blk
        kind, j = nt // FT, nt % FT
        ps = P.psb[pb][:, :T]
        if kind == 3:
            eb, ke = eb_r.next()
            P.copy(eb[:, :T], ps, [("ps", pb)], [ke], eng="act")
            P.dma(vh[j][:, s:s + T], eb[:, :T], [ke], [("o", nt, s)])
            return
        eo, ke = eo_r.next()
        if kind == 0:
            P.act(eo[:, :T], ps, AF.Silu, [("ps", pb)], [ke])
            dst = qh
        elif kind in (1, 2):
            d = kind - 1
            P.act(eo[:, :T], ps, AF.Sigmoid, [("ps", pb)], [ke])
            P.ts(eo[:, :T], eo[:, :T], oml[:, d, j:j + 1], ALU.mult, [ke, "oml", "lb"], [ke], s2=lb[:, d, j:j + 1], op1=ALU.add)
            dst = fF if d == 0 else fB
        else:
            P.copy(eo[:, :T], ps, [("ps", pb)], [ke])
            dst = gh
        P.dma(dst[j][:, s:s + T], eo[:, :T], [ke], [("o", nt, s)])
    gemm_ws(P, ws, wci, range(5 * FT), FT, hT, "hT", blks, evac)


def run_L4b(cfg, inp, mod, xTs, fTs):
    FT = cfg.FT
    wci = tile_w(inp["c_w_in"][0], FT)
    lbl = np.ascontiguousarray(inp["hgrn_lb_logits"].reshape(2, 2, FT, 128).transpose(3, 0, 1, 2))
    maps = []
    for core in range(NCORES):
        b = core // 2
        maps.append({"xT": xTs[core], "yT": fTs[core], "mods0": mods_for(cfg, mod, 0, b), "gS0": gS_of(cfg, inp, 0),
                     "mods1": mods_for(cfg, mod, 1, b), "gS1": gS_of(cfg, inp, 1), "lbl": lbl, "c_w_in": wci})
    res = launch(lambda P: body_L4b(P, cfg), maps, ['gS0', 'gS1', 'lbl', 'c_w_in'], VW)
    return [res[c] for c in range(NCORES)]


def body_L5(P, cfg):
    LS, NC, SEQ = cfg.LS, cfg.NC, cfg.SEQ
    HC = cfg.HH // 2
    NTL = LS // 128
    qd = [P.dram("q%s" % x, [HC, 128, LS], F32, "ExternalInput") for x in "FB"]
    fd = [P.dram("f%s" % x, [HC, 128, LS], F32, "ExternalInput") for x in "FB"]
    Vd = [P.dram("V%s" % x, [128, NTL, HC, 128], BF16, "ExternalInput") for x in "FB"]
    od = [P.dram("o%s" % x, [HC, 128, SEQ], F32, "ExternalOutput") for x in "FB"]
    m64_d = P.dram("mask64", [128, 64], F32, "ExternalInput", shared=True)
    cm_d = P.dram("cmask", [128, 512], F32, "ExternalInput", shared=True)
    id_d = P.dram("identb", [128, 128], BF16, "ExternalInput", shared=True)
    mask64 = P.sbuf([128, 64], F32, "mask64"); cmask = P.sbuf([128, 512], F32, "cmask")
    identb = P.sbuf([128, 128], BF16, "identb")
    P.dma(mask64[:], m64_d, [], ["mask64"]); P.dma(cmask[:], cm_d, [], ["cmask"]); P.dma(identb[:], id_d, [], ["identb"])
    blks = [(0, NC, 0)] if NC else []
    s = NC
    while s < LS:
        blks.append((s, min(512, LS - s), 1))
        s += 512
    D2 = range(2)
    Vh = [P.sbuf([128, NTL, 128], BF16, "Vh%d" % d) for d in D2]
    qs = [Rot(P, "qs%d" % d, [128, 512], F32, 2) for d in D2]
    fs = [Rot(P, "fs%d" % d, [128, 512], F32, 2) for d in D2]
    lf = [P.sbuf([128, 512], F32, "lf%d" % d) for d in D2]
    bb = [P.sbuf([128, 512], F32, "bb%d" % d) for d in D2]
    eb = [Rot(P, "eb%d" % d, [128, 512], F32, 2) for d in D2]
    enb = [P.sbuf([128, 512], F32, "enb%d" % d) for d in D2]
    Qt = [Rot(P, "Qt%d" % d, [128, 512], BF16, 2) for d in D2]
    Kt = [Rot(P, "Kt%d" % d, [128, 512], BF16, 2) for d in D2]
    KtT = [Rot(P, "KtT%d" % d, [128, 128], BF16, 2) for d in D2]
    ATm = [Rot(P, "ATm%d" % d, [128, 64], BF16, 2) for d in D2]
    S32 = [P.sbuf([128, 128], F32, "S32_%d" % d) for d in D2]
    Sb = [P.sbuf([128, 128], BF16, "Sb_%d" % d) for d in D2]
    tmpS = [Rot(P, "tmpS%d" % d, [128, 128], F32, 2) for d in D2]
    osb = [Rot(P, "osb%d" % d, [128, 512], F32, 2) for d in D2]
    for h in range(HC):
        for d in D2:
            P.dma(Vh[d][:], Vd[d][:, :, h, :], [], [("Vh", d)])
            P.memset(S32[d][:], 0.0, [("S32", d)])
            P.memset(Sb[d][:], 0.0, [("Sb", d)])
        for (s, T, lat) in blks:
            cur = []
            for d in D2:
                q_, kq = qs[d].next()
                f_, kf = fs[d].next()
                P.dma(q_[:, :T], qd[d][h][:, s:s + T], [], [kq])
                P.dma(f_[:, :T], fd[d][h][:, s:s + T], [], [kf])
                P.act(lf[d][:, :T], f_[:, :T], AF.Ln, [kf], [("lf", d)])
                P.ts(f_[:, :T], f_[:, :T], -1.0, ALU.mult, [kf], [kf], s2=1.0, op1=ALU.add, eng="pool")
                P.op("dve", lambda e, d=d, T=T: e.tensor_tensor_scan(out=bb[d][:, :T], data0=cmask[:, :T], data1=lf[d][:, :T],
                                                                    initial=0.0, op0=ALU.mult, op1=ALU.add),
                     reads=["cmask", ("lf", d)], writes=[("bb", d)])
                e_, ke = eb[d].next()
                P.act(e_[:, :T], bb[d][:, :T], AF.Exp, [("bb", d)], [ke])
                P.act(enb[d][:, :T], bb[d][:, :T], AF.Exp, [("bb", d)], [("enb", d)], scale=-1.0)
                Q_, kQ = Qt[d].next()
                K_, kK = Kt[d].next()
                P.tt(Q_[:, :T], q_[:, :T], e_[:, :T], ALU.mult, [kq, ke], [kQ])
                P.tt(K_[:, :T], f_[:, :T], enb[d][:, :T], ALU.mult, [kf, ("enb", d)], [kK], eng="pool")
                cur.append((Q_, kQ, K_, kK, e_, ke))
            for t128 in range(T // 128):
                tile = (s + t128 * 128) // 128
                ktt = []
                for d in D2:
                    Q_, kQ, K_, kK, e_, ke = cur[d]
                    pv = P.psb[6 + d][:].bitcast(BF16)
                    P.op("pe", lambda e, pv=pv, K_=K_, t128=t128: e.transpose(out=pv[:, 0:128], in_=K_[:, t128 * 128:(t128 + 1) * 128],
                                                                         identity=identb[:]),
                         reads=[kK, "identb"], writes=[("ps", 6 + d)])
                    kt_, kkt = KtT[d].next()
                    P.copy(kt_[:], pv[:, 0:128], [("ps", 6 + d)], [kkt], eng="act")
                    ktt.append((kt_, kkt))
                for c2 in range(2):
                    cc = t128 * 128 + c2 * 64
                    R = slice(64 * c2, 64 * c2 + 64)
                    for d in D2:
                        Q_, kQ, K_, kK, e_, ke = cur[d]
                        kt_, kkt = ktt[d]
                        if lat:
                            P.mm(P.psb[d][R, 0:64], K_[:, cc:cc + 64], Q_[:, cc:cc + 64], True, True, [kK, kQ], [("ps", d)])
                            am, kam = ATm[d].next()
                            P.tt(am[R, :], P.psb[d][R, 0:64], mask64[R, :], ALU.mult, [("ps", d), "mask64"], [kam])
                            P.mm(P.psb[2 + d][:, cc:cc + 64], Vh[d][R, tile, :], am[R, :], True, False, [("Vh", d), kam],
                                 [("ps", 2 + d)])
                            P.mm(P.psb[2 + d][:, cc:cc + 64], Sb[d][:], Q_[:, cc:cc + 64], False, True, [("Sb", d), kQ],
                                 [("ps", 2 + d)])
                        P.mm(P.psb[4 + d][:, 0:128], kt_[R, :], Vh[d][R, tile, :], True, True, [kkt, ("Vh", d)], [("ps", 4 + d)])
                        ts_, kts = tmpS[d].next()
                        P.tt(ts_[:], S32[d][:], P.psb[4 + d][:, 0:128], ALU.add, [("S32", d), ("ps", 4 + d)], [kts])
                        P.act(S32[d][:], ts_[:], AF.Identity, [kts, ke], [("S32", d)], scale=e_[:, cc + 63:cc + 64])
                        P.act(Sb[d][:], ts_[:], AF.Identity, [kts, ke], [("Sb", d)], scale=e_[:, cc + 63:cc + 64])
            if lat:
                for d in D2:
                    o_, ko = osb[d].next()
                    P.copy(o_[:, :T], P.psb[2 + d][:, :T], [("ps", 2 + d)], [ko], eng="dve" if d else "act")
                    P.dma(od[d][h][:, s - NC:s - NC + T], o_[:, :T], [ko], [("o", d, h, s)])


def run_L5(cfg, r4):
    NC, NL, LS, FT = cfg.NC, cfg.NL, cfg.LS, cfg.FT
    HC = cfg.HH // 2
    m64 = np.zeros((128, 64), np.float32)
    for p in range(128):
        m64[p, (p % 64):] = 1.0
    cm = np.ones((128, 512), np.float32)
    cm[:, ::64] = 0.0
    ident = np.eye(128, dtype=np.float32).astype(NPBF)
    maps = []
    for core in range(NCORES):
        b, hh = core // 2, core % 2
        ra, rb = r4[2 * b], r4[2 * b + 1]
        hs = slice(hh * HC, (hh + 1) * HC)

        def full(nm):
            return np.concatenate([ra[nm][hs][:, :, :NC], ra[nm][hs][:, :, NC:], rb[nm][hs][:, :, NC:]], 2)
        q = full("qh")
        v = full("vh")

        def vtok(vv):
            return np.ascontiguousarray(vv.transpose(2, 0, 1).reshape(LS // 128, 128, HC, 128).transpose(1, 0, 2, 3))
        maps.append({"qF": np.ascontiguousarray(q), "qB": np.ascontiguousarray(seg_rev(q, NC)),
                     "fF": np.ascontiguousarray(full("fF")), "fB": np.ascontiguousarray(seg_rev(full("fB"), NC)),
                     "VF": vtok(v), "VB": vtok(seg_rev(v, NC)), "mask64": m64, "cmask": cm, "identb": ident})
    res = launch(lambda P: body_L5(P, cfg), maps, ['mask64', 'cmask', 'identb'], 1)
    return [res[c] for c in range(NCORES)]


def body_L6a(P, cfg):
    FT, NT = cfg.FT, cfg.NT
    oFd = P.dram("oF", [FT, 128, NT], F32, "ExternalInput")
    oBd = P.dram("oB", [FT, 128, NT], F32, "ExternalInput")
    ghd = P.dram("gh", [FT, 128, NT], F32, "ExternalInput")
    hgn_d = P.dram("hgn", [128, 1], F32, "ExternalInput", shared=True)
    wo = P.dram("c_w_out", [FT, 128, FT, 128], F32, "ExternalInput", shared=True)
    yT = P.dram("yT", [FT, 128, NT], F32, "ExternalOutput")
    hgn = P.sbuf([128, 1], F32, "hgn")
    ones = P.sbuf([128, 128], F32, "ones")
    P.dma(hgn[:], hgn_d, [], ["hgn"])
    P.memset(ones[:], 1.0, ["ones"])
    mixT = P.sbuf([128, FT, NT], BF16, "mixT")
    blks = col_blocks(cfg)
    TM = max(w for _, w, _ in blks)
    oa_r = Rot(P, "oa", [128, TM], F32, 2)
    ob_r = Rot(P, "ob", [128, TM], F32, 2)
    g_r = Rot(P, "gg", [128, TM], F32, 2)
    sq_r = Rot(P, "sq", [128, TM], F32, 2)
    rs = P.sbuf([128, TM], F32, "rs")
    rstd = P.sbuf([128, TM], F32, "rstd")
    for (s, T, lat) in blks:
        for j in range(FT):
            oa, ka = oa_r.next()
            ob, kb = ob_r.next()
            gg, kg = g_r.next()
            sq, ks = sq_r.next()
            P.dma(oa[:, :T], oFd[j][:, s:s + T], [], [ka])
            P.dma(ob[:, :T], oBd[j][:, s:s + T], [], [kb])
            P.dma(gg[:, :T], ghd[j][:, s:s + T], [], [kg])
            P.tt(oa[:, :T], oa[:, :T], ob[:, :T], ALU.add, [ka, kb], [ka], eng="pool")
            P.act(sq[:, :T], oa[:, :T], AF.Square, [ka], [ks])
            pb = P.bank()
            P.mm(P.psb[pb][:, :T], ones[:], sq[:, :T], True, True, [ks, "ones"], [("ps", pb)])
            P.act(rs[:, :T], P.psb[pb][:, :T], AF.Sqrt, [("ps", pb)], ["rs"], scale=1.0 / 128, bias=EPS)
            P.op("dve", lambda e, T=T: e.reciprocal(out=rstd[:, :T], in_=rs[:, :T]), reads=["rs"], writes=["rstd"])
            P.stt(oa[:, :T], oa[:, :T], hgn[:, 0:1], rstd[:, :T], ALU.mult, ALU.mult, [ka, "hgn", "rstd"], [ka])
            P.act(gg[:, :T], gg[:, :T], AF.Silu, [kg], [kg])
            P.tt(mixT[:, j, s:s + T], oa[:, :T], gg[:, :T], ALU.mult, [ka, kg], colkeys("mixT", s, T), eng="pool")
    ws = WStream(P, "wco", FT)
    yo_r = Rot(P, "yo", [128, TM], F32, 3)

    def evac(nt, blk, pb):
        s, T, lat = blk
        yo, ko = yo_r.next()
        P.copy(yo[:, :T], P.psb[pb][:, :T], [("ps", pb)], [ko], eng="act" if (nt + s // 128) % 2 else "dve")
        P.dma(yT[nt][:, s:s + T], yo[:, :T], [ko], [("yT", nt, s)])
    gemm_ws(P, ws, wo, range(FT), FT, mixT, "mixT", blks, evac)


def run_L6a(cfgL, cfg, inp, r4, r5):
    FT, NC, NL = cfg.FT, cfg.NC, cfg.NL
    wo = tile_w(inp["c_w_out"][0], FT)
    hgn = np.ascontiguousarray(inp["hgrn_norm"][0].reshape(128, 1))
    maps = []
    for core in range(NCORES):
        b, half = core // 2, core % 2
        ra, rb = r5[2 * b], r5[2 * b + 1]
        oF = np.concatenate([ra["oF"], rb["oF"]], 0)
        oB = np.concatenate([ra["oB"], rb["oB"]], 0)[:, :, ::-1]
        sl = slice(half * NL, (half + 1) * NL)
        maps.append({"oF": np.ascontiguousarray(oF[:, :, sl]), "oB": np.ascontiguousarray(oB[:, :, sl]),
                     "gh": np.ascontiguousarray(r4[core]["gh"][:, :, NC:]), "hgn": hgn, "c_w_out": wo})
    res = launch(lambda P: body_L6a(P, cfgL), maps, ['hgn', 'c_w_out'], VW)
    return [res[c]["yT"] for c in range(NCORES)]


def fused_body(P, bodies):
    outer = P.es
    for i, b in enumerate(bodies):
        sub = ExitStack()
        P.es = sub
        b(P)
        P.flush()
        sub.close()
    P.es = outer


def run_L3aR(cfg, inp, mod, r2, xTs):
    FT = cfg.FT
    NU = cfg.S5W // 128
    gw = tile_w(inp["s5_glu_w"][0], NU)
    gb = fm(inp["s5_glu_b"][0])
    wo = tile_w(inp["ab_w_out"][0], FT)
    maps = []
    for core in range(NCORES):
        b, half = core // 2, core % 2
        ra, rb = r2[2 * b], r2[2 * b + 1]
        yFf = np.concatenate([ra["yF"], rb["yF"]], 0).reshape(NU, 128, cfg.LS)
        yBf = np.concatenate([seg_rev(ra["yB"], cfg.NC), seg_rev(rb["yB"], cfg.NC)], 0).reshape(NU, 128, cfg.LS)
        maps.append({"aT": r2[core]["aT"], "yF": np.ascontiguousarray(pair_tokens(cfg, yFf, half)),
                     "yB": np.ascontiguousarray(pair_tokens(cfg, yBf, half)), "glu_w": gw, "glu_b": gb, "w_out": wo,
                     "xT": xTs[core], "mods": mods_for(cfg, mod, 0, b), "gS": gS_of(cfg, inp, 0)})
    return launch(lambda P: fused_body(P, [lambda P: body_L3a(P, cfg), lambda P: body_R(P, cfg, 1, 2)]), maps,
                  ["glu_w", "glu_b", "w_out", "gS"], VW, internal=["yT"])


def run_FL4b(cfg, inp, mod, hs, xTs):
    FT, FC = cfg.FT, cfg.FC
    l = 0
    wup = tile_w(inp["ffn_w_up"][l], FT)
    wdn = tile_w(inp["ffn_w_down"][l], FC)
    cwb = np.concatenate([inp["ffn_conv_w"][l], inp["ffn_conv_b"][l][None]], 0)
    cw = np.ascontiguousarray(cwb.T.reshape(2 * FC, 128, 4).transpose(1, 0, 2))
    wci = tile_w(inp["c_w_in"][0], FT)
    lbl = np.ascontiguousarray(inp["hgrn_lb_logits"].reshape(2, 2, FT, 128).transpose(3, 0, 1, 2))
    maps = []
    for core in range(NCORES):
        b = core // 2
        maps.append({"h2h": halo_h2(cfg, hs, b, core % 2), "w_up": wup, "cw": cw, "w_dn": wdn,
                     "xT": xTs[core], "mods0": mods_for(cfg, mod, 0, b), "gS0": gS_of(cfg, inp, 0),
                     "mods1": mods_for(cfg, mod, 1, b), "gS1": gS_of(cfg, inp, 1), "lbl": lbl, "c_w_in": wci})
    return launch(lambda P: fused_body(P, [lambda P: body_F(P, cfg), lambda P: body_L4b(P, cfg)]), maps,
                  ["w_up", "cw", "w_dn", "gS0", "gS1", "lbl", "c_w_in"], VW, internal=["fT"])


def run_L6aR(cfgL, cfg, inp, mod, r4, r5, x1):
    FT, NC, NL = cfg.FT, cfg.NC, cfg.NL
    wo = tile_w(inp["c_w_out"][0], FT)
    hgn = np.ascontiguousarray(inp["hgrn_norm"][0].reshape(128, 1))
    maps = []
    for core in range(NCORES):
        b, half = core // 2, core % 2
        ra, rb = r5[2 * b], r5[2 * b + 1]
        oF = np.concatenate([ra["oF"], rb["oF"]], 0)
        oB = np.concatenate([ra["oB"], rb["oB"]], 0)[:, :, ::-1]
        sl = slice(half * NL, (half + 1) * NL)
        maps.append({"oF": np.ascontiguousarray(oF[:, :, sl]), "oB": np.ascontiguousarray(oB[:, :, sl]),
                     "gh": np.ascontiguousarray(r4[core]["gh"][:, :, NC:]), "hgn": hgn, "c_w_out": wo,
                     "xT": x1[core], "mods": mods_for(cfg, mod, 1, b), "gS": gS_of(cfg, inp, 1)})
    return launch(lambda P: fused_body(P, [lambda P: body_L6a(P, cfgL), lambda P: body_R(P, cfgL, 1, 2)]), maps,
                  ["hgn", "c_w_out", "gS"], VW, internal=["yT"])


def run_FR2(cfgL, inp, mod, hs, xTs):
    FT, FC = cfgL.FT, cfgL.FC
    l = 1
    wup = tile_w(inp["ffn_w_up"][l], FT)
    wdn = tile_w(inp["ffn_w_down"][l], FC)
    cwb = np.concatenate([inp["ffn_conv_w"][l], inp["ffn_conv_b"][l][None]], 0)
    cw = np.ascontiguousarray(cwb.T.reshape(2 * FC, 128, 4).transpose(1, 0, 2))
    maps = []
    for core in range(NCORES):
        b = core // 2
        maps.append({"h2h": halo_h2(cfgL, hs, b, core % 2), "w_up": wup, "cw": cw, "w_dn": wdn,
                     "xT": xTs[core], "mods": mods_for(cfgL, mod, 1, b), "gS": gS_of(cfgL, inp, 1)})
    return launch(lambda P: fused_body(P, [lambda P: body_F(P, cfgL), lambda P: body_R(P, cfgL, 3, None, "fT")]), maps,
                  ["w_up", "cw", "w_dn", "gS"], VW, internal=["fT"])


def forward(cfg, inp):
    import copy
    inp = {k: np.asarray(v) for k, v in inp.items()}
    FT, NC, NL = cfg.FT, cfg.NC, cfg.NL
    cfgL = copy.copy(cfg)
    cfgL.NC = 0
    cfgL.NT = NL
    mod = run_L0(cfg, inp)
    r1 = run_L1(cfg, inp, mod)
    r2 = run_L2(cfg, inp, r1)
    del r1
    xTs = []
    for core in range(NCORES):
        b, half = core // 2, core % 2
        X = np.concatenate([inp["ctx"][b], inp["x"][b, half * NL:(half + 1) * NL]], 0)
        xTs.append(tok_fm(X, FT))
    rr = run_L3aR(cfg, inp, mod, r2, xTs)
    del r2, xTs
    r4 = run_FL4b(cfg, inp, mod, [r["hoT"] for r in rr], [r["xoT"] for r in rr])
    del rr
    r5 = run_L5(cfg, r4)
    x1 = [np.ascontiguousarray(r["xoT"][:, :, NC:]) for r in r4]
    rr = run_L6aR(cfgL, cfg, inp, mod, r4, r5, x1)
    del r4, r5, x1
    ro = run_FR2(cfgL, inp, mod, [r["hoT"] for r in rr], [r["xoT"] for r in rr])
    out = np.zeros((cfg.BATCH, cfg.SEQ, cfg.D), np.float32)
    for core in range(NCORES):
        b, half = core // 2, core % 2
        out[b, half * NL:(half + 1) * NL] = ro[core]["xoT"].transpose(2, 1, 0).reshape(NL, cfg.D)
    return out


def kernel(**inputs):
    return forward(Cfg(), inputs)
```

```python
import math
import numpy as np
import ml_dtypes
from contextlib import ExitStack
import concourse.bass as bass
import concourse.mybir as mybir
from concourse.bass_utils import run_bass_kernel_spmd

F32 = mybir.dt.float32
BF16 = mybir.dt.bfloat16
I32 = mybir.dt.int32
AF = mybir.ActivationFunctionType
ALU = mybir.AluOpType
NPBF = ml_dtypes.bfloat16
NCORES = 8
VW = 1
EPS = 1e-6


class Cfg:
    def __init__(self, D_MODEL=2048, SEQ=4096, CTX_LEN=256):
        self.D = D_MODEL
        self.SEQ = SEQ
        self.NC = CTX_LEN
        self.BATCH = 4
        self.FT = D_MODEL // 128
        self.NL = SEQ // 2
        self.NT = self.NC + self.NL
        self.LS = self.NC + SEQ
        self.AH = D_MODEL // 256
        self.KVH = max(1, self.AH // 4)
        self.AG = self.AH // self.KVH
        self.AW = self.AH * 128
        self.KVW = self.KVH * 128
        self.S5W = D_MODEL // 2
        self.S5G = self.S5W // 16
        self.AB_IN = self.AW + 2 * self.KVW + self.S5W
        self.HH = D_MODEL // 128
        self.DFF = ((8 * D_MODEL) // 3 + 255) // 256 * 256
        self.FC = self.DFF // 128


class Prog:
    ENG = ("pe", "act", "dve", "pool", "sp")
    NDMA = 12

    def __init__(self, nc):
        self.nc = nc
        self.es = ExitStack()
        self.ops = {e: [] for e in self.ENG}
        self.count = {e: 0 for e in self.ENG}
        self.sem = {e: self.es.enter_context(nc.semaphore("prg_" + e)) for e in self.ENG}
        self.dma_sems = {}
        self.dma_cnt = {}
        self.dma_rr = {}
        for e in ("sp", "act", "pool"):
            self.dma_sems[e] = [self.es.enter_context(nc.semaphore("dq_%s_%d" % (e, i)))
                                for i in range(self.NDMA)]
            self.dma_cnt[e] = [0] * self.NDMA
            self.dma_rr[e] = 0
        self.waited = {e: {} for e in self.ENG}
        self.last_w = {}
        self.readers = {}
        self.semid = {}
        self.ntiles = 0
        self.psb = [self.psum([128, 512], F32, "psb%d" % i) for i in range(8)]
        self.ps_rr = 0
        self.V = 1
        self.v = 0
        self.dcache = {}

    def sbuf(self, shape, dtype, name=None):
        self.ntiles += 1
        name = (name or "t") + "_%d" % self.ntiles
        return self.es.enter_context(self.nc.sbuf_tensor(name, list(shape), dtype))

    def psum(self, shape, dtype, name=None):
        self.ntiles += 1
        name = (name or "p") + "_%d" % self.ntiles
        return self.es.enter_context(self.nc.psum_tensor(name, list(shape), dtype))

    def dram(self, name, shape, dtype, kind, shared=False):
        if name not in self.dcache:
            if name in getattr(self, "internal", ()):
                kind = "Internal"
            shp = list(shape) if shared else [self.V] + list(shape)
            self.dcache[name] = (self.nc.dram_tensor(name, shp, dtype, kind=kind).ap(), shared)
        ap, sh = self.dcache[name]
        return ap if sh else ap[self.v]

    def op(self, eng, fn, reads=(), writes=(), dma=False):
        writes = list(writes) + [k for k in reads if isinstance(k, tuple) and k[0] == "ps"]
        deps = {}

        def add(tok):
            s, v = tok
            k = id(s)
            self.semid[k] = s
            if deps.get(k, 0) < v:
                deps[k] = v

        for k in reads:
            w = self.last_w.get(k)
            if w is not None:
                add(w)
        for k in writes:
            w = self.last_w.get(k)
            if w is not None:
                add(w)
            for sk, v in self.readers.get(k, {}).items():
                add((self.semid[sk], v))
        if dma:
            slot = self.dma_rr[eng]
            self.dma_rr[eng] = (slot + 1) % self.NDMA
            sem = self.dma_sems[eng][slot]
            prev = self.dma_cnt[eng][slot]
            if prev > 0:
                add((sem, prev))
            val = prev + 16
            self.dma_cnt[eng][slot] = val
            inc = 16
        else:
            self.count[eng] += 1
            sem = self.sem[eng]
            val = self.count[eng]
            inc = 1
        tok = (sem, val)
        self.semid[id(sem)] = sem
        own = id(self.sem[eng])
        waits = []
        for sk, v in deps.items():
            if sk == own and eng == "pe":
                continue
            if self.waited[eng].get(sk, 0) >= v:
                continue
            self.waited[eng][sk] = v
            waits.append((self.semid[sk], v))
        self.ops[eng].append((waits, fn, sem, inc))
        for k in writes:
            self.last_w[k] = tok
            self.readers[k] = {}
        for k in reads:
            d = self.readers.setdefault(k, {})
            if d.get(id(sem), 0) < val:
                d[id(sem)] = val
        return tok

    def barrier(self):
        toks = []
        for e in self.ENG:
            if self.count[e] > 0:
                toks.append((self.sem[e], self.count[e]))
        for e in self.dma_sems:
            for s, c in zip(self.dma_sems[e], self.dma_cnt[e]):
                if c > 0:
                    toks.append((s, c))
        for e in self.ENG:
            waits = []
            for s, v in toks:
                if self.waited[e].get(id(s), 0) >= v:
                    continue
                if e == "pe" and s is self.sem["pe"]:
                    continue
                self.waited[e][id(s)] = v
                waits.append((s, v))
            if waits:
                self.ops[e].append((waits, None, None, 0))

    def finish(self):
        self.flush()
        self.es.close()

    def flush(self):
        self.barrier()
        nc = self.nc
        with nc.Block() as block:
            @block.tensor
            def _(e):
                self._emit("pe", e)

            @block.scalar
            def _(e):
                self._emit("act", e)

            @block.vector
            def _(e):
                self._emit("dve", e)

            @block.gpsimd
            def _(e):
                self._emit("pool", e)

            @block.sync
            def _(e):
                self._emit("sp", e)
        for e in self.ENG:
            self.ops[e] = []

    def _emit(self, name, e):
        for waits, fn, sem, inc in self.ops[name]:
            for (s, v) in waits:
                e.wait_ge(s, v)
            if fn is None:
                continue
            ins = fn(e)
            ins.then_inc(sem, inc)

    def dma(self, out, in_, reads, writes, eng="sp"):
        return self.op(eng, lambda e: e.dma_start(out=out, in_=in_), reads=reads, writes=writes, dma=True)

    def mm(self, out, lhsT, rhs, start, stop, reads, writes, **kw):
        return self.op("pe", lambda e: e.matmul(out, lhsT=lhsT, rhs=rhs, start=start, stop=stop, **kw),
                       reads=reads, writes=writes)

    def act(self, out, in_, func, reads, writes, scale=1.0, bias=None, eng="act"):
        if bias is None:
            return self.op(eng, lambda e: e.activation(out=out, in_=in_, func=func, scale=scale),
                           reads=reads, writes=writes)
        return self.op(eng, lambda e: e.activation(out=out, in_=in_, func=func, scale=scale, bias=bias),
                       reads=reads, writes=writes)

    def tt(self, out, in0, in1, op, reads, writes, eng="dve"):
        return self.op(eng, lambda e: e.tensor_tensor(out=out, in0=in0, in1=in1, op=op), reads=reads, writes=writes)

    def ts(self, out, in0, s1, op0, reads, writes, s2=None, op1=None, eng="dve"):
        if op1 is None:
            return self.op(eng, lambda e: e.tensor_scalar(out=out, in0=in0, scalar1=s1, scalar2=None, op0=op0),
                           reads=reads, writes=writes)
        return self.op(eng, lambda e: e.tensor_scalar(out=out, in0=in0, scalar1=s1, scalar2=s2, op0=op0, op1=op1),
                       reads=reads, writes=writes)

    def stt(self, out, in0, scalar, in1, op0, op1, reads, writes):
        return self.op("dve", lambda e: e.scalar_tensor_tensor(out=out, in0=in0, scalar=scalar, in1=in1,
                                                               op0=op0, op1=op1), reads=reads, writes=writes)

    def copy(self, out, in_, reads, writes, eng="dve"):
        if eng == "act":
            return self.op("act", lambda e: e.copy(out=out, in_=in_), reads=reads, writes=writes)
        return self.op(eng, lambda e: e.tensor_copy(out=out, in_=in_), reads=reads, writes=writes)

    def memset(self, ap, val, writes, eng="pool"):
        return self.op(eng, lambda e: e.memset(ap, val), writes=writes)

    def bank(self):
        i = self.ps_rr
        self.ps_rr = (i + 1) % 8
        return i


def colkeys(name, s, T):
    return [(name, c) for c in range(s // 128, (s + T + 127) // 128)]


def build_prog(body, V=1, internal=()):
    nc = bass.Bass("TRN2", target_bir_lowering=False)
    P = Prog(nc)
    P.V = V
    P.internal = set(internal)
    outer = P.es
    for v in range(V):
        P.v = v
        sub = ExitStack()
        P.es = sub
        body(P)
        P.flush()
        sub.close()
    P.es = outer
    P.es.close()
    return nc


def launch(body, maps, shared, V, internal=()):
    nc = build_prog(body, V, internal)
    npc = NCORES // V
    pmaps = []
    for pc in range(npc):
        m = {}
        for k in maps[0]:
            if k in shared:
                m[k] = maps[pc * V][k]
            else:
                m[k] = np.ascontiguousarray(np.stack([maps[pc * V + v][k] for v in range(V)], 0))
        pmaps.append(m)
    res = run_bass_kernel_spmd(nc, pmaps, core_ids=list(range(npc)))
    out = []
    for pc in range(npc):
        for v in range(V):
            out.append({k: a[v] for k, a in res.results[pc].items()})
    return out


def col_blocks(cfg, maxw=512):
    blks = []
    s = 0
    while s < cfg.NC:
        w = min(maxw, cfg.NC - s)
        blks.append((s, w, 0))
        s += w
    while s < cfg.NT:
        w = min(maxw, cfg.NT - s)
        blks.append((s, w, 1))
        s += w
    return blks


def tile_w(w, kc):
    K, N = w.shape
    return np.ascontiguousarray(w.reshape(K // 128, 128, N // 128, 128).transpose(2, 1, 0, 3))


def fm(v):
    return np.ascontiguousarray(v.reshape(-1, 128).T)


class WStream:
    def __init__(self, P, name, KC, nbuf=2, cast_eng="pool"):
        self.P = P
        self.name = name
        self.KC = KC
        self.nbuf = nbuf
        self.st = [P.sbuf([128, KC, 128], F32, name + "_st%d" % i) for i in range(nbuf)]
        self.wb = [P.sbuf([128, KC, 128], BF16, name + "_wb%d" % i) for i in range(nbuf)]
        self.i = 0
        self.cast_eng = cast_eng

    def load(self, src):
        P = self.P
        b = self.i % self.nbuf
        self.i += 1
        ks, kb = (self.name, "st", b), (self.name, "wb", b)
        P.dma(self.st[b][:], src, reads=[], writes=[ks])
        P.copy(self.wb[b][:], self.st[b][:], reads=[ks], writes=[kb], eng=self.cast_eng)
        return self.wb[b], kb


def body_L0(P, cfg):
    FT = cfg.FT
    NM = 6 * FT // NCORES
    c5 = P.dram("c5", [128, FT, 5], F32, "ExternalInput")
    mw = P.dram("mw", [2, NM, 128, FT, 128], F32, "ExternalInput")
    mb = P.dram("mb", [128, 2, NM], F32, "ExternalInput")
    mo = P.dram("mo", [128, 2, NM, 5], F32, "ExternalOutput")
    c5s = P.sbuf([128, FT, 5], F32, "c5s")
    scs = P.sbuf([128, FT, 5], F32, "scs")
    mbs = P.sbuf([128, 2, NM], F32, "mbs")
    mos = P.sbuf([128, 2, NM, 5], F32, "mos")
    wt = [P.sbuf([128, FT, 128], F32, "wt%d" % i) for i in range(2)]
    P.dma(c5s[:], c5, [], ["c5s"])
    P.dma(mbs[:], mb, [], ["mbs"])
    P.act(scs[:], c5s[:], AF.Silu, ["c5s"], ["scs"])
    it = 0
    for l in range(2):
        for j in range(NM):
            b = it % 2
            it += 1
            P.dma(wt[b][:], mw[l, j], [], [("wt", b)])
            pb = P.bank()
            for kc in range(FT):
                P.mm(P.psb[pb][:, 0:5], wt[b][:, kc, :], scs[:, kc, :], kc == 0, kc == FT - 1,
                     [("wt", b), "scs"], [("ps", pb)])
            P.ts(mos[:, l, j, :], P.psb[pb][:, 0:5], mbs[:, l, j:j + 1], ALU.add, [("ps", pb), "mbs"], ["mos"])
    P.dma(mo, mos[:], ["mos"], ["mo"])


def run_L0(cfg, inp):
    FT, D = cfg.FT, cfg.D
    NM = 6 * FT // NCORES
    c5 = np.concatenate([inp["c_ctx"][None, :], inp["c"]], 0)
    c5T = np.ascontiguousarray(c5.T.reshape(FT, 128, 5).transpose(1, 0, 2))
    maps = []
    for core in range(NCORES):
        mwl, mbl = [], []
        for l in range(2):
            cols = slice(core * NM * 128, (core + 1) * NM * 128)
            mwl.append(tile_w(inp["mod_w"][l][:, cols], FT))
            mbl.append(fm(inp["mod_b"][l][cols]))
        maps.append({"c5": c5T, "mw": np.stack(mwl, 0), "mb": np.ascontiguousarray(np.stack(mbl, 1))})
    res = launch(lambda P: body_L0(P, cfg), maps, [], 1)
    mod = np.zeros((2, 6 * D, 5), np.float32)
    for core in range(NCORES):
        mo = res[core]["mo"]
        for l in range(2):
            mod[l, core * NM * 128:(core + 1) * NM * 128] = mo[:, l].transpose(1, 0, 2).reshape(NM * 128, 5)
    return mod


def mods_for(cfg, mod, l, b):
    m = mod[l][:, [0, 1 + b]]
    return np.ascontiguousarray(m.reshape(6, cfg.FT, 128, 2).transpose(2, 0, 1, 3))


class Rot:
    def __init__(self, P, name, shape, dtype, n=2):
        self.name = name
        self.t = [P.sbuf(shape, dtype, "%s%d" % (name, i)) for i in range(n)]
        self.i = 0

    def next(self):
        b = self.i % len(self.t)
        self.i += 1
        return self.t[b], (self.name, b)


def mod_scalars(P, mods, gS, cfg):
    FT = cfg.FT
    sc = P.sbuf([128, 4, FT, 2], F32, "modsc")
    for c in range(2):
        for j, (mi, gi, plus1) in enumerate(((1, 0, True), (2, 1, False), (4, 2, True), (5, 3, False))):
            if plus1:
                P.ts(sc[:, j, :, c], mods[:, mi, :, c], 1.0, ALU.add, ["mods"], [("modsc", j, c)])
                P.tt(sc[:, j, :, c], sc[:, j, :, c], gS[:, gi, :], ALU.mult, [("modsc", j, c), "gS"], [("modsc", j, c)])
            else:
                P.tt(sc[:, j, :, c], mods[:, mi, :, c], gS[:, gi, :], ALU.mult, ["mods", "gS"], [("modsc", j, c)])
    return sc


def norm_block(P, cfg, xs, kx, T, ones, rot_sq, rs_t, rstd_t, nfeat_tiles, nfeat):
    pb = P.bank()
    for ft in range(nfeat_tiles):
        sq, ksq = rot_sq.next()
        P.act(sq[:, :T], xs[:, ft, :T], AF.Square, [kx], [ksq])
        P.mm(P.psb[pb][:, :T], ones[:], sq[:, :T], ft == 0, ft == nfeat_tiles - 1, [ksq, "ones"], [("ps", pb)])
    P.act(rs_t[:, :T], P.psb[pb][:, :T], AF.Sqrt, [("ps", pb)], ["rs_t"], scale=1.0 / nfeat, bias=EPS)
    P.op("dve", lambda e: e.reciprocal(out=rstd_t[:, :T], in_=rs_t[:, :T]), reads=["rs_t"], writes=["rstd_t"])


def body_L1(P, cfg):
    FT, NT, NL, NC = cfg.FT, cfg.NT, cfg.NL, cfg.NC
    NTI = cfg.AB_IN // 128
    NU = cfg.S5W // 128
    xT = P.dram("xT", [128, FT, NT], F32, "ExternalInput")
    mods_d = P.dram("mods", [128, 6, FT, 2], F32, "ExternalInput")
    gS_d = P.dram("gS", [128, 4, FT], F32, "ExternalInput", shared=True)
    w_in = P.dram("w_in", [NTI, 128, FT, 128], F32, "ExternalInput", shared=True)
    qkg_d = P.dram("qkg", [128, 2], F32, "ExternalInput", shared=True)
    cos_d = P.dram("cosT", [128, NL], F32, "ExternalInput")
    sin_d = P.dram("sinT", [128, NL], F32, "ExternalInput")
    prot_d = P.dram("prot", [128, 128], F32, "ExternalInput", shared=True)
    qT = P.dram("qT", [cfg.AH, 128, NT], BF16, "ExternalOutput")
    kT = P.dram("kT", [cfg.KVH, 128, NT], BF16, "ExternalOutput")
    vT = P.dram("vT", [cfg.KVH, 128, NT], BF16, "ExternalOutput")
    uT = P.dram("uT", [NU, 128, NT], F32, "ExternalOutput")

    mods = P.sbuf([128, 6, FT, 2], F32, "mods")
    gS = P.sbuf([128, 4, FT], F32, "gS")
    qkg = P.sbuf([128, 2], F32, "qkg")
    cosS = P.sbuf([128, NL], F32, "cosS")
    sinS = P.sbuf([128, NL], F32, "sinS")
    prot = P.sbuf([128, 128], F32, "prot")
    ones = P.sbuf([128, 128], F32, "ones")
    hT = P.sbuf([128, FT, NT], BF16, "hT")
    for t, d, k in ((mods, mods_d, "mods"), (gS, gS_d, "gS"), (qkg, qkg_d, "qkg"), (cosS, cos_d, "cos"),
                    (sinS, sin_d, "sin"), (prot, prot_d, "prot")):
        P.dma(t[:], d, [], [k])
    P.memset(ones[:], 1.0, ["ones"])
    sc = mod_scalars(P, mods, gS, cfg)
    blks = col_blocks(cfg)
    TM = max(w for _, w, _ in blks)
    blksA = col_blocks(cfg, 256)
    xs_r = Rot(P, "xs", [128, FT, 256], F32, 2)
    sq_r = Rot(P, "sq", [128, TM], F32, 2)
    tmp_r = Rot(P, "tmp", [128, TM], F32, 2)
    rs_t = P.sbuf([128, TM], F32, "rs_t")
    rstd_t = P.sbuf([128, TM], F32, "rstd_t")
    for (s, T, lat) in blksA:
        xs, kx = xs_r.next()
        P.dma(xs[:, :, :T], xT[:, :, s:s + T], [], [kx])
        norm_block(P, cfg, xs, kx, T, ones, sq_r, rs_t, rstd_t, FT, cfg.D)
        for ft in range(FT):
            tmp, kt = tmp_r.next()
            P.tt(tmp[:, :T], xs[:, ft, :T], rstd_t[:, :T], ALU.mult, [kx, "rstd_t"], [kt])
            P.act(hT[:, ft, s:s + T], tmp[:, :T], AF.Identity, [kt, ("modsc", 0, lat), "mods"], colkeys("hT", s, T),
                  scale=sc[:, 0, ft, lat:lat + 1], bias=mods[:, 0, ft, lat:lat + 1])
    ws = WStream(P, "win", FT)
    qraw_r = Rot(P, "qraw", [128, TM], F32, 2)
    qn_r = Rot(P, "qn", [128, TM], F32, 2)
    t1_r = Rot(P, "t1", [128, TM], F32, 2)
    t2_r = Rot(P, "t2", [128, TM], F32, 2)
    qo_r = Rot(P, "qo", [128, TM], BF16, 3)
    uo_r = Rot(P, "uo", [128, TM], F32, 3)
    rs2 = P.sbuf([128, TM], F32, "rs2")
    rstd2 = P.sbuf([128, TM], F32, "rstd2")
    for nt in range(NTI):
        wb, kw = ws.load(w_in[nt])
        for (s, T, lat) in blks:
            pb = P.bank()
            ps = P.psb[pb]
            for kc in range(FT):
                P.mm(ps[:, :T], wb[:, kc, :], hT[:, kc, s:s + T], kc == 0, kc == FT - 1, [kw] + colkeys("hT", s, T), [("ps", pb)])
            if nt < cfg.AH + cfg.KVH:
                isq = nt < cfg.AH
                dst = qT[nt] if isq else kT[nt - cfg.AH]
                gcol = qkg[:, 0:1] if isq else qkg[:, 1:2]
                sq, ksq = sq_r.next()
                P.act(sq[:, :T], ps[:, :T], AF.Square, [("ps", pb)], [ksq])
                qraw, kq = qraw_r.next()
                P.copy(qraw[:, :T], ps[:, :T], [("ps", pb)], [kq])
                pb2 = P.bank()
                P.mm(P.psb[pb2][:, :T], ones[:], sq[:, :T], True, True, [ksq, "ones"], [("ps", pb2)])
                P.act(rs2[:, :T], P.psb[pb2][:, :T], AF.Sqrt, [("ps", pb2)], ["rs2"], scale=1.0 / 128, bias=EPS)
                P.op("dve", lambda e, T=T: e.reciprocal(out=rstd2[:, :T], in_=rs2[:, :T]), reads=["rs2"], writes=["rstd2"])
                qn, kn = qn_r.next()
                P.stt(qn[:, :T], qraw[:, :T], gcol, rstd2[:, :T], ALU.mult, ALU.mult, [kq, "rstd2", "qkg"], [kn])
                qo, ko = qo_r.next()
                if lat:
                    lc = s - NC
                    pb3 = P.bank()
                    P.mm(P.psb[pb3][:, :T], prot[:], qn[:, :T], True, True, [kn, "prot"], [("ps", pb3)])
                    t1, k1 = t1_r.next()
                    t2, k2 = t2_r.next()
                    P.tt(t1[:, :T], qn[:, :T], cosS[:, lc:lc + T], ALU.mult, [kn, "cos"], [k1])
                    P.tt(t2[:, :T], P.psb[pb3][:, :T], sinS[:, lc:lc + T], ALU.mult, [("ps", pb3), "sin"], [k2])
                    P.tt(qo[:, :T], t1[:, :T], t2[:, :T], ALU.add, [k1, k2], [ko], eng="pool")
                else:
                    P.copy(qo[:, :T], qn[:, :T], [kn], [ko], eng="act")
                P.dma(dst[:, s:s + T], qo[:, :T], [ko], [("out", nt, s)])
            elif nt < cfg.AH + 2 * cfg.KVH:
                qo, ko = qo_r.next()
                P.copy(qo[:, :T], ps[:, :T], [("ps", pb)], [ko], eng="act")
                P.dma(vT[nt - cfg.AH - cfg.KVH][:, s:s + T], qo[:, :T], [ko], [("out", nt, s)])
            else:
                uo, ko = uo_r.next()
                P.copy(uo[:, :T], ps[:, :T], [("ps", pb)], [ko])
                P.dma(uT[nt - cfg.AH - 2 * cfg.KVH][:, s:s + T], uo[:, :T], [ko], [("out", nt, s)])


def rope_tables(cfg, pos):
    pos = np.asarray(pos)
    inv = (10000.0 ** (-np.arange(32, dtype=np.float32) / 32)).astype(np.float32)
    row = (pos // 64).astype(np.float32)
    col = (pos % 64).astype(np.float32)
    cosT = np.zeros((128, len(pos)), np.float32)
    sinT = np.zeros((128, len(pos)), np.float32)
    for d in range(128):
        axis, j = d // 64, d % 64
        f = j % 32
        ang = (row if axis == 0 else col) * inv[f]
        cosT[d] = np.cos(ang)
        sinT[d] = np.sin(ang)
    prot = np.zeros((128, 128), np.float32)
    for d in range(128):
        j = d % 64
        if j < 32:
            prot[d + 32, d] = -1.0
        else:
            prot[d - 32, d] = 1.0
    return cosT, sinT, prot


def tok_fm(X, FT):
    T = X.shape[0]
    return np.ascontiguousarray(X.T.reshape(-1, 128, T).transpose(1, 0, 2))


def run_L1(cfg, inp, mod):
    FT = cfg.FT
    w_in_t = tile_w(inp["ab_w_in"][0], FT)
    gS = np.ascontiguousarray(inp["norm_g"][0].reshape(4, FT, 128).transpose(2, 0, 1))
    qkg = np.ascontiguousarray(np.stack([inp["attn_q_norm"][0], inp["attn_k_norm"][0]], 1))
    maps = []
    for core in range(NCORES):
        b, half = core // 2, core % 2
        lat = inp["x"][b, half * cfg.NL:(half + 1) * cfg.NL]
        X = np.concatenate([inp["ctx"][b], lat], 0)
        cosT, sinT, prot = rope_tables(cfg, np.arange(half * cfg.NL, (half + 1) * cfg.NL))
        maps.append({"xT": tok_fm(X, FT), "mods": mods_for(cfg, mod, 0, b), "gS": gS, "w_in": w_in_t,
                     "qkg": qkg, "cosT": cosT, "sinT": sinT, "prot": prot})
    res = launch(lambda P: body_L1(P, cfg), maps, ['w_in', 'qkg', 'prot', 'gS'], VW)
    return [res[c] for c in range(NCORES)]


def attention_phase(P, cfg, qS, KTs, Vs, onesb, aT):
    NC, NT = cfg.NC, cfg.NT
    NK = cfg.LS
    blks = col_blocks(cfg)
    TM = max(w for _, w, _ in blks)
    pT_r = Rot(P, "pT", [128, TM], BF16, 3)
    rl = P.sbuf([128, TM], F32, "rl")
    ao_r = Rot(P, "ao", [128, TM], BF16, 2)
    scale = 1.0 / math.sqrt(128.0)
    it = 0
    for h in range(cfg.AH):
        kvh = h // cfg.AG
        for (s, T, lat) in blks:
            nkt = (NK if lat else NC) // 128
            bo, bl = 3 + it % 2, 5 + it % 2
            it += 1
            pend = None
            for kt in range(nkt + 1):
                cur = None
                if kt < nkt:
                    bs = kt % 3
                    P.mm(P.psb[bs][:, :T], KTs[:, kvh, kt * 128:(kt + 1) * 128], qS[:, h, s:s + T], True, True,
                         ["KT", "qS"], [("ps", bs)])
                    pT, kp = pT_r.next()
                    P.act(pT[:, :T], P.psb[bs][:, :T], AF.Exp, [("ps", bs)], [kp], scale=scale)
                    cur = (pT, kp, kt)
                if pend is not None:
                    pT0, kp0, k0 = pend
                    P.mm(P.psb[bo][:, :T], Vs[:, kvh, k0, :], pT0[:, :T], k0 == 0, k0 == nkt - 1, ["V", kp0], [("ps", bo)])
                    P.mm(P.psb[bl][:, :T], onesb[:], pT0[:, :T], k0 == 0, k0 == nkt - 1, ["onesb", kp0], [("ps", bl)])
                pend = cur
            P.op("dve", lambda e, T=T, bl=bl: e.reciprocal(out=rl[:, :T], in_=P.psb[bl][:, :T]), reads=[("ps", bl)], writes=["rl"])
            ao, ka = ao_r.next()
            P.tt(ao[:, :T], P.psb[bo][:, :T], rl[:, :T], ALU.mult, [("ps", bo), "rl"], [ka])
            P.dma(aT[h][:, s:s + T], ao[:, :T], [ka], [("aT", h, s)])


def s5_phase(P, cfg, uF, uB, prm, yF, yB, identb):
    LS = cfg.LS
    NJ = LS // 8
    G2 = cfg.S5G // 2
    NP = G2 // 2
    NQT = max(1, NP // 4)
    RPT = min(4, NP)
    M2 = 2 * NP
    TWO_PI = 2.0 * math.pi
    lam_re, lam_im, logdt, bre, bim, cre, cim, dsk, K9, JJ = prm
    sb = lambda shape, name, dt=F32: P.sbuf(shape, dt, name)
    lr = sb([128, M2], "lr"); dt_ = sb([128, M2], "dt"); a_ = sb([128, M2], "a_"); thn = sb([128, M2], "thn")
    P.ts(lr[:], lam_re[:], -1e-4, ALU.min, ["s5prm"], ["lr"])
    P.act(dt_[:], logdt[:], AF.Exp, ["s5prm"], ["dt"])
    P.tt(a_[:], lr[:], dt_[:], ALU.mult, ["lr", "dt"], ["a_"])
    P.tt(thn[:], lam_im[:], dt_[:], ALU.mult, ["s5prm", "dt"], ["thn"])
    P.ts(thn[:], thn[:], 1.0 / TWO_PI, ALU.mult, ["thn"], ["thn"])
    X = sb([128, M2, 9], "X"); Xi = sb([128, M2, 9], "Xi", I32); Xf = sb([128, M2, 9], "Xf")
    mag = sb([128, M2, 9], "mag"); sn = sb([128, M2, 9], "sn"); s2 = sb([128, M2, 9], "s2")
    LRE = sb([128, M2, 9], "LRE"); LIM = sb([128, M2, 9], "LIM"); NLIM = sb([128, M2, 9], "NLIM")
    for k in range(9):
        P.ts(X[:, :, k], thn[:], float(k), ALU.mult, ["thn"], ["X"])
        P.act(mag[:, :, k], a_[:], AF.Exp, ["a_"], ["mag"], scale=float(k))
    P.copy(Xi[:], X[:], ["X"], ["Xi"])
    P.copy(Xf[:], Xi[:], ["Xi"], ["Xf"])
    P.tt(X[:], X[:], Xf[:], ALU.subtract, ["X", "Xf"], ["X"])
    P.act(sn[:], X[:], AF.Sin, ["X"], ["sn"], scale=TWO_PI)
    P.act(s2[:], X[:], AF.Sin, ["X"], ["s2"], scale=math.pi)
    P.tt(s2[:], s2[:], s2[:], ALU.mult, ["s2"], ["s2"])
    P.ts(s2[:], s2[:], -2.0, ALU.mult, ["s2"], ["s2"], s2=1.0, op1=ALU.add)
    P.tt(LRE[:], mag[:], s2[:], ALU.mult, ["mag", "s2"], ["LRE"])
    P.tt(LIM[:], mag[:], sn[:], ALU.mult, ["mag", "sn"], ["LIM"])
    P.ts(NLIM[:], LIM[:], -1.0, ALU.mult, ["LIM"], ["NLIM"])
    den = sb([128, M2], "den"); t0 = sb([128, M2], "t0"); t1 = sb([128, M2], "t1")
    kr = sb([128, M2], "kr"); ki = sb([128, M2], "ki"); lm1 = sb([128, M2], "lm1")
    P.tt(den[:], lr[:], lr[:], ALU.mult, ["lr"], ["den"])
    P.tt(t0[:], lam_im[:], lam_im[:], ALU.mult, ["s5prm"], ["t0"])
    P.tt(den[:], den[:], t0[:], ALU.add, ["den", "t0"], ["den"])
    P.op("dve", lambda e: e.reciprocal(out=den[:], in_=den[:]), reads=["den"], writes=["den"])
    P.ts(lm1[:], LRE[:, :, 1], -1.0, ALU.add, ["LRE"], ["lm1"])
    P.tt(t0[:], lm1[:], lr[:], ALU.mult, ["lm1", "lr"], ["t0"])
    P.tt(t1[:], LIM[:, :, 1], lam_im[:], ALU.mult, ["LIM", "s5prm"], ["t1"])
    P.tt(t0[:], t0[:], t1[:], ALU.add, ["t0", "t1"], ["t0"])
    P.tt(kr[:], t0[:], den[:], ALU.mult, ["t0", "den"], ["kr"])
    P.tt(t0[:], LIM[:, :, 1], lr[:], ALU.mult, ["LIM", "lr"], ["t0"])
    P.tt(t1[:], lm1[:], lam_im[:], ALU.mult, ["lm1", "s5prm"], ["t1"])
    P.tt(t0[:], t0[:], t1[:], ALU.subtract, ["t0", "t1"], ["t0"])
    P.tt(ki[:], t0[:], den[:], ALU.mult, ["t0", "den"], ["ki"])
    BBre = sb([128, M2, 16], "BBre"); BBim = sb([128, M2, 16], "BBim"); tb = sb([128, M2, 16], "tb")
    NCim = sb([128, M2, 16], "NCim")
    krb = kr[:].unsqueeze(2).broadcast_to([128, M2, 16])
    kib = ki[:].unsqueeze(2).broadcast_to([128, M2, 16])
    P.tt(BBre[:], bre[:], krb, ALU.mult, ["s5prm", "kr"], ["BBre"])
    P.tt(tb[:], bim[:], kib, ALU.mult, ["s5prm", "ki"], ["tb"])
    P.tt(BBre[:], BBre[:], tb[:], ALU.subtract, ["BBre", "tb"], ["BBre"])
    P.tt(BBim[:], bim[:], krb, ALU.mult, ["s5prm", "kr"], ["BBim"])
    P.tt(tb[:], bre[:], kib, ALU.mult, ["s5prm", "ki"], ["tb"])
    P.tt(BBim[:], BBim[:], tb[:], ALU.add, ["BBim", "tb"], ["BBim"])
    P.ts(NCim[:], cim[:], -1.0, ALU.mult, ["s5prm"], ["NCim"])
    E4 = [sb([128, 8, 2, 128], "E4_%d" % r, BF16) for r in range(RPT)]
    R4E = [sb([128, 8, 2, 128], "R4E_%d" % r, BF16) for r in range(RPT)]
    CE = [sb([128, 2, 32], "CE_%d" % r, BF16) for r in range(2)]
    KernE = sb([128, 8, 128], "KernE", BF16)
    WS = sb([128, 8, 2, 128], "WS", BF16)
    for r in range(RPT):
        P.memset(E4[r][:], 0.0, [("E4", r)])
        P.memset(R4E[r][:], 0.0, [("R4E", r)])
    for r in range(2):
        P.memset(CE[r][:], 0.0, [("CE", r)])
    P.memset(KernE[:], 0.0, [("KernE", r) for r in range(RPT)])
    W8 = sb([128, 8, 2, 16], "W8"); W8b = sb([128, 8, 2, 16], "W8b")
    R8 = sb([128, 8, 2, 16], "R8"); R8b = sb([128, 8, 2, 16], "R8b")
    Hfull = [sb([128, 2, 257], "Hfull%d" % r, BF16) for r in range(RPT)]
    hcar = sb([128, RPT, 2], "hcar")
    uf = sb([128, LS], "uf"); ub = sb([128, LS], "ub", BF16)
    NB = 256
    xj = sb([128, NB], "xj"); xji = sb([128, NB], "xji", I32); xjf = sb([128, NB], "xjf")
    snj = sb([128, NB], "snj"); csj = sb([128, NB], "csj")
    va = sb([128, 2, NB], "va"); vb = sb([128, 2, NB], "vb"); vv = sb([128, 2, NB], "vv")
    hh = sb([128, 2, NB], "hh"); r8t = sb([128, NB], "r8t"); onesf = sb([128, NB], "onesf")
    ha = sb([128, 2, NB], "ha"); hb = sb([128, 2, NB], "hb")
    ysb_r = Rot(P, "ysb", [128, 8 * NB], F32, 2)
    P.memset(onesf[:], 1.0, ["onesf"])
    jblocks = []
    j = 0
    while j < NJ:
        n = min(NB, NJ - j)
        jblocks.append((j, n))
        j += n
    b16 = lambda ap: ap.unsqueeze(1).broadcast_to([128, 8, 16])
    for d in range(2):
        usrc, ydst = (uF, yF) if d == 0 else (uB, yB)
        for qt in range(NQT):
            P.dma(uf[:], usrc[qt], [], ["uf"])
            P.copy(ub[:], uf[:], ["uf"], ["ub"], eng="pool")
            for r in range(RPT):
                q = qt * RPT + r
                m = d * NP + q
                R0 = 32 * r
                lre8 = LRE[:, m, 0:8].unsqueeze(2).broadcast_to([128, 8, 16])
                lim8 = LIM[:, m, 0:8].unsqueeze(2).broadcast_to([128, 8, 16])
                nlim8 = NLIM[:, m, 0:8].unsqueeze(2).broadcast_to([128, 8, 16])
                P.tt(W8[:, :, 0, :], b16(BBre[:, m, :]), lre8, ALU.mult, ["BBre", "LRE"], ["W8"])
                P.tt(W8b[:, :, 0, :], b16(BBim[:, m, :]), nlim8, ALU.mult, ["BBim", "NLIM"], ["W8b"])
                P.tt(W8[:, :, 1, :], b16(BBim[:, m, :]), lre8, ALU.mult, ["BBim", "LRE"], ["W8"])
                P.tt(W8b[:, :, 1, :], b16(BBre[:, m, :]), lim8, ALU.mult, ["BBre", "LIM"], ["W8b"])
                P.tt(W8[:], W8[:], W8b[:], ALU.add, ["W8", "W8b"], ["W8"])
                for gi in range(2):
                    P.copy(E4[r][64 * gi:64 * gi + 64, :, :, R0 + 16 * gi:R0 + 16 * gi + 16], W8[64 * gi:64 * gi + 64],
                           ["W8"], [("E4", r)], eng="act" if gi else "dve")
                for half in range(2):
                    pbk = 5 + half
                    pv = P.psb[pbk][:].bitcast(BF16)
                    for kk in range(4):
                        for ri in range(2):
                            k = half * 4 + kk
                            idx = kk * 2 + ri
                            P.op("pe", lambda e, pv=pv, idx=idx, k=k, ri=ri, r=r: e.transpose(
                                out=pv[:, idx * 128:(idx + 1) * 128], in_=E4[r][:, k, ri, :], identity=identb[:]),
                                reads=[("E4", r), "identb"], writes=[("ps", pbk)])
                    P.copy(WS[R0:R0 + 32, half * 4:half * 4 + 4, :, :],
                           pv[R0:R0 + 32, :].rearrange("p (k r c) -> p k r c", k=4, r=2), [("ps", pbk)], [("WS", r)],
                           eng="act" if half else "dve")
                ce = CE[q % 2]
                for gi in range(2):
                    P.copy(ce[64 * gi:64 * gi + 64, 0, 16 * gi:16 * gi + 16], cre[64 * gi:64 * gi + 64, m, :],
                           ["s5prm"], [("CE", q % 2)])
                    P.copy(ce[64 * gi:64 * gi + 64, 1, 16 * gi:16 * gi + 16], NCim[64 * gi:64 * gi + 64, m, :],
                           ["NCim"], [("CE", q % 2)])
                for tau in range(8):
                    P.mm(P.psb[7][:, tau * 32:(tau + 1) * 32], E4[r][:, tau, 0, :], ce[:, 0, :], True, False,
                         [("E4", r), ("CE", q % 2)], [("ps", 7)])
                    P.mm(P.psb[7][:, tau * 32:(tau + 1) * 32], E4[r][:, tau, 1, :], ce[:, 1, :], False, True,
                         [("E4", r), ("CE", q % 2)], [("ps", 7)])
                P.copy(KernE[R0:R0 + 32, :, R0:R0 + 32], P.psb[7][R0:R0 + 32, 0:256].rearrange("p (t c) -> p t c", t=8),
                       [("ps", 7)], [("KernE", r)])
                lre1 = LRE[:, m, 1:9].unsqueeze(2).broadcast_to([128, 8, 16])
                lim1 = LIM[:, m, 1:9].unsqueeze(2).broadcast_to([128, 8, 16])
                nlim1 = NLIM[:, m, 1:9].unsqueeze(2).broadcast_to([128, 8, 16])
                P.tt(R8[:, :, 0, :], b16(cre[:, m, :]), lre1, ALU.mult, ["s5prm", "LRE"], ["R8"])
                P.tt(R8b[:, :, 0, :], b16(NCim[:, m, :]), lim1, ALU.mult, ["NCim", "LIM"], ["R8b"])
                P.tt(R8[:, :, 1, :], b16(cre[:, m, :]), nlim1, ALU.mult, ["s5prm", "NLIM"], ["R8"])
                P.tt(R8b[:, :, 1, :], b16(NCim[:, m, :]), lre1, ALU.mult, ["NCim", "LRE"], ["R8b"])
                P.tt(R8[:], R8[:], R8b[:], ALU.add, ["R8", "R8b"], ["R8"])
                for gi in range(2):
                    P.copy(R4E[r][64 * gi:64 * gi + 64, :, :, R0 + 16 * gi:R0 + 16 * gi + 16], R8[64 * gi:64 * gi + 64],
                           ["R8"], [("R4E", r)], eng="act" if gi else "dve")
                P.memset(Hfull[r][:, :, 0:1], 0.0, [("Hfull", r)])
                P.memset(hcar[:, r, :], 0.0, [("hcar", r)])
            for (j0, n) in jblocks:
                first_in_bank = [True] * 4
                for r in range(RPT):
                    q = qt * RPT + r
                    m = d * NP + q
                    R0 = 32 * r
                    tp = (R0, 0)
                    for ri in range(2):
                        for s_ in range(8):
                            c0 = 8 * j0 + s_
                            P.mm(P.psb[4][:, ri * NB:ri * NB + n], WS[R0:R0 + 32, 7 - s_, ri, :],
                                 ub[R0:R0 + 32, c0:c0 + 8 * (n - 1) + 1:8], s_ == 0, s_ == 7,
                                 [("WS", r), "ub"], [("ps", 4)], tile_position=tp)
                    P.ts(xj[:, :n], JJ[:, j0:j0 + n], X[:, m, 8:9], ALU.mult, ["s5prm", "X"], ["xj"])
                    P.copy(xji[:, :n], xj[:, :n], ["xj"], ["xji"])
                    P.copy(xjf[:, :n], xji[:, :n], ["xji"], ["xjf"])
                    P.tt(xj[:, :n], xj[:, :n], xjf[:, :n], ALU.subtract, ["xj", "xjf"], ["xj"])
                    P.act(snj[:, :n], xj[:, :n], AF.Sin, ["xj"], ["snj"], scale=TWO_PI)
                    P.act(csj[:, :n], xj[:, :n], AF.Sin, ["xj"], ["csj"], scale=math.pi)
                    P.act(csj[:, :n], csj[:, :n], AF.Square, ["csj"], ["csj"])
                    P.ts(csj[:, :n], csj[:, :n], -2.0, ALU.mult, ["csj"], ["csj"], s2=1.0, op1=ALU.add)
                    S2 = P.psb[4][:, 0:2 * NB].rearrange("p (r n) -> p r n", r=2)
                    csb = csj[:, :n].unsqueeze(1).broadcast_to([128, 2, n])
                    snb = snj[:, :n].unsqueeze(1).broadcast_to([128, 2, n])
                    P.tt(va[:, :, :n], S2[:, :, :n], csb, ALU.mult, [("ps", 4), "csj"], ["va"])
                    P.tt(vb[:, :, :n], S2[:, :, :n], snb, ALU.mult, [("ps", 4), "snj"], ["vb"])
                    P.tt(vv[:, 0, :n], va[:, 0, :n], vb[:, 1, :n], ALU.add, ["va", "vb"], ["vv"])
                    P.tt(vv[:, 1, :n], va[:, 1, :n], vb[:, 0, :n], ALU.subtract, ["va", "vb"], ["vv"])
                    P.ts(r8t[:, :n], onesf[:, :n], mag[:, m, 8:9], ALU.mult, ["onesf", "mag"], ["r8t"])
                    for ri in range(2):
                        P.op("dve", lambda e, ri=ri, r=r, n=n: e.tensor_tensor_scan(
                            out=hh[:, ri, :n], data0=r8t[:, :n], data1=vv[:, ri, :n], initial=hcar[:, r, ri:ri + 1],
                            op0=ALU.mult, op1=ALU.add), reads=["r8t", "vv", ("hcar", r)], writes=["hh"])
                    P.copy(hcar[:, r, :], hh[:, :, n - 1], ["hh"], [("hcar", r)])
                    P.tt(ha[:, :, :n], hh[:, :, :n], csb, ALU.mult, ["hh", "csj"], ["ha"])
                    P.tt(hb[:, :, :n], hh[:, :, :n], snb, ALU.mult, ["hh", "snj"], ["hb"])
                    P.tt(Hfull[r][:, 0, 1:n + 1], ha[:, 0, :n], hb[:, 1, :n], ALU.subtract, ["ha", "hb"], [("Hfull", r)])
                    P.tt(Hfull[r][:, 1, 1:n + 1], ha[:, 1, :n], hb[:, 0, :n], ALU.add, ["ha", "hb"], [("Hfull", r)])
                    for t in range(8):
                        bk = t // 2
                        yreg = P.psb[bk][:, (t % 2) * NB:(t % 2) * NB + n]
                        for ri in range(2):
                            P.mm(yreg, R4E[r][:, t, ri, :], Hfull[r][:, ri, 0:n], first_in_bank[bk], False,
                                 [("R4E", r), ("Hfull", r)], [("ps", bk)], skip_group_check=True)
                            first_in_bank[bk] = False
                        for s_ in range(t + 1):
                            c0 = 8 * j0 + s_
                            P.mm(yreg, KernE[R0:R0 + 32, t - s_, :], ub[R0:R0 + 32, c0:c0 + 8 * (n - 1) + 1:8], False,
                                 (r == RPT - 1 and s_ == t), [("KernE", r), "ub"], [("ps", bk)], tile_position=tp,
                                 skip_group_check=True)
                    P.copy(Hfull[r][:, :, 0:1], Hfull[r][:, :, n:n + 1], [("Hfull", r)], [("Hfull", r)], eng="pool")
                ysb, ky = ysb_r.next()
                for t in range(8):
                    bk = t // 2
                    yreg = P.psb[bk][:, (t % 2) * NB:(t % 2) * NB + n]
                    c0 = 8 * j0 + t
                    if d == 0:
                        P.stt(ysb[:, t:t + 8 * (n - 1) + 1:8], uf[:, c0:c0 + 8 * (n - 1) + 1:8], dsk[:, qt:qt + 1], yreg,
                              ALU.mult, ALU.add, ["uf", "s5prm", ("ps", bk)], [ky])
                    else:
                        P.copy(ysb[:, t:t + 8 * (n - 1) + 1:8], yreg, [("ps", bk)], [ky], eng="act")
                P.dma(ydst[qt][:, 8 * j0:8 * (j0 + n)], ysb[:, :8 * n], [ky], [("y", d, qt, j0)])


def body_L2(P, cfg):
    NT, LS = cfg.NT, cfg.LS
    NK = LS
    KT = NK // 128
    G2 = cfg.S5G // 2
    NP = G2 // 2
    NQT = max(1, NP // 4)
    M2 = 2 * NP
    NJ = LS // 8

    def part_attn(P):
        qT_d = P.dram("qT", [cfg.AH, 128, NT], BF16, "ExternalInput")
        KT_d = P.dram("KT", [cfg.KVH, 128, NK], BF16, "ExternalInput")
        V_d = P.dram("V", [128, cfg.KVH, KT, 128], BF16, "ExternalInput")
        aT = P.dram("aT", [cfg.AH, 128, NT], BF16, "ExternalOutput")
        onesb = P.sbuf([128, 128], BF16, "onesb")
        P.memset(onesb[:], 1.0, ["onesb"])
        qS = P.sbuf([128, cfg.AH, NT], BF16, "qS")
        KTs = P.sbuf([128, cfg.KVH, NK], BF16, "KTs")
        Vs = P.sbuf([128, cfg.KVH, KT, 128], BF16, "Vs")
        P.dma(qS[:], qT_d.rearrange("h p n -> p h n"), [], ["qS"])
        P.dma(KTs[:], KT_d.rearrange("h p n -> p h n"), [], ["KT"])
        P.dma(Vs[:], V_d, [], ["V"])
        attention_phase(P, cfg, qS, KTs, Vs, onesb, aT)

    def part_s5(P):
        uF = P.dram("uF", [NQT, 128, LS], F32, "ExternalInput")
        uB = P.dram("uB", [NQT, 128, LS], F32, "ExternalInput")
        yF = P.dram("yF", [NQT, 128, LS], F32, "ExternalOutput")
        yB = P.dram("yB", [NQT, 128, LS], F32, "ExternalOutput")
        names = [("lam_re", [128, M2]), ("lam_im", [128, M2]), ("logdt", [128, M2]), ("bre", [128, M2, 16]),
                 ("bim", [128, M2, 16]), ("cre", [128, M2, 16]), ("cim", [128, M2, 16]), ("dsk", [128, NQT]),
                 ("K9", [128, 9]), ("JJ", [128, NJ])]
        prm = []
        for nm, shp in names:
            dd = P.dram(nm, shp, F32, "ExternalInput", shared=nm in ("K9", "JJ"))
            t = P.sbuf(shp, F32, "p_" + nm)
            P.dma(t[:], dd, [], ["s5prm"])
            prm.append(t)
        ident_d = P.dram("identb", [128, 128], BF16, "ExternalInput", shared=True)
        identb = P.sbuf([128, 128], BF16, "identb")
        P.dma(identb[:], ident_d, [], ["identb"])
        s5_phase(P, cfg, uF, uB, prm, yF, yB, identb)
    fused_body(P, [part_attn, part_s5])


def seg_rev(a, NC):
    return np.concatenate([a[..., :NC][..., ::-1], a[..., NC:][..., ::-1]], -1)


def run_L2(cfg, inp, r1):
    NC, NL, LS = cfg.NC, cfg.NL, cfg.LS
    G2 = cfg.S5G // 2
    NP = G2 // 2
    NQT = max(1, NP // 4)
    CH = cfg.S5W // 2
    NJ = LS // 8
    maps = []
    for core in range(NCORES):
        b, half = core // 2, core % 2
        ra, rb = r1[2 * b], r1[2 * b + 1]
        KTf = np.concatenate([ra["kT"][:, :, :NC], ra["kT"][:, :, NC:], rb["kT"][:, :, NC:]], 2)
        vTf = np.concatenate([ra["vT"][:, :, :NC], ra["vT"][:, :, NC:], rb["vT"][:, :, NC:]], 2)
        Vtok = vTf.transpose(0, 2, 1).reshape(cfg.KVH, LS // 128, 128, 128).transpose(2, 0, 1, 3)
        uTf = np.concatenate([ra["uT"][:, :, :NC], ra["uT"][:, :, NC:], rb["uT"][:, :, NC:]], 2)
        uT2 = uTf.reshape(cfg.S5W, LS)[half * CH:(half + 1) * CH].reshape(NQT, -1, LS)
        if uT2.shape[1] != 128:
            raise ValueError("S5 channel tile must be 128")
        gs = slice(half * G2, (half + 1) * G2)

        def pp(a):
            return np.ascontiguousarray(a.reshape(2, NP, 2, 64).transpose(2, 3, 0, 1).reshape(128, 2 * NP))

        def pp16(a):
            return np.ascontiguousarray(a.reshape(2, NP, 2, 64, 16).transpose(2, 3, 0, 1, 4).reshape(128, 2 * NP, 16))
        ldt = np.broadcast_to(inp["s5_log_dt"][0][:, gs, None], (2, G2, 64))
        m = {"qT": r1[core]["qT"], "KT": np.ascontiguousarray(KTf), "V": np.ascontiguousarray(Vtok),
             "uF": np.ascontiguousarray(uT2), "uB": np.ascontiguousarray(seg_rev(uT2, NC)),
             "lam_re": pp(inp["s5_lam_re"][0][:, gs]), "lam_im": pp(inp["s5_lam_im"][0][:, gs]), "logdt": pp(ldt),
             "bre": pp16(inp["s5_b_re"][0][:, gs]), "bim": pp16(inp["s5_b_im"][0][:, gs]),
             "cre": pp16(inp["s5_c_re"][0][:, gs].transpose(0, 1, 3, 2)),
             "cim": pp16(inp["s5_c_im"][0][:, gs].transpose(0, 1, 3, 2)),
             "dsk": fm(inp["s5_d"][0][half * CH:(half + 1) * CH]),
             "K9": np.broadcast_to(np.arange(9, dtype=np.float32), (128, 9)).copy(),
             "JJ": np.broadcast_to(np.arange(1, NJ + 1, dtype=np.float32), (128, NJ)).copy(),
             "identb": np.eye(128, dtype=np.float32).astype(NPBF)}
        maps.append(m)
    res = launch(lambda P: body_L2(P, cfg), maps, ['K9', 'JJ', 'identb'], 1)
    return [res[c] for c in range(NCORES)]


def gemm_ws(P, ws, w_dram, nts, KC, inT, in_key, blks, evac):
    for nt in nts:
        wb, kw = ws.load(w_dram[nt])
        for blk in blks:
            s, T, lat = blk
            pb = P.bank()
            for kc in range(KC):
                P.mm(P.psb[pb][:, :T], wb[:, kc, :], inT[:, kc, s:s + T], kc == 0, kc == KC - 1,
                     [kw] + colkeys(in_key, s, T), [("ps", pb)])
            evac(nt, blk, pb)


def body_L3a(P, cfg):
    FT, NT, AH = cfg.FT, cfg.NT, cfg.AH
    NU = cfg.S5W // 128
    aT = P.dram("aT", [AH, 128, NT], BF16, "ExternalInput")
    yFd = P.dram("yF", [NU, 128, NT], F32, "ExternalInput")
    yBd = P.dram("yB", [NU, 128, NT], F32, "ExternalInput")
    gw = P.dram("glu_w", [2 * NU, 128, NU, 128], F32, "ExternalInput", shared=True)
    gb_d = P.dram("glu_b", [128, 2 * NU], F32, "ExternalInput", shared=True)
    wo = P.dram("w_out", [FT, 128, FT, 128], F32, "ExternalInput", shared=True)
    yT = P.dram("yT", [FT, 128, NT], F32, "ExternalOutput")
    gb = P.sbuf([128, 2 * NU], F32, "gb")
    P.dma(gb[:], gb_d, [], ["gb"])
    mixT = P.sbuf([128, FT, NT], BF16, "mixT")
    gyT = P.sbuf([128, NU, NT], BF16, "gyT")
    blks = col_blocks(cfg)
    TM = max(w for _, w, _ in blks)
    for h in range(AH):
        P.dma(mixT[:, h, :], aT[h], [], colkeys("mixT", 0, NT))
    ya_r = Rot(P, "ya", [128, TM], F32, 2)
    yb_r = Rot(P, "yb", [128, TM], F32, 2)
    t_r = Rot(P, "tg", [128, TM], F32, 2)
    for (s, T, lat) in blks:
        for j in range(NU):
            ya, ka = ya_r.next()
            yb, kb = yb_r.next()
            tg, kt = t_r.next()
            P.dma(ya[:, :T], yFd[j][:, s:s + T], [], [ka])
            P.dma(yb[:, :T], yBd[j][:, s:s + T], [], [kb])
            P.tt(ya[:, :T], ya[:, :T], yb[:, :T], ALU.add, [ka, kb], [ka], eng="pool")
            P.act(tg[:, :T], ya[:, :T], AF.Square, [ka], [kt])
            P.ts(tg[:, :T], tg[:, :T], 0.044715, ALU.mult, [kt], [kt], s2=1.0, op1=ALU.add)
            P.tt(tg[:, :T], tg[:, :T], ya[:, :T], ALU.mult, [kt, ka], [kt])
            P.act(tg[:, :T], tg[:, :T], AF.Sigmoid, [kt], [kt], scale=2.0 * math.sqrt(2.0 / math.pi))
            P.tt(gyT[:, j, s:s + T], tg[:, :T], ya[:, :T], ALU.mult, [kt, ka], colkeys("gyT", s, T))
    ws = WStream(P, "w3", FT)
    sg_r = Rot(P, "sg", [128, TM], F32, 2)
    for j in range(NU):
        wa, kwa = ws.load_kc(gw[j], NU)
        wg, kwg = ws.load_kc(gw[NU + j], NU)
        for (s, T, lat) in blks:
            pa, pg = P.bank(), P.bank()
            for kc in range(NU):
                P.mm(P.psb[pa][:, :T], wa[:, kc, :], gyT[:, kc, s:s + T], kc == 0, kc == NU - 1, [kwa] + colkeys("gyT", s, T), [("ps", pa)])
            for kc in range(NU):
                P.mm(P.psb[pg][:, :T], wg[:, kc, :], gyT[:, kc, s:s + T], kc == 0, kc == NU - 1, [kwg] + colkeys("gyT", s, T), [("ps", pg)])
            sg, ks = sg_r.next()
            P.act(sg[:, :T], P.psb[pg][:, :T], AF.Sigmoid, [("ps", pg), "gb"], [ks], bias=gb[:, NU + j:NU + j + 1])
            P.stt(mixT[:, AH + j, s:s + T], P.psb[pa][:, :T], gb[:, j:j + 1], sg[:, :T], ALU.add, ALU.mult,
                  [("ps", pa), "gb", ks], colkeys("mixT", s, T))
    yo_r = Rot(P, "yo", [128, TM], F32, 3)

    def evac(nt, blk, pb):
        s, T, lat = blk
        yo, ko = yo_r.next()
        P.copy(yo[:, :T], P.psb[pb][:, :T], [("ps", pb)], [ko], eng="act" if (nt + s // 128) % 2 else "dve")
        P.dma(yT[nt][:, s:s + T], yo[:, :T], [ko], [("yT", nt, s)])
    gemm_ws(P, ws, wo, range(FT), FT, mixT, "mixT", blks, evac)


def _ws_load_kc(self, src, kc):
    P = self.P
    b = self.i % self.nbuf
    self.i += 1
    ks, kb = (self.name, "st", b), (self.name, "wb", b)
    P.dma(self.st[b][:, :kc, :], src, reads=[], writes=[ks])
    P.copy(self.wb[b][:, :kc, :], self.st[b][:, :kc, :], reads=[ks], writes=[kb], eng=self.cast_eng)
    return self.wb[b], kb


WStream.load_kc = _ws_load_kc


def pair_tokens(cfg, arrs, half, axis=-1):
    NC, NL = cfg.NC, cfg.NL
    a = arrs
    idx = np.r_[0:NC, NC + half * NL:NC + (half + 1) * NL]
    return np.take(a, idx, axis=axis)


def run_L3a(cfg, inp, r2):
    FT = cfg.FT
    NU = cfg.S5W // 128
    gw = tile_w(inp["s5_glu_w"][0], NU)
    gb = fm(inp["s5_glu_b"][0])
    wo = tile_w(inp["ab_w_out"][0], FT)
    maps = []
    for core in range(NCORES):
        b, half = core // 2, core % 2
        ra, rb = r2[2 * b], r2[2 * b + 1]
        yFf = np.concatenate([ra["yF"], rb["yF"]], 0).reshape(NU, 128, cfg.LS)
        yBf = np.concatenate([seg_rev(ra["yB"], cfg.NC), seg_rev(rb["yB"], cfg.NC)], 0).reshape(NU, 128, cfg.LS)
        maps.append({"aT": r2[core]["aT"], "yF": np.ascontiguousarray(pair_tokens(cfg, yFf, half)),
                     "yB": np.ascontiguousarray(pair_tokens(cfg, yBf, half)), "glu_w": gw, "glu_b": gb, "w_out": wo})
    res = launch(lambda P: body_L3a(P, cfg), maps, ['glu_w', 'glu_b', 'w_out'], VW)
    return [res[c]["yT"] for c in range(NCORES)]


def resid_phase(P, cfg, xT, yT, xoT, hoT, mods_g, gS_g, gate_j, mods_h, gS_h, h_sel, ones, hT_res=None):
    FT = cfg.FT
    sc_g = mod_scalars(P, mods_g, gS_g, cfg)
    sc_h = sc_g if (mods_h is mods_g) else (mod_scalars(P, mods_h, gS_h, cfg) if h_sel is not None else None)
    blks = col_blocks(cfg, 256)
    TM = max(w for _, w, _ in blks)
    ys_r = Rot(P, "ys", [128, FT, TM], F32, 2)
    xs_r = Rot(P, "xr", [128, FT, TM], F32, 2)
    hb_r = Rot(P, "hb", [128, FT, TM], BF16, 2) if (h_sel is not None and hT_res is None) else None
    sq_r = Rot(P, "sqr", [128, TM], F32, 2)
    tmp_r = Rot(P, "tmr", [128, TM], F32, 2)
    rs_t = P.sbuf([128, TM], F32, "rs_t")
    rstd_t = P.sbuf([128, TM], F32, "rstd_t")
    for (s, T, lat) in blks:
        ys, ky = ys_r.next()
        xs, kx = xs_r.next()
        P.dma(ys[:, :, :T], yT.rearrange("f p n -> p f n")[:, :, s:s + T], [], [ky])
        P.dma(xs[:, :, :T], xT[:, :, s:s + T], [], [kx])
        norm_block(P, cfg, ys, ky, T, ones, sq_r, rs_t, rstd_t, FT, cfg.D)
        for ft in range(FT):
            tmp, kt = tmp_r.next()
            P.tt(tmp[:, :T], ys[:, ft, :T], rstd_t[:, :T], ALU.mult, [ky, "rstd_t"], [kt])
            P.stt(xs[:, ft, :T], tmp[:, :T], sc_g[:, gate_j, ft, lat:lat + 1], xs[:, ft, :T], ALU.mult, ALU.add,
                  [kt, ("modsc", gate_j, lat), kx], [kx])
        P.dma(xoT[:, :, s:s + T], xs[:, :, :T], [kx], [("xo", s)])
        if h_sel is not None:
            norm_block(P, cfg, xs, kx, T, ones, sq_r, rs_t, rstd_t, FT, cfg.D)
            bidx = 0 if h_sel == 0 else 3
            if hT_res is None:
                hb, kh = hb_r.next()
            for ft in range(FT):
                tmp, kt = tmp_r.next()
                P.tt(tmp[:, :T], xs[:, ft, :T], rstd_t[:, :T], ALU.mult, [kx, "rstd_t"], [kt])
                if hT_res is None:
                    P.act(hb[:, ft, :T], tmp[:, :T], AF.Identity, [kt, ("modsc", h_sel, lat), "mods"], [kh],
                          scale=sc_h[:, h_sel, ft, lat:lat + 1], bias=mods_h[:, bidx, ft, lat:lat + 1])
                else:
                    P.act(hT_res[:, ft, s:s + T], tmp[:, :T], AF.Identity, [kt, ("modsc", h_sel, lat), "mods"],
                          colkeys("hT", s, T), scale=sc_h[:, h_sel, ft, lat:lat + 1], bias=mods_h[:, bidx, ft, lat:lat + 1])
            if hT_res is None:
                P.dma(hoT[:, :, s:s + T], hb[:, :, :T], [kh], [("ho", s)])


def body_R(P, cfg, gate_j, h_sel, yname="yT"):
    FT, NT = cfg.FT, cfg.NT
    xT = P.dram("xT", [128, FT, NT], F32, "ExternalInput")
    yT = P.dram(yname, [FT, 128, NT], F32, "ExternalInput")
    mods_d = P.dram("mods", [128, 6, FT, 2], F32, "ExternalInput")
    gS_d = P.dram("gS", [128, 4, FT], F32, "ExternalInput", shared=True)
    xoT = P.dram("xoT", [128, FT, NT], F32, "ExternalOutput")
    hoT = P.dram("hoT", [128, FT, NT], BF16, "ExternalOutput") if h_sel is not None else None
    mods = P.sbuf([128, 6, FT, 2], F32, "mods")
    gS = P.sbuf([128, 4, FT], F32, "gS")
    ones = P.sbuf([128, 128], F32, "ones")
    P.dma(mods[:], mods_d, [], ["mods"])
    P.dma(gS[:], gS_d, [], ["gS"])
    P.memset(ones[:], 1.0, ["ones"])
    resid_phase(P, cfg, xT, yT, xoT, hoT, mods, gS, gate_j, mods, gS, h_sel, ones)


def gS_of(cfg, inp, l):
    return np.ascontiguousarray(inp["norm_g"][l].reshape(4, cfg.FT, 128).transpose(2, 0, 1))


def run_R(cfg, inp, mod, l, xTs, yTs, gate_j, h_sel):
    maps = []
    for core in range(NCORES):
        maps.append({"xT": xTs[core], "yT": yTs[core], "mods": mods_for(cfg, mod, l, core // 2), "gS": gS_of(cfg, inp, l)})
    res = launch(lambda P: body_R(P, cfg, gate_j, h_sel), maps, ['gS'], 1)
    return [res[c] for c in range(NCORES)]


def arena_view(arena, off, shape, dtype):
    n = 1
    for d in shape[1:]:
        n *= d
    nbytes = n * (2 if dtype == BF16 else 4)
    assert off % 4 == 0 and nbytes % 4 == 0
    a = arena[:, off // 4:(off + nbytes) // 4]
    if dtype == BF16:
        a = a.bitcast(BF16)
    if len(shape) == 3:
        a = a.rearrange("p (a b) -> p a b", a=shape[1])
    return a, off + nbytes


def ffn_layout(cfg):
    NC, NL = cfg.NC, cfg.NL
    segs = []
    hl = 0
    if NC:
        segs.append(("c", 0, NC, 0))
        hl = NC + 2
    NLa = (NL - NC) // 2
    mbs = [[], []]

    def pieces(h0, w, o0):
        npc = -(-w // 510)
        base = -(-w // npc)
        out = []
        a = 0
        while a < w:
            ww = min(base, w - a)
            out.append((h0 + a, ww, o0 + a))
            a += ww
        return out
    if NC:
        mbs[0] += pieces(0, NC, 0)
    mbs[0] += pieces(hl, NLa, NC)
    mbs[1] += pieces(hl + NLa, NL - NLa, NC + NLa)
    return mbs


def body_F(P, cfg):
    FT, NT, FC = cfg.FT, cfg.NT, cfg.FC
    WH = NT + (4 if cfg.NC else 2)
    h2h = P.dram("h2h", [128, FT, WH], BF16, "ExternalInput")
    wup = P.dram("w_up", [2 * FC, 128, FT, 128], F32, "ExternalInput", shared=True)
    cw_d = P.dram("cw", [128, 2 * FC, 4], F32, "ExternalInput", shared=True)
    wdn = P.dram("w_dn", [FT, 128, FC, 128], F32, "ExternalInput", shared=True)
    fT = P.dram("fT", [FT, 128, NT], F32, "ExternalOutput")
    cw = P.sbuf([128, 2 * FC, 4], F32, "cw")
    P.dma(cw[:], cw_d, [], ["cw"])
    mbs = ffn_layout(cfg)
    VM = max(sum(w for _, w, _ in mb) for mb in mbs)
    WM = max(sum(w + 2 for _, w, _ in mb) for mb in mbs)
    PW = max(w for mb in mbs for _, w, _ in mb) + 2
    hid = P.sbuf([128, FC, VM], BF16, "hid")
    up_bytes = FT * WM * 2 + 3 * FT * 128 * 4 + 3 * FT * 128 * 2 + 8 * PW * 4
    dn_bytes = 2 * FC * 128 * 4 + 2 * FC * 128 * 2 + 3 * 512 * 4
    arena = P.sbuf([128, (max(up_bytes, dn_bytes) + 3) // 4], F32, "arena")
    off = 0
    h2s, off = arena_view(arena, off, [128, FT, WM], BF16)
    ust, uwb = [], []
    for i in range(3):
        v, off = arena_view(arena, off, [128, FT, 128], F32)
        ust.append(v)
    for i in range(3):
        v, off = arena_view(arena, off, [128, FT, 128], BF16)
        uwb.append(v)
    ctile = []
    for i in range(8):
        v, off = arena_view(arena, off, [128, PW], F32)
        ctile.append(v)
    off = 0
    dst_, dwb = [], []
    for i in range(2):
        v, off = arena_view(arena, off, [128, FC, 128], F32)
        dst_.append(v)
    for i in range(2):
        v, off = arena_view(arena, off, [128, FC, 128], BF16)
        dwb.append(v)
    etile = []
    for i in range(3):
        v, off = arena_view(arena, off, [128, 512], F32)
        etile.append(v)
    uit = [0]
    dit = [0]
    eit = [0]
    for mi, mb in enumerate(mbs):
        lcs = []
        lc = 0
        for (h0, w, o0) in mb:
            P.dma(h2s[:, :, lc:lc + w + 2], h2h[:, :, h0:h0 + w + 2], [], [("h2s", mi, lc)])
            lcs.append(lc)
            lc += w + 2
        for c in range(FC):
            wt = []
            for nt in (c, FC + c):
                b = uit[0] % 3
                uit[0] += 1
                P.dma(ust[b], wup[nt], [], [("ust", mi, b)])
                P.copy(uwb[b], ust[b], [("ust", mi, b)], [("uwb", mi, b)], eng="pool")
                wt.append((uwb[b], ("uwb", mi, b)))
            vo = 0
            for pi, (h0, w, o0) in enumerate(mb):
                lc = lcs[pi]
                N = w + 2
                pa, pv = P.bank(), P.bank()
                for (wb_, kw), pbk in zip(wt, (pa, pv)):
                    for kc in range(FT):
                        P.mm(P.psb[pbk][:, :N], wb_[:, kc, :], h2s[:, kc, lc:lc + N], kc == 0, kc == FT - 1,
                             [kw, ("h2s", mi, lc)], [("ps", pbk)])
                par = (c * len(mb) + pi) % 2
                Ua, Uv, ca, cv = (ctile[par * 4 + i] for i in range(4))
                ka, kv, kca, kcv = (("ct", mi, par, i) for i in range(4))
                P.copy(Ua[:, :N], P.psb[pa][:, :N], [("ps", pa)], [ka], eng="act")
                P.copy(Uv[:, :N], P.psb[pv][:, :N], [("ps", pv)], [kv], eng="dve")
                for (U, ku, cc, kc_, ch) in ((Ua, ka, ca, kca, c), (Uv, kv, cv, kcv, FC + c)):
                    P.act(cc[:, :w], U[:, 1:w + 1], AF.Identity, [ku, "cw"], [kc_], scale=cw[:, ch, 1:2], bias=cw[:, ch, 3:4])
                    P.stt(cc[:, :w], U[:, 0:w], cw[:, ch, 0:1], cc[:, :w], ALU.mult, ALU.add, [ku, "cw", kc_], [kc_])
                    P.stt(cc[:, :w], U[:, 2:w + 2], cw[:, ch, 2:3], cc[:, :w], ALU.mult, ALU.add, [ku, "cw", kc_], [kc_])
                P.act(ca[:, :w], ca[:, :w], AF.Silu, [kca], [kca])
                P.tt(hid[:, c, vo:vo + w], ca[:, :w], cv[:, :w], ALU.mult, [kca, kcv], [("hid", vo)], eng="pool")
                vo += w
        P.barrier()
        for nt in range(FT):
            b = dit[0] % 2
            dit[0] += 1
            P.dma(dst_[b], wdn[nt], [], [("dst", mi, b)])
            P.copy(dwb[b], dst_[b], [("dst", mi, b)], [("dwb", mi, b)], eng="pool")
            vo = 0
            for (h0, w, o0) in mb:
                pbk = P.bank()
                for kc in range(FC):
                    P.mm(P.psb[pbk][:, :w], dwb[b][:, kc, :], hid[:, kc, vo:vo + w], kc == 0, kc == FC - 1,
                         [("dwb", mi, b), ("hid", vo)], [("ps", pbk)])
                e = eit[0] % 3
                eit[0] += 1
                P.copy(etile[e][:, :w], P.psb[pbk][:, :w], [("ps", pbk)], [("et", mi, e)], eng="act" if e % 2 else "dve")
                P.dma(fT[nt][:, o0:o0 + w], etile[e][:, :w], [("et", mi, e)], [("fT", nt, o0)])
                vo += w
        P.barrier()


def halo_h2(cfg, hs, b, half):
    NC, NL = cfg.NC, cfg.NL
    me, other = hs[2 * b + half], hs[2 * b + 1 - half]
    z = np.zeros(me.shape[:2] + (1,), me.dtype)
    parts = []
    if NC:
        parts += [z, me[:, :, :NC], z]
    left = z if half == 0 else other[:, :, NT_last(cfg)]
    right = other[:, :, NC:NC + 1] if half == 0 else z
    parts += [left, me[:, :, NC:], right]
    return np.ascontiguousarray(np.concatenate(parts, 2))


def NT_last(cfg):
    return slice(cfg.NT - 1, cfg.NT)


def run_F(cfg, inp, l, hs):
    FT, FC = cfg.FT, cfg.FC
    wup = tile_w(inp["ffn_w_up"][l], FT)
    wdn = tile_w(inp["ffn_w_down"][l], FC)
    cwb = np.concatenate([inp["ffn_conv_w"][l], inp["ffn_conv_b"][l][None]], 0)
    cw = np.ascontiguousarray(cwb.T.reshape(2 * FC, 128, 4).transpose(1, 0, 2))
    maps = []
    for core in range(NCORES):
        maps.append({"h2h": halo_h2(cfg, hs, core // 2, core % 2), "w_up": wup, "cw": cw, "w_dn": wdn})
    res = launch(lambda P: body_F(P, cfg), maps, ['w_up', 'cw', 'w_dn'], VW)
    return [res[c]["fT"] for c in range(NCORES)]


def body_L4b(P, cfg):
    FT, NT = cfg.FT, cfg.NT
    xT = P.dram("xT", [128, FT, NT], F32, "ExternalInput")
    yT = P.dram("fT", [FT, 128, NT], F32, "ExternalInput")
    md0 = P.dram("mods0", [128, 6, FT, 2], F32, "ExternalInput")
    g0 = P.dram("gS0", [128, 4, FT], F32, "ExternalInput", shared=True)
    md1 = P.dram("mods1", [128, 6, FT, 2], F32, "ExternalInput")
    g1 = P.dram("gS1", [128, 4, FT], F32, "ExternalInput", shared=True)
    lbl_d = P.dram("lbl", [128, 2, 2, FT], F32, "ExternalInput", shared=True)
    wci = P.dram("c_w_in", [5 * FT, 128, FT, 128], F32, "ExternalInput", shared=True)
    xoT = P.dram("xoT", [128, FT, NT], F32, "ExternalOutput")
    qh = P.dram("qh", [FT, 128, NT], BF16, "ExternalOutput")
    fF = P.dram("fF", [FT, 128, NT], F32, "ExternalOutput")
    fB = P.dram("fB", [FT, 128, NT], F32, "ExternalOutput")
    vh = P.dram("vh", [FT, 128, NT], BF16, "ExternalOutput")
    gh = P.dram("gh", [FT, 128, NT], F32, "ExternalOutput")
    mods0 = P.sbuf([128, 6, FT, 2], F32, "mods0"); gS0 = P.sbuf([128, 4, FT], F32, "gS0")
    mods1 = P.sbuf([128, 6, FT, 2], F32, "mods1"); gS1 = P.sbuf([128, 4, FT], F32, "gS1")
    lbl = P.sbuf([128, 2, 2, FT], F32, "lbl"); lb = P.sbuf([128, 2, FT], F32, "lb"); oml = P.sbuf([128, 2, FT], F32, "oml")
    ones = P.sbuf([128, 128], F32, "ones")
    P.dma(mods0[:], md0, [], ["mods"]); P.dma(gS0[:], g0, [], ["gS"])
    P.dma(mods1[:], md1, [], ["mods"]); P.dma(gS1[:], g1, [], ["gS"])
    P.dma(lbl[:], lbl_d, [], ["lbl"])
    P.memset(ones[:], 1.0, ["ones"])
    P.tt(lb[:], lbl[:, 1], lbl[:, 0], ALU.subtract, ["lbl"], ["lb"])
    P.act(lb[:], lb[:], AF.Sigmoid, ["lb"], ["lb"])
    P.ts(oml[:], lb[:], -1.0, ALU.mult, ["lb"], ["oml"], s2=1.0, op1=ALU.add)
    hT = P.sbuf([128, FT, NT], BF16, "hT")
    resid_phase(P, cfg, xT, yT, xoT, None, mods0, gS0, 3, mods1, gS1, 0, ones, hT_res=hT)
    blks = col_blocks(cfg)
    TM = max(w for _, w, _ in blks)
    ws = WStream(P, "wci", FT)
    eo_r = Rot(P, "eo", [128, TM], F32, 3)
    eb_r = Rot(P, "eb", [128, TM], BF16, 3)

    def evac(nt, blk, pb):
        s, T, lat = blk
        kind, j = nt // FT, nt % FT
        ps = P.psb[pb][:, :T]
        if kind == 3:
            eb, ke = eb_r.next()
            P.copy(eb[:, :T], ps, [("ps", pb)], [ke], eng="act")
            P.dma(vh[j][:, s:s + T], eb[:, :T], [ke], [("o", nt, s)])
            return
        if kind == 0:
            eb, ke = eb_r.next()
            P.act(eb[:, :T], ps, AF.Silu, [("ps", pb)], [ke])
            P.dma(qh[j][:, s:s + T], eb[:, :T], [ke], [("o", nt, s)])
            return
        eo, ke = eo_r.next()
        if kind == 0:
            P.act(eo[:, :T], ps, AF.Silu, [("ps", pb)], [ke])
            dst = qh
        elif kind in (1, 2):
            d = kind - 1
            P.act(eo[:, :T], ps, AF.Sigmoid, [("ps", pb)], [ke])
            P.ts(eo[:, :T], eo[:, :T], oml[:, d, j:j + 1], ALU.mult, [ke, "oml", "lb"], [ke], s2=lb[:, d, j:j + 1], op1=ALU.add)
            dst = fF if d == 0 else fB
        else:
            P.copy(eo[:, :T], ps, [("ps", pb)], [ke])
            dst = gh
        P.dma(dst[j][:, s:s + T], eo[:, :T], [ke], [("o", nt, s)])
    gemm_ws(P, ws, wci, range(5 * FT), FT, hT, "hT", blks, evac)


def run_L4b(cfg, inp, mod, xTs, fTs):
    FT = cfg.FT
    wci = tile_w(inp["c_w_in"][0], FT)
    lbl = np.ascontiguousarray(inp["hgrn_lb_logits"].reshape(2, 2, FT, 128).transpose(3, 0, 1, 2))
    maps = []
    for core in range(NCORES):
        b = core // 2
        maps.append({"xT": xTs[core], "yT": fTs[core], "mods0": mods_for(cfg, mod, 0, b), "gS0": gS_of(cfg, inp, 0),
                     "mods1": mods_for(cfg, mod, 1, b), "gS1": gS_of(cfg, inp, 1), "lbl": lbl, "c_w_in": wci})
    res = launch(lambda P: body_L4b(P, cfg), maps, ['gS0', 'gS1', 'lbl', 'c_w_in'], VW)
    return [res[c] for c in range(NCORES)]


def body_L5(P, cfg):
    LS, NC, SEQ = cfg.LS, cfg.NC, cfg.SEQ
    HC = cfg.HH // 2
    NTL = LS // 128
    qd = [P.dram("q%s" % x, [HC, 128, LS], BF16, "ExternalInput") for x in "FB"]
    fd = [P.dram("f%s" % x, [HC, 128, LS], F32, "ExternalInput") for x in "FB"]
    Vd = [P.dram("V%s" % x, [128, NTL, HC, 128], BF16, "ExternalInput") for x in "FB"]
    od = [P.dram("o%s" % x, [HC, 128, SEQ], F32, "ExternalOutput") for x in "FB"]
    m64_d = P.dram("mask64", [128, 64], F32, "ExternalInput", shared=True)
    cm_d = P.dram("cmask", [128, 512], F32, "ExternalInput", shared=True)
    id_d = P.dram("identb", [128, 128], BF16, "ExternalInput", shared=True)
    mask64 = P.sbuf([128, 64], F32, "mask64"); cmask = P.sbuf([128, 512], F32, "cmask")
    identb = P.sbuf([128, 128], BF16, "identb")
    P.dma(mask64[:], m64_d, [], ["mask64"]); P.dma(cmask[:], cm_d, [], ["cmask"]); P.dma(identb[:], id_d, [], ["identb"])
    blks = [(0, NC, 0)] if NC else []
    s = NC
    while s < LS:
        blks.append((s, min(512, LS - s), 1))
        s += 512
    D2 = range(2)
    Vh = [P.sbuf([128, NTL, 128], BF16, "Vh%d" % d) for d in D2]
    qs = [Rot(P, "qs%d" % d, [128, 512], BF16, 2) for d in D2]
    fs = [Rot(P, "fs%d" % d, [128, 512], F32, 2) for d in D2]
    lf = [P.sbuf([128, 512], F32, "lf%d" % d) for d in D2]
    bb = [P.sbuf([128, 512], F32, "bb%d" % d) for d in D2]
    eb = [Rot(P, "eb%d" % d, [128, 512], F32, 2) for d in D2]
    enb = [P.sbuf([128, 512], F32, "enb%d" % d) for d in D2]
    Qt = [Rot(P, "Qt%d" % d, [128, 512], BF16, 2) for d in D2]
    Kt = [Rot(P, "Kt%d" % d, [128, 512], BF16, 2) for d in D2]
    KtT = [Rot(P, "KtT%d" % d, [128, 128], BF16, 2) for d in D2]
    ATm = [Rot(P, "ATm%d" % d, [128, 64], BF16, 2) for d in D2]
    S32 = [P.sbuf([128, 128], F32, "S32_%d" % d) for d in D2]
    Sb = [P.sbuf([128, 128], BF16, "Sb_%d" % d) for d in D2]
    tmpS = [Rot(P, "tmpS%d" % d, [128, 128], F32, 2) for d in D2]
    osb = [Rot(P, "osb%d" % d, [128, 512], F32, 2) for d in D2]
    for h in range(HC):
        for d in D2:
            P.dma(Vh[d][:], Vd[d][:, :, h, :], [], [("Vh", d)])
            P.memset(S32[d][:], 0.0, [("S32", d)])
            P.memset(Sb[d][:], 0.0, [("Sb", d)])
        for (s, T, lat) in blks:
            cur = []
            for d in D2:
                q_, kq = qs[d].next()
                f_, kf = fs[d].next()
                P.dma(q_[:, :T], qd[d][h][:, s:s + T], [], [kq])
                P.dma(f_[:, :T], fd[d][h][:, s:s + T], [], [kf])
                P.act(lf[d][:, :T], f_[:, :T], AF.Ln, [kf], [("lf", d)])
                P.ts(f_[:, :T], f_[:, :T], -1.0, ALU.mult, [kf], [kf], s2=1.0, op1=ALU.add, eng="pool")
                P.op("dve", lambda e, d=d, T=T: e.tensor_tensor_scan(out=bb[d][:, :T], data0=cmask[:, :T], data1=lf[d][:, :T],
                                                                    initial=0.0, op0=ALU.mult, op1=ALU.add),
                     reads=["cmask", ("lf", d)], writes=[("bb", d)])
                e_, ke = eb[d].next()
                P.act(e_[:, :T], bb[d][:, :T], AF.Exp, [("bb", d)], [ke])
                P.act(enb[d][:, :T], bb[d][:, :T], AF.Exp, [("bb", d)], [("enb", d)], scale=-1.0)
                Q_, kQ = Qt[d].next()
                K_, kK = Kt[d].next()
                P.tt(Q_[:, :T], q_[:, :T], e_[:, :T], ALU.mult, [kq, ke], [kQ])
                P.tt(K_[:, :T], f_[:, :T], enb[d][:, :T], ALU.mult, [kf, ("enb", d)], [kK], eng="pool")
                cur.append((Q_, kQ, K_, kK, e_, ke))
            for t128 in range(T // 128):
                tile = (s + t128 * 128) // 128
                ktt = []
                for d in D2:
                    Q_, kQ, K_, kK, e_, ke = cur[d]
                    pv = P.psb[6 + d][:].bitcast(BF16)
                    P.op("pe", lambda e, pv=pv, K_=K_, t128=t128: e.transpose(out=pv[:, 0:128], in_=K_[:, t128 * 128:(t128 + 1) * 128],
                                                                         identity=identb[:]),
                         reads=[kK, "identb"], writes=[("ps", 6 + d)])
                    kt_, kkt = KtT[d].next()
                    P.copy(kt_[:], pv[:, 0:128], [("ps", 6 + d)], [kkt], eng="act")
                    ktt.append((kt_, kkt))
                for c2 in range(2):
                    cc = t128 * 128 + c2 * 64
                    R = slice(64 * c2, 64 * c2 + 64)
                    for d in D2:
                        Q_, kQ, K_, kK, e_, ke = cur[d]
                        kt_, kkt = ktt[d]
                        if lat:
                            P.mm(P.psb[d][R, 0:64], K_[:, cc:cc + 64], Q_[:, cc:cc + 64], True, True, [kK, kQ], [("ps", d)])
                            am, kam = ATm[d].next()
                            P.tt(am[R, :], P.psb[d][R, 0:64], mask64[R, :], ALU.mult, [("ps", d), "mask64"], [kam])
                            P.mm(P.psb[2 + d][:, cc:cc + 64], Vh[d][R, tile, :], am[R, :], True, False, [("Vh", d), kam],
                                 [("ps", 2 + d)])
                            P.mm(P.psb[2 + d][:, cc:cc + 64], Sb[d][:], Q_[:, cc:cc + 64], False, True, [("Sb", d), kQ],
                                 [("ps", 2 + d)])
                        P.mm(P.psb[4 + d][:, 0:128], kt_[R, :], Vh[d][R, tile, :], True, True, [kkt, ("Vh", d)], [("ps", 4 + d)])
                        ts_, kts = tmpS[d].next()
                        P.tt(ts_[:], S32[d][:], P.psb[4 + d][:, 0:128], ALU.add, [("S32", d), ("ps", 4 + d)], [kts])
                        P.act(S32[d][:], ts_[:], AF.Identity, [kts, ke], [("S32", d)], scale=e_[:, cc + 63:cc + 64])
                        P.act(Sb[d][:], ts_[:], AF.Identity, [kts, ke], [("Sb", d)], scale=e_[:, cc + 63:cc + 64])
            if lat:
                for d in D2:
                    o_, ko = osb[d].next()
                    P.copy(o_[:, :T], P.psb[2 + d][:, :T], [("ps", 2 + d)], [ko], eng="dve" if d else "act")
                    P.dma(od[d][h][:, s - NC:s - NC + T], o_[:, :T], [ko], [("o", d, h, s)])


def run_L5(cfg, r4):
    NC, NL, LS, FT = cfg.NC, cfg.NL, cfg.LS, cfg.FT
    HC = cfg.HH // 2
    m64 = np.zeros((128, 64), np.float32)
    for p in range(128):
        m64[p, (p % 64):] = 1.0
    cm = np.ones((128, 512), np.float32)
    cm[:, ::64] = 0.0
    ident = np.eye(128, dtype=np.float32).astype(NPBF)
    maps = []
    for core in range(NCORES):
        b, hh = core // 2, core % 2
        ra, rb = r4[2 * b], r4[2 * b + 1]
        hs = slice(hh * HC, (hh + 1) * HC)

        def full(nm):
            return np.concatenate([ra[nm][hs][:, :, :NC], ra[nm][hs][:, :, NC:], rb[nm][hs][:, :, NC:]], 2)
        q = full("qh")
        v = full("vh")

        def vtok(vv):
            return np.ascontiguousarray(vv.transpose(2, 0, 1).reshape(LS // 128, 128, HC, 128).transpose(1, 0, 2, 3))
        maps.append({"qF": np.ascontiguousarray(q), "qB": np.ascontiguousarray(seg_rev(q, NC)),
                     "fF": np.ascontiguousarray(full("fF")), "fB": np.ascontiguousarray(seg_rev(full("fB"), NC)),
                     "VF": vtok(v), "VB": vtok(seg_rev(v, NC)), "mask64": m64, "cmask": cm, "identb": ident})
    res = launch(lambda P: body_L5(P, cfg), maps, ['mask64', 'cmask', 'identb'], 1)
    return [res[c] for c in range(NCORES)]


def body_L6a(P, cfg):
    FT, NT = cfg.FT, cfg.NT
    oFd = P.dram("oF", [FT, 128, NT], F32, "ExternalInput")
    oBd = P.dram("oB", [FT, 128, NT], F32, "ExternalInput")
    ghd = P.dram("gh", [FT, 128, NT], F32, "ExternalInput")
    hgn_d = P.dram("hgn", [128, 1], F32, "ExternalInput", shared=True)
    wo = P.dram("c_w_out", [FT, 128, FT, 128], F32, "ExternalInput", shared=True)
    yT = P.dram("yT", [FT, 128, NT], F32, "ExternalOutput")
    hgn = P.sbuf([128, 1], F32, "hgn")
    ones = P.sbuf([128, 128], F32, "ones")
    P.dma(hgn[:], hgn_d, [], ["hgn"])
    P.memset(ones[:], 1.0, ["ones"])
    mixT = P.sbuf([128, FT, NT], BF16, "mixT")
    blks = col_blocks(cfg)
    TM = max(w for _, w, _ in blks)
    oa_r = Rot(P, "oa", [128, TM], F32, 2)
    ob_r = Rot(P, "ob", [128, TM], F32, 2)
    g_r = Rot(P, "gg", [128, TM], F32, 2)
    sq_r = Rot(P, "sq", [128, TM], F32, 2)
    rs = P.sbuf([128, TM], F32, "rs")
    rstd = P.sbuf([128, TM], F32, "rstd")
    for (s, T, lat) in blks:
        for j in range(FT):
            oa, ka = oa_r.next()
            ob, kb = ob_r.next()
            gg, kg = g_r.next()
            sq, ks = sq_r.next()
            P.dma(oa[:, :T], oFd[j][:, s:s + T], [], [ka])
            P.dma(ob[:, :T], oBd[j][:, s:s + T], [], [kb])
            P.dma(gg[:, :T], ghd[j][:, s:s + T], [], [kg])
            P.tt(oa[:, :T], oa[:, :T], ob[:, :T], ALU.add, [ka, kb], [ka], eng="pool")
            P.act(sq[:, :T], oa[:, :T], AF.Square, [ka], [ks])
            pb = P.bank()
            P.mm(P.psb[pb][:, :T], ones[:], sq[:, :T], True, True, [ks, "ones"], [("ps", pb)])
            P.act(rs[:, :T], P.psb[pb][:, :T], AF.Sqrt, [("ps", pb)], ["rs"], scale=1.0 / 128, bias=EPS)
            P.op("dve", lambda e, T=T: e.reciprocal(out=rstd[:, :T], in_=rs[:, :T]), reads=["rs"], writes=["rstd"])
            P.stt(oa[:, :T], oa[:, :T], hgn[:, 0:1], rstd[:, :T], ALU.mult, ALU.mult, [ka, "hgn", "rstd"], [ka])
            P.act(gg[:, :T], gg[:, :T], AF.Silu, [kg], [kg])
            P.tt(mixT[:, j, s:s + T], oa[:, :T], gg[:, :T], ALU.mult, [ka, kg], colkeys("mixT", s, T), eng="pool")
    ws = WStream(P, "wco", FT)
    yo_r = Rot(P, "yo", [128, TM], F32, 3)

    def evac(nt, blk, pb):
        s, T, lat = blk
        yo, ko = yo_r.next()
        P.copy(yo[:, :T], P.psb[pb][:, :T], [("ps", pb)], [ko], eng="act" if (nt + s // 128) % 2 else "dve")
        P.dma(yT[nt][:, s:s + T], yo[:, :T], [ko], [("yT", nt, s)])
    gemm_ws(P, ws, wo, range(FT), FT, mixT, "mixT", blks, evac)


def run_L6a(cfgL, cfg, inp, r4, r5):
    FT, NC, NL = cfg.FT, cfg.NC, cfg.NL
    wo = tile_w(inp["c_w_out"][0], FT)
    hgn = np.ascontiguousarray(inp["hgrn_norm"][0].reshape(128, 1))
    maps = []
    for core in range(NCORES):
        b, half = core // 2, core % 2
        ra, rb = r5[2 * b], r5[2 * b + 1]
        oF = np.concatenate([ra["oF"], rb["oF"]], 0)
        oB = np.concatenate([ra["oB"], rb["oB"]], 0)[:, :, ::-1]
        sl = slice(half * NL, (half + 1) * NL)
        maps.append({"oF": np.ascontiguousarray(oF[:, :, sl]), "oB": np.ascontiguousarray(oB[:, :, sl]),
                     "gh": np.ascontiguousarray(r4[core]["gh"][:, :, NC:]), "hgn": hgn, "c_w_out": wo})
    res = launch(lambda P: body_L6a(P, cfgL), maps, ['hgn', 'c_w_out'], VW)
    return [res[c]["yT"] for c in range(NCORES)]


def fused_body(P, bodies):
    outer = P.es
    for i, b in enumerate(bodies):
        sub = ExitStack()
        P.es = sub
        b(P)
        P.flush()
        sub.close()
    P.es = outer


def run_L3aR(cfg, inp, mod, r2, xTs):
    FT = cfg.FT
    NU = cfg.S5W // 128
    gw = tile_w(inp["s5_glu_w"][0], NU)
    gb = fm(inp["s5_glu_b"][0])
    wo = tile_w(inp["ab_w_out"][0], FT)
    maps = []
    for core in range(NCORES):
        b, half = core // 2, core % 2
        ra, rb = r2[2 * b], r2[2 * b + 1]
        yFf = np.concatenate([ra["yF"], rb["yF"]], 0).reshape(NU, 128, cfg.LS)
        yBf = np.concatenate([seg_rev(ra["yB"], cfg.NC), seg_rev(rb["yB"], cfg.NC)], 0).reshape(NU, 128, cfg.LS)
        maps.append({"aT": r2[core]["aT"], "yF": np.ascontiguousarray(pair_tokens(cfg, yFf, half)),
                     "yB": np.ascontiguousarray(pair_tokens(cfg, yBf, half)), "glu_w": gw, "glu_b": gb, "w_out": wo,
                     "xT": xTs[core], "mods": mods_for(cfg, mod, 0, b), "gS": gS_of(cfg, inp, 0)})
    return launch(lambda P: fused_body(P, [lambda P: body_L3a(P, cfg), lambda P: body_R(P, cfg, 1, 2)]), maps,
                  ["glu_w", "glu_b", "w_out", "gS"], VW, internal=["yT"])


def run_FL4b(cfg, inp, mod, hs, xTs):
    FT, FC = cfg.FT, cfg.FC
    l = 0
    wup = tile_w(inp["ffn_w_up"][l], FT)
    wdn = tile_w(inp["ffn_w_down"][l], FC)
    cwb = np.concatenate([inp["ffn_conv_w"][l], inp["ffn_conv_b"][l][None]], 0)
    cw = np.ascontiguousarray(cwb.T.reshape(2 * FC, 128, 4).transpose(1, 0, 2))
    wci = tile_w(inp["c_w_in"][0], FT)
    lbl = np.ascontiguousarray(inp["hgrn_lb_logits"].reshape(2, 2, FT, 128).transpose(3, 0, 1, 2))
    maps = []
    for core in range(NCORES):
        b = core // 2
        maps.append({"h2h": halo_h2(cfg, hs, b, core % 2), "w_up": wup, "cw": cw, "w_dn": wdn,
                     "xT": xTs[core], "mods0": mods_for(cfg, mod, 0, b), "gS0": gS_of(cfg, inp, 0),
                     "mods1": mods_for(cfg, mod, 1, b), "gS1": gS_of(cfg, inp, 1), "lbl": lbl, "c_w_in": wci})
    return launch(lambda P: fused_body(P, [lambda P: body_F(P, cfg), lambda P: body_L4b(P, cfg)]), maps,
                  ["w_up", "cw", "w_dn", "gS0", "gS1", "lbl", "c_w_in"], VW, internal=["fT"])


def run_L6aR(cfgL, cfg, inp, mod, r4, r5, x1):
    FT, NC, NL = cfg.FT, cfg.NC, cfg.NL
    wo = tile_w(inp["c_w_out"][0], FT)
    hgn = np.ascontiguousarray(inp["hgrn_norm"][0].reshape(128, 1))
    maps = []
    for core in range(NCORES):
        b, half = core // 2, core % 2
        ra, rb = r5[2 * b], r5[2 * b + 1]
        oF = np.concatenate([ra["oF"], rb["oF"]], 0)
        oB = np.concatenate([ra["oB"], rb["oB"]], 0)[:, :, ::-1]
        sl = slice(half * NL, (half + 1) * NL)
        maps.append({"oF": np.ascontiguousarray(oF[:, :, sl]), "oB": np.ascontiguousarray(oB[:, :, sl]),
                     "gh": np.ascontiguousarray(r4[core]["gh"][:, :, NC:]), "hgn": hgn, "c_w_out": wo,
                     "xT": x1[core], "mods": mods_for(cfg, mod, 1, b), "gS": gS_of(cfg, inp, 1)})
    return launch(lambda P: fused_body(P, [lambda P: body_L6a(P, cfgL), lambda P: body_R(P, cfgL, 1, 2)]), maps,
                  ["hgn", "c_w_out", "gS"], VW, internal=["yT"])


def run_FR2(cfgL, inp, mod, hs, xTs):
    FT, FC = cfgL.FT, cfgL.FC
    l = 1
    wup = tile_w(inp["ffn_w_up"][l], FT)
    wdn = tile_w(inp["ffn_w_down"][l], FC)
    cwb = np.concatenate([inp["ffn_conv_w"][l], inp["ffn_conv_b"][l][None]], 0)
    cw = np.ascontiguousarray(cwb.T.reshape(2 * FC, 128, 4).transpose(1, 0, 2))
    maps = []
    for core in range(NCORES):
        b = core // 2
        maps.append({"h2h": halo_h2(cfgL, hs, b, core % 2), "w_up": wup, "cw": cw, "w_dn": wdn,
                     "xT": xTs[core], "mods": mods_for(cfgL, mod, 1, b), "gS": gS_of(cfgL, inp, 1)})
    return launch(lambda P: fused_body(P, [lambda P: body_F(P, cfgL), lambda P: body_R(P, cfgL, 3, None, "fT")]), maps,
                  ["w_up", "cw", "w_dn", "gS"], VW, internal=["fT"])


def forward(cfg, inp):
    import copy
    inp = {k: np.asarray(v) for k, v in inp.items()}
    FT, NC, NL = cfg.FT, cfg.NC, cfg.NL
    cfgL = copy.copy(cfg)
    cfgL.NC = 0
    cfgL.NT = NL
    mod = run_L0(cfg, inp)
    r1 = run_L1(cfg, inp, mod)
    r2 = run_L2(cfg, inp, r1)
    del r1
    xTs = []
    for core in range(NCORES):
        b, half = core // 2, core % 2
        X = np.concatenate([inp["ctx"][b], inp["x"][b, half * NL:(half + 1) * NL]], 0)
        xTs.append(tok_fm(X, FT))
    rr = run_L3aR(cfg, inp, mod, r2, xTs)
    del r2, xTs
    r4 = run_FL4b(cfg, inp, mod, [r["hoT"] for r in rr], [r["xoT"] for r in rr])
    del rr
    r5 = run_L5(cfg, r4)
    x1 = [np.ascontiguousarray(r["xoT"][:, :, NC:]) for r in r4]
    rr = run_L6aR(cfgL, cfg, inp, mod, r4, r5, x1)
    del r4, r5, x1
    ro = run_FR2(cfgL, inp, mod, [r["hoT"] for r in rr], [r["xoT"] for r in rr])
    out = np.zeros((cfg.BATCH, cfg.SEQ, cfg.D), np.float32)
    for core in range(NCORES):
        b, half = core // 2, core % 2
        out[b, half * NL:(half + 1) * NL] = ro[core]["xoT"].transpose(2, 1, 0).reshape(NL, cfg.D)
    return out


def kernel(**inputs):
    return forward(Cfg(), inputs)
```

```python
import math
import numpy as np
import ml_dtypes
from contextlib import ExitStack
import concourse.bass as bass
import concourse.mybir as mybir
from concourse.bass_utils import run_bass_kernel_spmd

F32 = mybir.dt.float32
BF16 = mybir.dt.bfloat16
I32 = mybir.dt.int32
AF = mybir.ActivationFunctionType
ALU = mybir.AluOpType
NPBF = ml_dtypes.bfloat16
NCORES = 8
VW = 1
EPS = 1e-6


class Cfg:
    def __init__(self, D_MODEL=2048, SEQ=4096, CTX_LEN=256):
        self.D = D_MODEL
        self.SEQ = SEQ
        self.NC = CTX_LEN
        self.BATCH = 4
        self.FT = D_MODEL // 128
        self.NL = SEQ // 2
        self.NT = self.NC + self.NL
        self.LS = self.NC + SEQ
        self.AH = D_MODEL // 256
        self.KVH = max(1, self.AH // 4)
        self.AG = self.AH // self.KVH
        self.AW = self.AH * 128
        self.KVW = self.KVH * 128
        self.S5W = D_MODEL // 2
        self.S5G = self.S5W // 16
        self.AB_IN = self.AW + 2 * self.KVW + self.S5W
        self.HH = D_MODEL // 128
        self.DFF = ((8 * D_MODEL) // 3 + 255) // 256 * 256
        self.FC = self.DFF // 128


class Prog:
    ENG = ("pe", "act", "dve", "pool", "sp")
    NDMA = 12

    def __init__(self, nc):
        self.nc = nc
        self.es = ExitStack()
        self.ops = {e: [] for e in self.ENG}
        self.count = {e: 0 for e in self.ENG}
        self.sem = {e: self.es.enter_context(nc.semaphore("prg_" + e)) for e in self.ENG}
        self.dma_sems = {}
        self.dma_cnt = {}
        self.dma_rr = {}
        for e in ("sp", "act", "pool"):
            self.dma_sems[e] = [self.es.enter_context(nc.semaphore("dq_%s_%d" % (e, i)))
                                for i in range(self.NDMA)]
            self.dma_cnt[e] = [0] * self.NDMA
            self.dma_rr[e] = 0
        self.waited = {e: {} for e in self.ENG}
        self.last_w = {}
        self.readers = {}
        self.semid = {}
        self.ntiles = 0
        self.psb = [self.psum([128, 512], F32, "psb%d" % i) for i in range(8)]
        self.ps_rr = 0
        self.V = 1
        self.v = 0
        self.dcache = {}

    def sbuf(self, shape, dtype, name=None):
        self.ntiles += 1
        name = (name or "t") + "_%d" % self.ntiles
        return self.es.enter_context(self.nc.sbuf_tensor(name, list(shape), dtype))

    def psum(self, shape, dtype, name=None):
        self.ntiles += 1
        name = (name or "p") + "_%d" % self.ntiles
        return self.es.enter_context(self.nc.psum_tensor(name, list(shape), dtype))

    def dram(self, name, shape, dtype, kind, shared=False):
        if name not in self.dcache:
            if name in getattr(self, "internal", ()):
                kind = "Internal"
            shp = list(shape) if shared else [self.V] + list(shape)
            self.dcache[name] = (self.nc.dram_tensor(name, shp, dtype, kind=kind).ap(), shared)
        ap, sh = self.dcache[name]
        return ap if sh else ap[self.v]

    def op(self, eng, fn, reads=(), writes=(), dma=False):
        writes = list(writes) + [k for k in reads if isinstance(k, tuple) and k[0] == "ps"]
        deps = {}

        def add(tok):
            s, v = tok
            k = id(s)
            self.semid[k] = s
            if deps.get(k, 0) < v:
                deps[k] = v

        for k in reads:
            w = self.last_w.get(k)
            if w is not None:
                add(w)
        for k in writes:
            w = self.last_w.get(k)
            if w is not None:
                add(w)
            for sk, v in self.readers.get(k, {}).items():
                add((self.semid[sk], v))
        if dma:
            slot = self.dma_rr[eng]
            self.dma_rr[eng] = (slot + 1) % self.NDMA
            sem = self.dma_sems[eng][slot]
            prev = self.dma_cnt[eng][slot]
            if prev > 0:
                add((sem, prev))
            val = prev + 16
            self.dma_cnt[eng][slot] = val
            inc = 16
        else:
            self.count[eng] += 1
            sem = self.sem[eng]
            val = self.count[eng]
            inc = 1
        tok = (sem, val)
        self.semid[id(sem)] = sem
        own = id(self.sem[eng])
        waits = []
        for sk, v in deps.items():
            if sk == own and eng == "pe":
                continue
            if self.waited[eng].get(sk, 0) >= v:
                continue
            self.waited[eng][sk] = v
            waits.append((self.semid[sk], v))
        self.ops[eng].append((waits, fn, sem, inc))
        for k in writes:
            self.last_w[k] = tok
            self.readers[k] = {}
        for k in reads:
            d = self.readers.setdefault(k, {})
            if d.get(id(sem), 0) < val:
                d[id(sem)] = val
        return tok

    def barrier(self):
        toks = []
        for e in self.ENG:
            if self.count[e] > 0:
                toks.append((self.sem[e], self.count[e]))
        for e in self.dma_sems:
            for s, c in zip(self.dma_sems[e], self.dma_cnt[e]):
                if c > 0:
                    toks.append((s, c))
        for e in self.ENG:
            waits = []
            for s, v in toks:
                if self.waited[e].get(id(s), 0) >= v:
                    continue
                if e == "pe" and s is self.sem["pe"]:
                    continue
                self.waited[e][id(s)] = v
                waits.append((s, v))
            if waits:
                self.ops[e].append((waits, None, None, 0))

    def finish(self):
        self.flush()
        self.es.close()

    def flush(self):
        self.barrier()
        nc = self.nc
        with nc.Block() as block:
            @block.tensor
            def _(e):
                self._emit("pe", e)

            @block.scalar
            def _(e):
                self._emit("act", e)

            @block.vector
            def _(e):
                self._emit("dve", e)

            @block.gpsimd
            def _(e):
                self._emit("pool", e)

            @block.sync
            def _(e):
                self._emit("sp", e)
        for e in self.ENG:
            self.ops[e] = []

    def _emit(self, name, e):
        for waits, fn, sem, inc in self.ops[name]:
            for (s, v) in waits:
                e.wait_ge(s, v)
            if fn is None:
                continue
            ins = fn(e)
            ins.then_inc(sem, inc)

    def dma(self, out, in_, reads, writes, eng="sp"):
        return self.op(eng, lambda e: e.dma_start(out=out, in_=in_), reads=reads, writes=writes, dma=True)

    def mm(self, out, lhsT, rhs, start, stop, reads, writes, **kw):
        return self.op("pe", lambda e: e.matmul(out, lhsT=lhsT, rhs=rhs, start=start, stop=stop, **kw),
                       reads=reads, writes=writes)

    def act(self, out, in_, func, reads, writes, scale=1.0, bias=None, eng="act"):
        if bias is None:
            return self.op(eng, lambda e: e.activation(out=out, in_=in_, func=func, scale=scale),
                           reads=reads, writes=writes)
        return self.op(eng, lambda e: e.activation(out=out, in_=in_, func=func, scale=scale, bias=bias),
                       reads=reads, writes=writes)

    def tt(self, out, in0, in1, op, reads, writes, eng="dve"):
        return self.op(eng, lambda e: e.tensor_tensor(out=out, in0=in0, in1=in1, op=op), reads=reads, writes=writes)

    def ts(self, out, in0, s1, op0, reads, writes, s2=None, op1=None, eng="dve"):
        if op1 is None:
            return self.op(eng, lambda e: e.tensor_scalar(out=out, in0=in0, scalar1=s1, scalar2=None, op0=op0),
                           reads=reads, writes=writes)
        return self.op(eng, lambda e: e.tensor_scalar(out=out, in0=in0, scalar1=s1, scalar2=s2, op0=op0, op1=op1),
                       reads=reads, writes=writes)

    def stt(self, out, in0, scalar, in1, op0, op1, reads, writes):
        return self.op("dve", lambda e: e.scalar_tensor_tensor(out=out, in0=in0, scalar=scalar, in1=in1,
                                                               op0=op0, op1=op1), reads=reads, writes=writes)

    def copy(self, out, in_, reads, writes, eng="dve"):
        if eng == "act":
            return self.op("act", lambda e: e.copy(out=out, in_=in_), reads=reads, writes=writes)
        return self.op(eng, lambda e: e.tensor_copy(out=out, in_=in_), reads=reads, writes=writes)

    def memset(self, ap, val, writes, eng="pool"):
        return self.op(eng, lambda e: e.memset(ap, val), writes=writes)

    def bank(self):
        i = self.ps_rr
        self.ps_rr = (i + 1) % 8
        return i


def colkeys(name, s, T):
    return [(name, c) for c in range(s // 128, (s + T + 127) // 128)]


def build_prog(body, V=1, internal=()):
    nc = bass.Bass("TRN2", target_bir_lowering=False)
    P = Prog(nc)
    P.V = V
    P.internal = set(internal)
    outer = P.es
    for v in range(V):
        P.v = v
        sub = ExitStack()
        P.es = sub
        body(P)
        P.flush()
        sub.close()
    P.es = outer
    P.es.close()
    return nc


def launch(body, maps, shared, V, internal=()):
    nc = build_prog(body, V, internal)
    npc = NCORES // V
    pmaps = []
    for pc in range(npc):
        m = {}
        for k in maps[0]:
            if k in shared:
                m[k] = maps[pc * V][k]
            else:
                m[k] = np.ascontiguousarray(np.stack([maps[pc * V + v][k] for v in range(V)], 0))
        pmaps.append(m)
    res = run_bass_kernel_spmd(nc, pmaps, core_ids=list(range(npc)))
    out = []
    for pc in range(npc):
        for v in range(V):
            out.append({k: a[v] for k, a in res.results[pc].items()})
    return out


def col_blocks(cfg, maxw=512):
    blks = []
    s = 0
    while s < cfg.NC:
        w = min(maxw, cfg.NC - s)
        blks.append((s, w, 0))
        s += w
    while s < cfg.NT:
        w = min(maxw, cfg.NT - s)
        blks.append((s, w, 1))
        s += w
    return blks


def tile_w(w, kc):
    K, N = w.shape
    return np.ascontiguousarray(w.reshape(K // 128, 128, N // 128, 128).transpose(2, 1, 0, 3))


def fm(v):
    return np.ascontiguousarray(v.reshape(-1, 128).T)


class WStream:
    def __init__(self, P, name, KC, nbuf=2, cast_eng="pool"):
        self.P = P
        self.name = name
        self.KC = KC
        self.nbuf = nbuf
        self.st = [P.sbuf([128, KC, 128], F32, name + "_st%d" % i) for i in range(nbuf)]
        self.wb = [P.sbuf([128, KC, 128], BF16, name + "_wb%d" % i) for i in range(nbuf)]
        self.i = 0
        self.cast_eng = cast_eng

    def load(self, src):
        P = self.P
        b = self.i % self.nbuf
        self.i += 1
        ks, kb = (self.name, "st", b), (self.name, "wb", b)
        P.dma(self.st[b][:], src, reads=[], writes=[ks])
        P.copy(self.wb[b][:], self.st[b][:], reads=[ks], writes=[kb], eng=self.cast_eng)
        return self.wb[b], kb


def body_L0(P, cfg):
    FT = cfg.FT
    NM = 6 * FT // NCORES
    c5 = P.dram("c5", [128, FT, 5], F32, "ExternalInput")
    mw = P.dram("mw", [2, NM, 128, FT, 128], F32, "ExternalInput")
    mb = P.dram("mb", [128, 2, NM], F32, "ExternalInput")
    mo = P.dram("mo", [128, 2, NM, 5], F32, "ExternalOutput")
    c5s = P.sbuf([128, FT, 5], F32, "c5s")
    scs = P.sbuf([128, FT, 5], F32, "scs")
    mbs = P.sbuf([128, 2, NM], F32, "mbs")
    mos = P.sbuf([128, 2, NM, 5], F32, "mos")
    wt = [P.sbuf([128, FT, 128], F32, "wt%d" % i) for i in range(2)]
    P.dma(c5s[:], c5, [], ["c5s"])
    P.dma(mbs[:], mb, [], ["mbs"])
    P.act(scs[:], c5s[:], AF.Silu, ["c5s"], ["scs"])
    it = 0
    for l in range(2):
        for j in range(NM):
            b = it % 2
            it += 1
            P.dma(wt[b][:], mw[l, j], [], [("wt", b)])
            pb = P.bank()
            for kc in range(FT):
                P.mm(P.psb[pb][:, 0:5], wt[b][:, kc, :], scs[:, kc, :], kc == 0, kc == FT - 1,
                     [("wt", b), "scs"], [("ps", pb)])
            P.ts(mos[:, l, j, :], P.psb[pb][:, 0:5], mbs[:, l, j:j + 1], ALU.add, [("ps", pb), "mbs"], ["mos"])
    P.dma(mo, mos[:], ["mos"], ["mo"])


def run_L0(cfg, inp):
    FT, D = cfg.FT, cfg.D
    NM = 6 * FT // NCORES
    c5 = np.concatenate([inp["c_ctx"][None, :], inp["c"]], 0)
    c5T = np.ascontiguousarray(c5.T.reshape(FT, 128, 5).transpose(1, 0, 2))
    maps = []
    for core in range(NCORES):
        mwl, mbl = [], []
        for l in range(2):
            cols = slice(core * NM * 128, (core + 1) * NM * 128)
            mwl.append(tile_w(inp["mod_w"][l][:, cols], FT))
            mbl.append(fm(inp["mod_b"][l][cols]))
        maps.append({"c5": c5T, "mw": np.stack(mwl, 0), "mb": np.ascontiguousarray(np.stack(mbl, 1))})
    res = launch(lambda P: body_L0(P, cfg), maps, [], 1)
    mod = np.zeros((2, 6 * D, 5), np.float32)
    for core in range(NCORES):
        mo = res[core]["mo"]
        for l in range(2):
            mod[l, core * NM * 128:(core + 1) * NM * 128] = mo[:, l].transpose(1, 0, 2).reshape(NM * 128, 5)
    return mod


def mods_for(cfg, mod, l, b):
    m = mod[l][:, [0, 1 + b]]
    return np.ascontiguousarray(m.reshape(6, cfg.FT, 128, 2).transpose(2, 0, 1, 3))


class Rot:
    def __init__(self, P, name, shape, dtype, n=2):
        self.name = name
        self.t = [P.sbuf(shape, dtype, "%s%d" % (name, i)) for i in range(n)]
        self.i = 0

    def next(self):
        b = self.i % len(self.t)
        self.i += 1
        return self.t[b], (self.name, b)


def mod_scalars(P, mods, gS, cfg):
    FT = cfg.FT
    sc = P.sbuf([128, 4, FT, 2], F32, "modsc")
    for c in range(2):
        for j, (mi, gi, plus1) in enumerate(((1, 0, True), (2, 1, False), (4, 2, True), (5, 3, False))):
            if plus1:
                P.ts(sc[:, j, :, c], mods[:, mi, :, c], 1.0, ALU.add, ["mods"], [("modsc", j, c)])
                P.tt(sc[:, j, :, c], sc[:, j, :, c], gS[:, gi, :], ALU.mult, [("modsc", j, c), "gS"], [("modsc", j, c)])
            else:
                P.tt(sc[:, j, :, c], mods[:, mi, :, c], gS[:, gi, :], ALU.mult, ["mods", "gS"], [("modsc", j, c)])
    return sc


def norm_block(P, cfg, xs, kx, T, ones, rot_sq, rs_t, rstd_t, nfeat_tiles, nfeat):
    pb = P.bank()
    for ft in range(nfeat_tiles):
        sq, ksq = rot_sq.next()
        P.act(sq[:, :T], xs[:, ft, :T], AF.Square, [kx], [ksq])
        P.mm(P.psb[pb][:, :T], ones[:], sq[:, :T], ft == 0, ft == nfeat_tiles - 1, [ksq, "ones"], [("ps", pb)])
    P.act(rs_t[:, :T], P.psb[pb][:, :T], AF.Sqrt, [("ps", pb)], ["rs_t"], scale=1.0 / nfeat, bias=EPS)
    P.op("dve", lambda e: e.reciprocal(out=rstd_t[:, :T], in_=rs_t[:, :T]), reads=["rs_t"], writes=["rstd_t"])


def body_L1(P, cfg):
    FT, NT, NL, NC = cfg.FT, cfg.NT, cfg.NL, cfg.NC
    NTI = cfg.AB_IN // 128
    NU = cfg.S5W // 128
    xT = P.dram("xT", [128, FT, NT], F32, "ExternalInput")
    mods_d = P.dram("mods", [128, 6, FT, 2], F32, "ExternalInput")
    gS_d = P.dram("gS", [128, 4, FT], F32, "ExternalInput", shared=True)
    w_in = P.dram("w_in", [NTI, 128, FT, 128], F32, "ExternalInput", shared=True)
    qkg_d = P.dram("qkg", [128, 2], F32, "ExternalInput", shared=True)
    cos_d = P.dram("cosT", [128, NL], F32, "ExternalInput")
    sin_d = P.dram("sinT", [128, NL], F32, "ExternalInput")
    prot_d = P.dram("prot", [128, 128], F32, "ExternalInput", shared=True)
    qT = P.dram("qT", [cfg.AH, 128, NT], BF16, "ExternalOutput")
    kT = P.dram("kT", [cfg.KVH, 128, NT], BF16, "ExternalOutput")
    vT = P.dram("vT", [cfg.KVH, 128, NT], BF16, "ExternalOutput")
    uT = P.dram("uT", [NU, 128, NT], F32, "ExternalOutput")

    mods = P.sbuf([128, 6, FT, 2], F32, "mods")
    gS = P.sbuf([128, 4, FT], F32, "gS")
    qkg = P.sbuf([128, 2], F32, "qkg")
    cosS = P.sbuf([128, NL], F32, "cosS")
    sinS = P.sbuf([128, NL], F32, "sinS")
    prot = P.sbuf([128, 128], F32, "prot")
    ones = P.sbuf([128, 128], F32, "ones")
    hT = P.sbuf([128, FT, NT], BF16, "hT")
    for t, d, k in ((mods, mods_d, "mods"), (gS, gS_d, "gS"), (qkg, qkg_d, "qkg"), (cosS, cos_d, "cos"),
                    (sinS, sin_d, "sin"), (prot, prot_d, "prot")):
        P.dma(t[:], d, [], [k])
    P.memset(ones[:], 1.0, ["ones"])
    sc = mod_scalars(P, mods, gS, cfg)
    blks = col_blocks(cfg)
    TM = max(w for _, w, _ in blks)
    blksA = col_blocks(cfg, 256)
    xs_r = Rot(P, "xs", [128, FT, 256], F32, 2)
    sq_r = Rot(P, "sq", [128, TM], F32, 2)
    tmp_r = Rot(P, "tmp", [128, TM], F32, 2)
    rs_t = P.sbuf([128, TM], F32, "rs_t")
    rstd_t = P.sbuf([128, TM], F32, "rstd_t")
    for (s, T, lat) in blksA:
        xs, kx = xs_r.next()
        P.dma(xs[:, :, :T], xT[:, :, s:s + T], [], [kx])
        norm_block(P, cfg, xs, kx, T, ones, sq_r, rs_t, rstd_t, FT, cfg.D)
        for ft in range(FT):
            tmp, kt = tmp_r.next()
            P.tt(tmp[:, :T], xs[:, ft, :T], rstd_t[:, :T], ALU.mult, [kx, "rstd_t"], [kt])
            P.act(hT[:, ft, s:s + T], tmp[:, :T], AF.Identity, [kt, ("modsc", 0, lat), "mods"], colkeys("hT", s, T),
                  scale=sc[:, 0, ft, lat:lat + 1], bias=mods[:, 0, ft, lat:lat + 1])
    ws = WStream(P, "win", FT)
    qraw_r = Rot(P, "qraw", [128, TM], F32, 2)
    qn_r = Rot(P, "qn", [128, TM], F32, 2)
    t1_r = Rot(P, "t1", [128, TM], F32, 2)
    t2_r = Rot(P, "t2", [128, TM], F32, 2)
    qo_r = Rot(P, "qo", [128, TM], BF16, 3)
    uo_r = Rot(P, "uo", [128, TM], F32, 3)
    rs2 = P.sbuf([128, TM], F32, "rs2")
    rstd2 = P.sbuf([128, TM], F32, "rstd2")
    for nt in range(NTI):
        wb, kw = ws.load(w_in[nt])
        for (s, T, lat) in blks:
            pb = P.bank()
            ps = P.psb[pb]
            for kc in range(FT):
                P.mm(ps[:, :T], wb[:, kc, :], hT[:, kc, s:s + T], kc == 0, kc == FT - 1, [kw] + colkeys("hT", s, T), [("ps", pb)])
            if nt < cfg.AH + cfg.KVH:
                isq = nt < cfg.AH
                dst = qT[nt] if isq else kT[nt - cfg.AH]
                gcol = qkg[:, 0:1] if isq else qkg[:, 1:2]
                sq, ksq = sq_r.next()
                P.act(sq[:, :T], ps[:, :T], AF.Square, [("ps", pb)], [ksq])
                qraw, kq = qraw_r.next()
                P.copy(qraw[:, :T], ps[:, :T], [("ps", pb)], [kq])
                pb2 = P.bank()
                P.mm(P.psb[pb2][:, :T], ones[:], sq[:, :T], True, True, [ksq, "ones"], [("ps", pb2)])
                P.act(rs2[:, :T], P.psb[pb2][:, :T], AF.Sqrt, [("ps", pb2)], ["rs2"], scale=1.0 / 128, bias=EPS)
                P.op("dve", lambda e, T=T: e.reciprocal(out=rstd2[:, :T], in_=rs2[:, :T]), reads=["rs2"], writes=["rstd2"])
                qn, kn = qn_r.next()
                P.stt(qn[:, :T], qraw[:, :T], gcol, rstd2[:, :T], ALU.mult, ALU.mult, [kq, "rstd2", "qkg"], [kn])
                qo, ko = qo_r.next()
                if lat:
                    lc = s - NC
                    pb3 = P.bank()
                    P.mm(P.psb[pb3][:, :T], prot[:], qn[:, :T], True, True, [kn, "prot"], [("ps", pb3)])
                    t1, k1 = t1_r.next()
                    t2, k2 = t2_r.next()
                    P.tt(t1[:, :T], qn[:, :T], cosS[:, lc:lc + T], ALU.mult, [kn, "cos"], [k1])
                    P.tt(t2[:, :T], P.psb[pb3][:, :T], sinS[:, lc:lc + T], ALU.mult, [("ps", pb3), "sin"], [k2])
                    P.tt(qo[:, :T], t1[:, :T], t2[:, :T], ALU.add, [k1, k2], [ko], eng="pool")
                else:
                    P.copy(qo[:, :T], qn[:, :T], [kn], [ko], eng="act")
                P.dma(dst[:, s:s + T], qo[:, :T], [ko], [("out", nt, s)])
            elif nt < cfg.AH + 2 * cfg.KVH:
                qo, ko = qo_r.next()
                P.copy(qo[:, :T], ps[:, :T], [("ps", pb)], [ko], eng="act")
                P.dma(vT[nt - cfg.AH - cfg.KVH][:, s:s + T], qo[:, :T], [ko], [("out", nt, s)])
            else:
                uo, ko = uo_r.next()
                P.copy(uo[:, :T], ps[:, :T], [("ps", pb)], [ko])
                P.dma(uT[nt - cfg.AH - 2 * cfg.KVH][:, s:s + T], uo[:, :T], [ko], [("out", nt, s)])


def rope_tables(cfg, pos):
    pos = np.asarray(pos)
    inv = (10000.0 ** (-np.arange(32, dtype=np.float32) / 32)).astype(np.float32)
    row = (pos // 64).astype(np.float32)
    col = (pos % 64).astype(np.float32)
    cosT = np.zeros((128, len(pos)), np.float32)
    sinT = np.zeros((128, len(pos)), np.float32)
    for d in range(128):
        axis, j = d // 64, d % 64
        f = j % 32
        ang = (row if axis == 0 else col) * inv[f]
        cosT[d] = np.cos(ang)
        sinT[d] = np.sin(ang)
    prot = np.zeros((128, 128), np.float32)
    for d in range(128):
        j = d % 64
        if j < 32:
            prot[d + 32, d] = -1.0
        else:
            prot[d - 32, d] = 1.0
    return cosT, sinT, prot


def tok_fm(X, FT):
    T = X.shape[0]
    return np.ascontiguousarray(X.T.reshape(-1, 128, T).transpose(1, 0, 2))


def run_L1(cfg, inp, mod):
    FT = cfg.FT
    w_in_t = tile_w(inp["ab_w_in"][0], FT)
    gS = np.ascontiguousarray(inp["norm_g"][0].reshape(4, FT, 128).transpose(2, 0, 1))
    qkg = np.ascontiguousarray(np.stack([inp["attn_q_norm"][0], inp["attn_k_norm"][0]], 1))
    maps = []
    for core in range(NCORES):
        b, half = core // 2, core % 2
        lat = inp["x"][b, half * cfg.NL:(half + 1) * cfg.NL]
        X = np.concatenate([inp["ctx"][b], lat], 0)
        cosT, sinT, prot = rope_tables(cfg, np.arange(half * cfg.NL, (half + 1) * cfg.NL))
        maps.append({"xT": tok_fm(X, FT), "mods": mods_for(cfg, mod, 0, b), "gS": gS, "w_in": w_in_t,
                     "qkg": qkg, "cosT": cosT, "sinT": sinT, "prot": prot})
    res = launch(lambda P: body_L1(P, cfg), maps, ['w_in', 'qkg', 'prot', 'gS'], VW)
    return [res[c] for c in range(NCORES)]


def attention_phase(P, cfg, qS, KTs, Vs, onesb, aT):
    NC, NT = cfg.NC, cfg.NT
    NK = cfg.LS
    blks = col_blocks(cfg)
    TM = max(w for _, w, _ in blks)
    pT_r = Rot(P, "pT", [128, TM], BF16, 3)
    rl = P.sbuf([128, TM], F32, "rl")
    ao_r = Rot(P, "ao", [128, TM], BF16, 2)
    scale = 1.0 / math.sqrt(128.0)
    it = 0
    for h in range(cfg.AH):
        kvh = h // cfg.AG
        for (s, T, lat) in blks:
            nkt = (NK if lat else NC) // 128
            bo, bl = 3 + it % 2, 5 + it % 2
            it += 1
            pend = None
            for kt in range(nkt + 1):
                cur = None
                if kt < nkt:
                    bs = kt % 3
                    P.mm(P.psb[bs][:, :T], KTs[:, kvh, kt * 128:(kt + 1) * 128], qS[:, h, s:s + T], True, True,
                         ["KT", "qS"], [("ps", bs)])
                    pT, kp = pT_r.next()
                    P.act(pT[:, :T], P.psb[bs][:, :T], AF.Exp, [("ps", bs)], [kp], scale=scale)
                    cur = (pT, kp, kt)
                if pend is not None:
                    pT0, kp0, k0 = pend
                    P.mm(P.psb[bo][:, :T], Vs[:, kvh, k0, :], pT0[:, :T], k0 == 0, k0 == nkt - 1, ["V", kp0], [("ps", bo)])
                    P.mm(P.psb[bl][:, :T], onesb[:], pT0[:, :T], k0 == 0, k0 == nkt - 1, ["onesb", kp0], [("ps", bl)])
                pend = cur
            P.op("dve", lambda e, T=T, bl=bl: e.reciprocal(out=rl[:, :T], in_=P.psb[bl][:, :T]), reads=[("ps", bl)], writes=["rl"])
            ao, ka = ao_r.next()
            P.tt(ao[:, :T], P.psb[bo][:, :T], rl[:, :T], ALU.mult, [("ps", bo), "rl"], [ka])
            P.dma(aT[h][:, s:s + T], ao[:, :T], [ka], [("aT", h, s)])


def s5_phase(P, cfg, uF, uB, prm, yF, yB, identb):
    LS = cfg.LS
    NJ = LS // 8
    G2 = cfg.S5G // 2
    NP = G2 // 2
    NQT = max(1, NP // 4)
    RPT = min(4, NP)
    M2 = 2 * NP
    TWO_PI = 2.0 * math.pi
    lam_re, lam_im, logdt, bre, bim, cre, cim, dsk, K9, JJ = prm
    sb = lambda shape, name, dt=F32: P.sbuf(shape, dt, name)
    lr = sb([128, M2], "lr"); dt_ = sb([128, M2], "dt"); a_ = sb([128, M2], "a_"); thn = sb([128, M2], "thn")
    P.ts(lr[:], lam_re[:], -1e-4, ALU.min, ["s5prm"], ["lr"])
    P.act(dt_[:], logdt[:], AF.Exp, ["s5prm"], ["dt"])
    P.tt(a_[:], lr[:], dt_[:], ALU.mult, ["lr", "dt"], ["a_"])
    P.tt(thn[:], lam_im[:], dt_[:], ALU.mult, ["s5prm", "dt"], ["thn"])
    P.ts(thn[:], thn[:], 1.0 / TWO_PI, ALU.mult, ["thn"], ["thn"])
    X = sb([128, M2, 9], "X"); Xi = sb([128, M2, 9], "Xi", I32); Xf = sb([128, M2, 9], "Xf")
    mag = sb([128, M2, 9], "mag"); sn = sb([128, M2, 9], "sn"); s2 = sb([128, M2, 9], "s2")
    LRE = sb([128, M2, 9], "LRE"); LIM = sb([128, M2, 9], "LIM"); NLIM = sb([128, M2, 9], "NLIM")
    for k in range(9):
        P.ts(X[:, :, k], thn[:], float(k), ALU.mult, ["thn"], ["X"])
        P.act(mag[:, :, k], a_[:], AF.Exp, ["a_"], ["mag"], scale=float(k))
    P.copy(Xi[:], X[:], ["X"], ["Xi"])
    P.copy(Xf[:], Xi[:], ["Xi"], ["Xf"])
    P.tt(X[:], X[:], Xf[:], ALU.subtract, ["X", "Xf"], ["X"])
    P.act(sn[:], X[:], AF.Sin, ["X"], ["sn"], scale=TWO_PI)
    P.act(s2[:], X[:], AF.Sin, ["X"], ["s2"], scale=math.pi)
    P.tt(s2[:], s2[:], s2[:], ALU.mult, ["s2"], ["s2"])
    P.ts(s2[:], s2[:], -2.0, ALU.mult, ["s2"], ["s2"], s2=1.0, op1=ALU.add)
    P.tt(LRE[:], mag[:], s2[:], ALU.mult, ["mag", "s2"], ["LRE"])
    P.tt(LIM[:], mag[:], sn[:], ALU.mult, ["mag", "sn"], ["LIM"])
    P.ts(NLIM[:], LIM[:], -1.0, ALU.mult, ["LIM"], ["NLIM"])
    den = sb([128, M2], "den"); t0 = sb([128, M2], "t0"); t1 = sb([128, M2], "t1")
    kr = sb([128, M2], "kr"); ki = sb([128, M2], "ki"); lm1 = sb([128, M2], "lm1")
    P.tt(den[:], lr[:], lr[:], ALU.mult, ["lr"], ["den"])
    P.tt(t0[:], lam_im[:], lam_im[:], ALU.mult, ["s5prm"], ["t0"])
    P.tt(den[:], den[:], t0[:], ALU.add, ["den", "t0"], ["den"])
    P.op("dve", lambda e: e.reciprocal(out=den[:], in_=den[:]), reads=["den"], writes=["den"])
    P.ts(lm1[:], LRE[:, :, 1], -1.0, ALU.add, ["LRE"], ["lm1"])
    P.tt(t0[:], lm1[:], lr[:], ALU.mult, ["lm1", "lr"], ["t0"])
    P.tt(t1[:], LIM[:, :, 1], lam_im[:], ALU.mult, ["LIM", "s5prm"], ["t1"])
    P.tt(t0[:], t0[:], t1[:], ALU.add, ["t0", "t1"], ["t0"])
    P.tt(kr[:], t0[:], den[:], ALU.mult, ["t0", "den"], ["kr"])
    P.tt(t0[:], LIM[:, :, 1], lr[:], ALU.mult, ["LIM", "lr"], ["t0"])
    P.tt(t1[:], lm1[:], lam_im[:], ALU.mult, ["lm1", "s5prm"], ["t1"])
    P.tt(t0[:], t0[:], t1[:], ALU.subtract, ["t0", "t1"], ["t0"])
    P.tt(ki[:], t0[:], den[:], ALU.mult, ["t0", "den"], ["ki"])
    BBre = sb([128, M2, 16], "BBre"); BBim = sb([128, M2, 16], "BBim"); tb = sb([128, M2, 16], "tb")
    NCim = sb([128, M2, 16], "NCim")
    krb = kr[:].unsqueeze(2).broadcast_to([128, M2, 16])
    kib = ki[:].unsqueeze(2).broadcast_to([128, M2, 16])
    P.tt(BBre[:], bre[:], krb, ALU.mult, ["s5prm", "kr"], ["BBre"])
    P.tt(tb[:], bim[:], kib, ALU.mult, ["s5prm", "ki"], ["tb"])
    P.tt(BBre[:], BBre[:], tb[:], ALU.subtract, ["BBre", "tb"], ["BBre"])
    P.tt(BBim[:], bim[:], krb, ALU.mult, ["s5prm", "kr"], ["BBim"])
    P.tt(tb[:], bre[:], kib, ALU.mult, ["s5prm", "ki"], ["tb"])
    P.tt(BBim[:], BBim[:], tb[:], ALU.add, ["BBim", "tb"], ["BBim"])
    P.ts(NCim[:], cim[:], -1.0, ALU.mult, ["s5prm"], ["NCim"])
    E4 = [sb([128, 8, 2, 128], "E4_%d" % r, BF16) for r in range(RPT)]
    R4E = [sb([128, 8, 2, 128], "R4E_%d" % r, BF16) for r in range(RPT)]
    CE = [sb([128, 2, 32], "CE_%d" % r, BF16) for r in range(2)]
    KernE = sb([128, 8, 128], "KernE", BF16)
    WS = sb([128, 8, 2, 128], "WS", BF16)
    for r in range(RPT):
        P.memset(E4[r][:], 0.0, [("E4", r)])
        P.memset(R4E[r][:], 0.0, [("R4E", r)])
    for r in range(2):
        P.memset(CE[r][:], 0.0, [("CE", r)])
    P.memset(KernE[:], 0.0, [("KernE", r) for r in range(RPT)])
    W8 = sb([128, 8, 2, 16], "W8"); W8b = sb([128, 8, 2, 16], "W8b")
    R8 = sb([128, 8, 2, 16], "R8"); R8b = sb([128, 8, 2, 16], "R8b")
    Hfull = [sb([128, 2, 257], "Hfull%d" % r, BF16) for r in range(RPT)]
    hcar = sb([128, RPT, 2], "hcar")
    uf = sb([128, LS], "uf"); ub = sb([128, LS], "ub", BF16)
    NB = 256
    xj = sb([128, NB], "xj"); xji = sb([128, NB], "xji", I32); xjf = sb([128, NB], "xjf")
    snj = sb([128, NB], "snj"); csj = sb([128, NB], "csj")
    va = sb([128, 2, NB], "va"); vb = sb([128, 2, NB], "vb"); vv = sb([128, 2, NB], "vv")
    hh = sb([128, 2, NB], "hh"); r8t = sb([128, NB], "r8t"); onesf = sb([128, NB], "onesf")
    ha = sb([128, 2, NB], "ha"); hb = sb([128, 2, NB], "hb")
    ysb_r = Rot(P, "ysb", [128, 8 * NB], F32, 2)
    P.memset(onesf[:], 1.0, ["onesf"])
    jblocks = []
    j = 0
    while j < NJ:
        n = min(NB, NJ - j)
        jblocks.append((j, n))
        j += n
    b16 = lambda ap: ap.unsqueeze(1).broadcast_to([128, 8, 16])
    for d in range(2):
        usrc, ydst = (uF, yF) if d == 0 else (uB, yB)
        for qt in range(NQT):
            P.dma(uf[:], usrc[qt], [], ["uf"])
            P.copy(ub[:], uf[:], ["uf"], ["ub"], eng="pool")
            for r in range(RPT):
                q = qt * RPT + r
                m = d * NP + q
                R0 = 32 * r
                lre8 = LRE[:, m, 0:8].unsqueeze(2).broadcast_to([128, 8, 16])
                lim8 = LIM[:, m, 0:8].unsqueeze(2).broadcast_to([128, 8, 16])
                nlim8 = NLIM[:, m, 0:8].unsqueeze(2).broadcast_to([128, 8, 16])
                P.tt(W8[:, :, 0, :], b16(BBre[:, m, :]), lre8, ALU.mult, ["BBre", "LRE"], ["W8"])
                P.tt(W8b[:, :, 0, :], b16(BBim[:, m, :]), nlim8, ALU.mult, ["BBim", "NLIM"], ["W8b"])
                P.tt(W8[:, :, 1, :], b16(BBim[:, m, :]), lre8, ALU.mult, ["BBim", "LRE"], ["W8"])
                P.tt(W8b[:, :, 1, :], b16(BBre[:, m, :]), lim8, ALU.mult, ["BBre", "LIM"], ["W8b"])
                P.tt(W8[:], W8[:], W8b[:], ALU.add, ["W8", "W8b"], ["W8"])
                for gi in range(2):
                    P.copy(E4[r][64 * gi:64 * gi + 64, :, :, R0 + 16 * gi:R0 + 16 * gi + 16], W8[64 * gi:64 * gi + 64],
                           ["W8"], [("E4", r)], eng="act" if gi else "dve")
                for half in range(2):
                    pbk = 5 + half
                    pv = P.psb[pbk][:].bitcast(BF16)
                    for kk in range(4):
                        for ri in range(2):
                            k = half * 4 + kk
                            idx = kk * 2 + ri
                            P.op("pe", lambda e, pv=pv, idx=idx, k=k, ri=ri, r=r: e.transpose(
                                out=pv[:, idx * 128:(idx + 1) * 128], in_=E4[r][:, k, ri, :], identity=identb[:]),
                                reads=[("E4", r), "identb"], writes=[("ps", pbk)])
                    P.copy(WS[R0:R0 + 32, half * 4:half * 4 + 4, :, :],
                           pv[R0:R0 + 32, :].rearrange("p (k r c) -> p k r c", k=4, r=2), [("ps", pbk)], [("WS", r)],
                           eng="act" if half else "dve")
                ce = CE[q % 2]
                for gi in range(2):
                    P.copy(ce[64 * gi:64 * gi + 64, 0, 16 * gi:16 * gi + 16], cre[64 * gi:64 * gi + 64, m, :],
                           ["s5prm"], [("CE", q % 2)])
                    P.copy(ce[64 * gi:64 * gi + 64, 1, 16 * gi:16 * gi + 16], NCim[64 * gi:64 * gi + 64, m, :],
                           ["NCim"], [("CE", q % 2)])
                for tau in range(8):
                    P.mm(P.psb[7][:, tau * 32:(tau + 1) * 32], E4[r][:, tau, 0, :], ce[:, 0, :], True, False,
                         [("E4", r), ("CE", q % 2)], [("ps", 7)])
                    P.mm(P.psb[7][:, tau * 32:(tau + 1) * 32], E4[r][:, tau, 1, :], ce[:, 1, :], False, True,
                         [("E4", r), ("CE", q % 2)], [("ps", 7)])
                P.copy(KernE[R0:R0 + 32, :, R0:R0 + 32], P.psb[7][R0:R0 + 32, 0:256].rearrange("p (t c) -> p t c", t=8),
                       [("ps", 7)], [("KernE", r)])
                lre1 = LRE[:, m, 1:9].unsqueeze(2).broadcast_to([128, 8, 16])
                lim1 = LIM[:, m, 1:9].unsqueeze(2).broadcast_to([128, 8, 16])
                nlim1 = NLIM[:, m, 1:9].unsqueeze(2).broadcast_to([128, 8, 16])
                P.tt(R8[:, :, 0, :], b16(cre[:, m, :]), lre1, ALU.mult, ["s5prm", "LRE"], ["R8"])
                P.tt(R8b[:, :, 0, :], b16(NCim[:, m, :]), lim1, ALU.mult, ["NCim", "LIM"], ["R8b"])
                P.tt(R8[:, :, 1, :], b16(cre[:, m, :]), nlim1, ALU.mult, ["s5prm", "NLIM"], ["R8"])
                P.tt(R8b[:, :, 1, :], b16(NCim[:, m, :]), lre1, ALU.mult, ["NCim", "LRE"], ["R8b"])
                P.tt(R8[:], R8[:], R8b[:], ALU.add, ["R8", "R8b"], ["R8"])
                for gi in range(2):
                    P.copy(R4E[r][64 * gi:64 * gi + 64, :, :, R0 + 16 * gi:R0 + 16 * gi + 16], R8[64 * gi:64 * gi + 64],
                           ["R8"], [("R4E", r)], eng="act" if gi else "dve")
                P.memset(Hfull[r][:, :, 0:1], 0.0, [("Hfull", r)])
                P.memset(hcar[:, r, :], 0.0, [("hcar", r)])
            for (j0, n) in jblocks:
                first_in_bank = [True] * 4
                for r in range(RPT):
                    q = qt * RPT + r
                    m = d * NP + q
                    R0 = 32 * r
                    tp = (R0, 0)
                    for ri in range(2):
                        for s_ in range(8):
                            c0 = 8 * j0 + s_
                            P.mm(P.psb[4][:, ri * NB:ri * NB + n], WS[R0:R0 + 32, 7 - s_, ri, :],
                                 ub[R0:R0 + 32, c0:c0 + 8 * (n - 1) + 1:8], s_ == 0, s_ == 7,
                                 [("WS", r), "ub"], [("ps", 4)], tile_position=tp)
                    P.ts(xj[:, :n], JJ[:, j0:j0 + n], X[:, m, 8:9], ALU.mult, ["s5prm", "X"], ["xj"])
                    P.copy(xji[:, :n], xj[:, :n], ["xj"], ["xji"])
                    P.copy(xjf[:, :n], xji[:, :n], ["xji"], ["xjf"])
                    P.tt(xj[:, :n], xj[:, :n], xjf[:, :n], ALU.subtract, ["xj", "xjf"], ["xj"])
                    P.act(snj[:, :n], xj[:, :n], AF.Sin, ["xj"], ["snj"], scale=TWO_PI)
                    P.act(csj[:, :n], xj[:, :n], AF.Sin, ["xj"], ["csj"], scale=math.pi)
                    P.act(csj[:, :n], csj[:, :n], AF.Square, ["csj"], ["csj"])
                    P.ts(csj[:, :n], csj[:, :n], -2.0, ALU.mult, ["csj"], ["csj"], s2=1.0, op1=ALU.add)
                    S2 = P.psb[4][:, 0:2 * NB].rearrange("p (r n) -> p r n", r=2)
                    csb = csj[:, :n].unsqueeze(1).broadcast_to([128, 2, n])
                    snb = snj[:, :n].unsqueeze(1).broadcast_to([128, 2, n])
                    P.tt(va[:, :, :n], S2[:, :, :n], csb, ALU.mult, [("ps", 4), "csj"], ["va"])
                    P.tt(vb[:, :, :n], S2[:, :, :n], snb, ALU.mult, [("ps", 4), "snj"], ["vb"])
                    P.tt(vv[:, 0, :n], va[:, 0, :n], vb[:, 1, :n], ALU.add, ["va", "vb"], ["vv"])
                    P.tt(vv[:, 1, :n], va[:, 1, :n], vb[:, 0, :n], ALU.subtract, ["va", "vb"], ["vv"])
                    P.ts(r8t[:, :n], onesf[:, :n], mag[:, m, 8:9], ALU.mult, ["onesf", "mag"], ["r8t"])
                    for ri in range(2):
                        P.op("dve", lambda e, ri=ri, r=r, n=n: e.tensor_tensor_scan(
                            out=hh[:, ri, :n], data0=r8t[:, :n], data1=vv[:, ri, :n], initial=hcar[:, r, ri:ri + 1],
                            op0=ALU.mult, op1=ALU.add), reads=["r8t", "vv", ("hcar", r)], writes=["hh"])
                    P.copy(hcar[:, r, :], hh[:, :, n - 1], ["hh"], [("hcar", r)])
                    P.tt(ha[:, :, :n], hh[:, :, :n], csb, ALU.mult, ["hh", "csj"], ["ha"])
                    P.tt(hb[:, :, :n], hh[:, :, :n], snb, ALU.mult, ["hh", "snj"], ["hb"])
                    P.tt(Hfull[r][:, 0, 1:n + 1], ha[:, 0, :n], hb[:, 1, :n], ALU.subtract, ["ha", "hb"], [("Hfull", r)])
                    P.tt(Hfull[r][:, 1, 1:n + 1], ha[:, 1, :n], hb[:, 0, :n], ALU.add, ["ha", "hb"], [("Hfull", r)])
                    for t in range(8):
                        bk = t // 2
                        yreg = P.psb[bk][:, (t % 2) * NB:(t % 2) * NB + n]
                        for ri in range(2):
                            P.mm(yreg, R4E[r][:, t, ri, :], Hfull[r][:, ri, 0:n], first_in_bank[bk], False,
                                 [("R4E", r), ("Hfull", r)], [("ps", bk)], skip_group_check=True)
                            first_in_bank[bk] = False
                        for s_ in range(t + 1):
                            c0 = 8 * j0 + s_
                            P.mm(yreg, KernE[R0:R0 + 32, t - s_, :], ub[R0:R0 + 32, c0:c0 + 8 * (n - 1) + 1:8], False,
                                 (r == RPT - 1 and s_ == t), [("KernE", r), "ub"], [("ps", bk)], tile_position=tp,
                                 skip_group_check=True)
                    P.copy(Hfull[r][:, :, 0:1], Hfull[r][:, :, n:n + 1], [("Hfull", r)], [("Hfull", r)], eng="pool")
                ysb, ky = ysb_r.next()
                for t in range(8):
                    bk = t // 2
                    yreg = P.psb[bk][:, (t % 2) * NB:(t % 2) * NB + n]
                    c0 = 8 * j0 + t
                    if d == 0:
                        P.stt(ysb[:, t:t + 8 * (n - 1) + 1:8], uf[:, c0:c0 + 8 * (n - 1) + 1:8], dsk[:, qt:qt + 1], yreg,
                              ALU.mult, ALU.add, ["uf", "s5prm", ("ps", bk)], [ky])
                    else:
                        P.copy(ysb[:, t:t + 8 * (n - 1) + 1:8], yreg, [("ps", bk)], [ky], eng="act")
                P.dma(ydst[qt][:, 8 * j0:8 * (j0 + n)], ysb[:, :8 * n], [ky], [("y", d, qt, j0)])


def body_L2(P, cfg):
    NT, LS = cfg.NT, cfg.LS
    NK = LS
    KT = NK // 128
    G2 = cfg.S5G // 2
    NP = G2 // 2
    NQT = max(1, NP // 4)
    M2 = 2 * NP
    NJ = LS // 8

    def part_attn(P):
        qT_d = P.dram("qT", [cfg.AH, 128, NT], BF16, "ExternalInput")
        KT_d = P.dram("KT", [cfg.KVH, 128, NK], BF16, "ExternalInput")
        V_d = P.dram("V", [128, cfg.KVH, KT, 128], BF16, "ExternalInput")
        aT = P.dram("aT", [cfg.AH, 128, NT], BF16, "ExternalOutput")
        onesb = P.sbuf([128, 128], BF16, "onesb")
        P.memset(onesb[:], 1.0, ["onesb"])
        qS = P.sbuf([128, cfg.AH, NT], BF16, "qS")
        KTs = P.sbuf([128, cfg.KVH, NK], BF16, "KTs")
        Vs = P.sbuf([128, cfg.KVH, KT, 128], BF16, "Vs")
        P.dma(qS[:], qT_d.rearrange("h p n -> p h n"), [], ["qS"])
        P.dma(KTs[:], KT_d.rearrange("h p n -> p h n"), [], ["KT"])
        P.dma(Vs[:], V_d, [], ["V"])
        attention_phase(P, cfg, qS, KTs, Vs, onesb, aT)

    def part_s5(P):
        uF = P.dram("uF", [NQT, 128, LS], F32, "ExternalInput")
        uB = P.dram("uB", [NQT, 128, LS], F32, "ExternalInput")
        yF = P.dram("yF", [NQT, 128, LS], F32, "ExternalOutput")
        yB = P.dram("yB", [NQT, 128, LS], F32, "ExternalOutput")
        names = [("lam_re", [128, M2]), ("lam_im", [128, M2]), ("logdt", [128, M2]), ("bre", [128, M2, 16]),
                 ("bim", [128, M2, 16]), ("cre", [128, M2, 16]), ("cim", [128, M2, 16]), ("dsk", [128, NQT]),
                 ("K9", [128, 9]), ("JJ", [128, NJ])]
        prm = []
        for nm, shp in names:
            dd = P.dram(nm, shp, F32, "ExternalInput", shared=nm in ("K9", "JJ"))
            t = P.sbuf(shp, F32, "p_" + nm)
            P.dma(t[:], dd, [], ["s5prm"])
            prm.append(t)
        ident_d = P.dram("identb", [128, 128], BF16, "ExternalInput", shared=True)
        identb = P.sbuf([128, 128], BF16, "identb")
        P.dma(identb[:], ident_d, [], ["identb"])
        s5_phase(P, cfg, uF, uB, prm, yF, yB, identb)
    fused_body(P, [part_attn, part_s5])


def seg_rev(a, NC):
    return np.concatenate([a[..., :NC][..., ::-1], a[..., NC:][..., ::-1]], -1)


def run_L2(cfg, inp, r1):
    NC, NL, LS = cfg.NC, cfg.NL, cfg.LS
    G2 = cfg.S5G // 2
    NP = G2 // 2
    NQT = max(1, NP // 4)
    CH = cfg.S5W // 2
    NJ = LS // 8
    maps = []
    for core in range(NCORES):
        b, half = core // 2, core % 2
        ra, rb = r1[2 * b], r1[2 * b + 1]
        KTf = np.concatenate([ra["kT"][:, :, :NC], ra["kT"][:, :, NC:], rb["kT"][:, :, NC:]], 2)
        vTf = np.concatenate([ra["vT"][:, :, :NC], ra["vT"][:, :, NC:], rb["vT"][:, :, NC:]], 2)
        Vtok = vTf.transpose(0, 2, 1).reshape(cfg.KVH, LS // 128, 128, 128).transpose(2, 0, 1, 3)
        uTf = np.concatenate([ra["uT"][:, :, :NC], ra["uT"][:, :, NC:], rb["uT"][:, :, NC:]], 2)
        uT2 = uTf.reshape(cfg.S5W, LS)[half * CH:(half + 1) * CH].reshape(NQT, -1, LS)
        if uT2.shape[1] != 128:
            raise ValueError("S5 channel tile must be 128")
        gs = slice(half * G2, (half + 1) * G2)

        def pp(a):
            return np.ascontiguousarray(a.reshape(2, NP, 2, 64).transpose(2, 3, 0, 1).reshape(128, 2 * NP))

        def pp16(a):
            return np.ascontiguousarray(a.reshape(2, NP, 2, 64, 16).transpose(2, 3, 0, 1, 4).reshape(128, 2 * NP, 16))
        ldt = np.broadcast_to(inp["s5_log_dt"][0][:, gs, None], (2, G2, 64))
        m = {"qT": r1[core]["qT"], "KT": np.ascontiguousarray(KTf), "V": np.ascontiguousarray(Vtok),
             "uF": np.ascontiguousarray(uT2), "uB": np.ascontiguousarray(seg_rev(uT2, NC)),
             "lam_re": pp(inp["s5_lam_re"][0][:, gs]), "lam_im": pp(inp["s5_lam_im"][0][:, gs]), "logdt": pp(ldt),
             "bre": pp16(inp["s5_b_re"][0][:, gs]), "bim": pp16(inp["s5_b_im"][0][:, gs]),
             "cre": pp16(inp["s5_c_re"][0][:, gs].transpose(0, 1, 3, 2)),
             "cim": pp16(inp["s5_c_im"][0][:, gs].transpose(0, 1, 3, 2)),
             "dsk": fm(inp["s5_d"][0][half * CH:(half + 1) * CH]),
             "K9": np.broadcast_to(np.arange(9, dtype=np.float32), (128, 9)).copy(),
             "JJ": np.broadcast_to(np.arange(1, NJ + 1, dtype=np.float32), (128, NJ)).copy(),
             "identb": np.eye(128, dtype=np.float32).astype(NPBF)}
        maps.append(m)
    res = launch(lambda P: body_L2(P, cfg), maps, ['K9', 'JJ', 'identb'], 1)
    return [res[c] for c in range(NCORES)]


def gemm_ws(P, ws, w_dram, nts, KC, inT, in_key, blks, evac):
    nts = list(nts)
    nxt = ws.load(w_dram[nts[0]])
    for i, nt in enumerate(nts):
        wb, kw = nxt
        if i + 1 < len(nts):
            nxt = ws.load(w_dram[nts[i + 1]])
        for blk in blks:
            s, T, lat = blk
            pb = P.bank()
            for kc in range(KC):
                P.mm(P.psb[pb][:, :T], wb[:, kc, :], inT[:, kc, s:s + T], kc == 0, kc == KC - 1,
                     [kw] + colkeys(in_key, s, T), [("ps", pb)])
            evac(nt, blk, pb)


def body_L3a(P, cfg):
    FT, NT, AH = cfg.FT, cfg.NT, cfg.AH
    NU = cfg.S5W // 128
    aT = P.dram("aT", [AH, 128, NT], BF16, "ExternalInput")
    yFd = P.dram("yF", [NU, 128, NT], F32, "ExternalInput")
    yBd = P.dram("yB", [NU, 128, NT], F32, "ExternalInput")
    gw = P.dram("glu_w", [2 * NU, 128, NU, 128], F32, "ExternalInput", shared=True)
    gb_d = P.dram("glu_b", [128, 2 * NU], F32, "ExternalInput", shared=True)
    wo = P.dram("w_out", [FT, 128, FT, 128], F32, "ExternalInput", shared=True)
    yT = P.dram("yT", [FT, 128, NT], F32, "ExternalOutput")
    gb = P.sbuf([128, 2 * NU], F32, "gb")
    P.dma(gb[:], gb_d, [], ["gb"])
    mixT = P.sbuf([128, FT, NT], BF16, "mixT")
    gyT = P.sbuf([128, NU, NT], BF16, "gyT")
    blks = col_blocks(cfg)
    TM = max(w for _, w, _ in blks)
    for h in range(AH):
        P.dma(mixT[:, h, :], aT[h], [], colkeys("mixT", 0, NT))
    ya_r = Rot(P, "ya", [128, TM], F32, 2)
    yb_r = Rot(P, "yb", [128, TM], F32, 2)
    t_r = Rot(P, "tg", [128, TM], F32, 2)
    for (s, T, lat) in blks:
        for j in range(NU):
            ya, ka = ya_r.next()
            yb, kb = yb_r.next()
            tg, kt = t_r.next()
            P.dma(ya[:, :T], yFd[j][:, s:s + T], [], [ka])
            P.dma(yb[:, :T], yBd[j][:, s:s + T], [], [kb])
            P.tt(ya[:, :T], ya[:, :T], yb[:, :T], ALU.add, [ka, kb], [ka], eng="pool")
            P.act(tg[:, :T], ya[:, :T], AF.Square, [ka], [kt])
            P.ts(tg[:, :T], tg[:, :T], 0.044715, ALU.mult, [kt], [kt], s2=1.0, op1=ALU.add)
            P.tt(tg[:, :T], tg[:, :T], ya[:, :T], ALU.mult, [kt, ka], [kt])
            P.act(tg[:, :T], tg[:, :T], AF.Sigmoid, [kt], [kt], scale=2.0 * math.sqrt(2.0 / math.pi))
            P.tt(gyT[:, j, s:s + T], tg[:, :T], ya[:, :T], ALU.mult, [kt, ka], colkeys("gyT", s, T))
    ws = WStream(P, "w3", FT)
    sg_r = Rot(P, "sg", [128, TM], F32, 2)
    for j in range(NU):
        wa, kwa = ws.load_kc(gw[j], NU)
        wg, kwg = ws.load_kc(gw[NU + j], NU)
        for (s, T, lat) in blks:
            pa, pg = P.bank(), P.bank()
            for kc in range(NU):
                P.mm(P.psb[pa][:, :T], wa[:, kc, :], gyT[:, kc, s:s + T], kc == 0, kc == NU - 1, [kwa] + colkeys("gyT", s, T), [("ps", pa)])
            for kc in range(NU):
                P.mm(P.psb[pg][:, :T], wg[:, kc, :], gyT[:, kc, s:s + T], kc == 0, kc == NU - 1, [kwg] + colkeys("gyT", s, T), [("ps", pg)])
            sg, ks = sg_r.next()
            P.act(sg[:, :T], P.psb[pg][:, :T], AF.Sigmoid, [("ps", pg), "gb"], [ks], bias=gb[:, NU + j:NU + j + 1])
            P.stt(mixT[:, AH + j, s:s + T], P.psb[pa][:, :T], gb[:, j:j + 1], sg[:, :T], ALU.add, ALU.mult,
                  [("ps", pa), "gb", ks], colkeys("mixT", s, T))
    yo_r = Rot(P, "yo", [128, TM], F32, 3)

    def evac(nt, blk, pb):
        s, T, lat = blk
        yo, ko = yo_r.next()
        P.copy(yo[:, :T], P.psb[pb][:, :T], [("ps", pb)], [ko], eng="act" if (nt + s // 128) % 2 else "dve")
        P.dma(yT[nt][:, s:s + T], yo[:, :T], [ko], [("yT", nt, s)])
    gemm_ws(P, ws, wo, range(FT), FT, mixT, "mixT", blks, evac)


def _ws_load_kc(self, src, kc):
    P = self.P
    b = self.i % self.nbuf
    self.i += 1
    ks, kb = (self.name, "st", b), (self.name, "wb", b)
    P.dma(self.st[b][:, :kc, :], src, reads=[], writes=[ks])
    P.copy(self.wb[b][:, :kc, :], self.st[b][:, :kc, :], reads=[ks], writes=[kb], eng=self.cast_eng)
    return self.wb[b], kb


WStream.load_kc = _ws_load_kc


def pair_tokens(cfg, arrs, half, axis=-1):
    NC, NL = cfg.NC, cfg.NL
    a = arrs
    idx = np.r_[0:NC, NC + half * NL:NC + (half + 1) * NL]
    return np.take(a, idx, axis=axis)


def run_L3a(cfg, inp, r2):
    FT = cfg.FT
    NU = cfg.S5W // 128
    gw = tile_w(inp["s5_glu_w"][0], NU)
    gb = fm(inp["s5_glu_b"][0])
    wo = tile_w(inp["ab_w_out"][0], FT)
    maps = []
    for core in range(NCORES):
        b, half = core // 2, core % 2
        ra, rb = r2[2 * b], r2[2 * b + 1]
        yFf = np.concatenate([ra["yF"], rb["yF"]], 0).reshape(NU, 128, cfg.LS)
        yBf = np.concatenate([seg_rev(ra["yB"], cfg.NC), seg_rev(rb["yB"], cfg.NC)], 0).reshape(NU, 128, cfg.LS)
        maps.append({"aT": r2[core]["aT"], "yF": np.ascontiguousarray(pair_tokens(cfg, yFf, half)),
                     "yB": np.ascontiguousarray(pair_tokens(cfg, yBf, half)), "glu_w": gw, "glu_b": gb, "w_out": wo})
    res = launch(lambda P: body_L3a(P, cfg), maps, ['glu_w', 'glu_b', 'w_out'], VW)
    return [res[c]["yT"] for c in range(NCORES)]


def resid_phase(P, cfg, xT, yT, xoT, hoT, mods_g, gS_g, gate_j, mods_h, gS_h, h_sel, ones, hT_res=None):
    FT = cfg.FT
    sc_g = mod_scalars(P, mods_g, gS_g, cfg)
    sc_h = sc_g if (mods_h is mods_g) else (mod_scalars(P, mods_h, gS_h, cfg) if h_sel is not None else None)
    blks = col_blocks(cfg, 256)
    TM = max(w for _, w, _ in blks)
    ys_r = Rot(P, "ys", [128, FT, TM], F32, 2)
    xs_r = Rot(P, "xr", [128, FT, TM], F32, 2)
    hb_r = Rot(P, "hb", [128, FT, TM], BF16, 2) if (h_sel is not None and hT_res is None) else None
    sq_r = Rot(P, "sqr", [128, TM], F32, 2)
    tmp_r = Rot(P, "tmr", [128, TM], F32, 2)
    rs_t = P.sbuf([128, TM], F32, "rs_t")
    rstd_t = P.sbuf([128, TM], F32, "rstd_t")
    for (s, T, lat) in blks:
        ys, ky = ys_r.next()
        xs, kx = xs_r.next()
        P.dma(ys[:, :, :T], yT.rearrange("f p n -> p f n")[:, :, s:s + T], [], [ky])
        P.dma(xs[:, :, :T], xT[:, :, s:s + T], [], [kx])
        norm_block(P, cfg, ys, ky, T, ones, sq_r, rs_t, rstd_t, FT, cfg.D)
        for ft in range(FT):
            tmp, kt = tmp_r.next()
            P.tt(tmp[:, :T], ys[:, ft, :T], rstd_t[:, :T], ALU.mult, [ky, "rstd_t"], [kt])
            P.stt(xs[:, ft, :T], tmp[:, :T], sc_g[:, gate_j, ft, lat:lat + 1], xs[:, ft, :T], ALU.mult, ALU.add,
                  [kt, ("modsc", gate_j, lat), kx], [kx])
        P.dma(xoT[:, :, s:s + T], xs[:, :, :T], [kx], [("xo", s)])
        if h_sel is not None:
            norm_block(P, cfg, xs, kx, T, ones, sq_r, rs_t, rstd_t, FT, cfg.D)
            bidx = 0 if h_sel == 0 else 3
            if hT_res is None:
                hb, kh = hb_r.next()
            for ft in range(FT):
                tmp, kt = tmp_r.next()
                P.tt(tmp[:, :T], xs[:, ft, :T], rstd_t[:, :T], ALU.mult, [kx, "rstd_t"], [kt])
                if hT_res is None:
                    P.act(hb[:, ft, :T], tmp[:, :T], AF.Identity, [kt, ("modsc", h_sel, lat), "mods"], [kh],
                          scale=sc_h[:, h_sel, ft, lat:lat + 1], bias=mods_h[:, bidx, ft, lat:lat + 1])
                else:
                    P.act(hT_res[:, ft, s:s + T], tmp[:, :T], AF.Identity, [kt, ("modsc", h_sel, lat), "mods"],
                          colkeys("hT", s, T), scale=sc_h[:, h_sel, ft, lat:lat + 1], bias=mods_h[:, bidx, ft, lat:lat + 1])
            if hT_res is None:
                P.dma(hoT[:, :, s:s + T], hb[:, :, :T], [kh], [("ho", s)])


def body_R(P, cfg, gate_j, h_sel, yname="yT"):
    FT, NT = cfg.FT, cfg.NT
    xT = P.dram("xT", [128, FT, NT], F32, "ExternalInput")
    yT = P.dram(yname, [FT, 128, NT], F32, "ExternalInput")
    mods_d = P.dram("mods", [128, 6, FT, 2], F32, "ExternalInput")
    gS_d = P.dram("gS", [128, 4, FT], F32, "ExternalInput", shared=True)
    xoT = P.dram("xoT", [128, FT, NT], F32, "ExternalOutput")
    hoT = P.dram("hoT", [128, FT, NT], BF16, "ExternalOutput") if h_sel is not None else None
    mods = P.sbuf([128, 6, FT, 2], F32, "mods")
    gS = P.sbuf([128, 4, FT], F32, "gS")
    ones = P.sbuf([128, 128], F32, "ones")
    P.dma(mods[:], mods_d, [], ["mods"])
    P.dma(gS[:], gS_d, [], ["gS"])
    P.memset(ones[:], 1.0, ["ones"])
    resid_phase(P, cfg, xT, yT, xoT, hoT, mods, gS, gate_j, mods, gS, h_sel, ones)


def gS_of(cfg, inp, l):
    return np.ascontiguousarray(inp["norm_g"][l].reshape(4, cfg.FT, 128).transpose(2, 0, 1))


def run_R(cfg, inp, mod, l, xTs, yTs, gate_j, h_sel):
    maps = []
    for core in range(NCORES):
        maps.append({"xT": xTs[core], "yT": yTs[core], "mods": mods_for(cfg, mod, l, core // 2), "gS": gS_of(cfg, inp, l)})
    res = launch(lambda P: body_R(P, cfg, gate_j, h_sel), maps, ['gS'], 1)
    return [res[c] for c in range(NCORES)]


def arena_view(arena, off, shape, dtype):
    n = 1
    for d in shape[1:]:
        n *= d
    nbytes = n * (2 if dtype == BF16 else 4)
    assert off % 4 == 0 and nbytes % 4 == 0
    a = arena[:, off // 4:(off + nbytes) // 4]
    if dtype == BF16:
        a = a.bitcast(BF16)
    if len(shape) == 3:
        a = a.rearrange("p (a b) -> p a b", a=shape[1])
    return a, off + nbytes


def ffn_layout(cfg):
    NC, NL = cfg.NC, cfg.NL
    segs = []
    hl = 0
    if NC:
        segs.append(("c", 0, NC, 0))
        hl = NC + 2
    NLa = (NL - NC) // 2
    mbs = [[], []]

    def pieces(h0, w, o0):
        npc = -(-w // 510)
        base = -(-w // npc)
        out = []
        a = 0
        while a < w:
            ww = min(base, w - a)
            out.append((h0 + a, ww, o0 + a))
            a += ww
        return out
    if NC:
        mbs[0] += pieces(0, NC, 0)
    mbs[0] += pieces(hl, NLa, NC)
    mbs[1] += pieces(hl + NLa, NL - NLa, NC + NLa)
    return mbs


def body_F(P, cfg):
    FT, NT, FC = cfg.FT, cfg.NT, cfg.FC
    WH = NT + (4 if cfg.NC else 2)
    h2h = P.dram("h2h", [128, FT, WH], BF16, "ExternalInput")
    wup = P.dram("w_up", [2 * FC, 128, FT, 128], F32, "ExternalInput", shared=True)
    cw_d = P.dram("cw", [128, 2 * FC, 4], F32, "ExternalInput", shared=True)
    wdn = P.dram("w_dn", [FT, 128, FC, 128], F32, "ExternalInput", shared=True)
    fT = P.dram("fT", [FT, 128, NT], F32, "ExternalOutput")
    cw = P.sbuf([128, 2 * FC, 4], F32, "cw")
    P.dma(cw[:], cw_d, [], ["cw"])
    mbs = ffn_layout(cfg)
    VM = max(sum(w for _, w, _ in mb) for mb in mbs)
    WM = max(sum(w + 2 for _, w, _ in mb) for mb in mbs)
    PW = max(w for mb in mbs for _, w, _ in mb) + 2
    hid = P.sbuf([128, FC, VM], BF16, "hid")
    up_bytes = FT * WM * 2 + 4 * FT * 128 * 4 + 4 * FT * 128 * 2 + 8 * PW * 4
    dn_bytes = 2 * FC * 128 * 4 + 2 * FC * 128 * 2 + 3 * 512 * 4
    arena = P.sbuf([128, (max(up_bytes, dn_bytes) + 3) // 4], F32, "arena")
    off = 0
    h2s, off = arena_view(arena, off, [128, FT, WM], BF16)
    ust, uwb = [], []
    for i in range(4):
        v, off = arena_view(arena, off, [128, FT, 128], F32)
        ust.append(v)
    for i in range(4):
        v, off = arena_view(arena, off, [128, FT, 128], BF16)
        uwb.append(v)
    ctile = []
    for i in range(8):
        v, off = arena_view(arena, off, [128, PW], F32)
        ctile.append(v)
    off = 0
    dst_, dwb = [], []
    for i in range(2):
        v, off = arena_view(arena, off, [128, FC, 128], F32)
        dst_.append(v)
    for i in range(2):
        v, off = arena_view(arena, off, [128, FC, 128], BF16)
        dwb.append(v)
    etile = []
    for i in range(3):
        v, off = arena_view(arena, off, [128, 512], F32)
        etile.append(v)
    uit = [0]
    dit = [0]
    eit = [0]
    for mi, mb in enumerate(mbs):
        lcs = []
        lc = 0
        for (h0, w, o0) in mb:
            P.dma(h2s[:, :, lc:lc + w + 2], h2h[:, :, h0:h0 + w + 2], [], [("h2s", mi, lc)])
            lcs.append(lc)
            lc += w + 2
        def load_up(c):
            wt_ = []
            for nt in (c, FC + c):
                b = uit[0] % 4
                uit[0] += 1
                P.dma(ust[b], wup[nt], [], [("ust", mi, b)])
                P.copy(uwb[b], ust[b], [("ust", mi, b)], [("uwb", mi, b)], eng="pool")
                wt_.append((uwb[b], ("uwb", mi, b)))
            return wt_
        nxt_up = load_up(0)
        for c in range(FC):
            wt = nxt_up
            if c + 1 < FC:
                nxt_up = load_up(c + 1)
            vo = 0
            for pi, (h0, w, o0) in enumerate(mb):
                lc = lcs[pi]
                N = w + 2
                pa, pv = P.bank(), P.bank()
                for (wb_, kw), pbk in zip(wt, (pa, pv)):
                    for kc in range(FT):
                        P.mm(P.psb[pbk][:, :N], wb_[:, kc, :], h2s[:, kc, lc:lc + N], kc == 0, kc == FT - 1,
                             [kw, ("h2s", mi, lc)], [("ps", pbk)])
                par = (c * len(mb) + pi) % 2
                Ua, Uv, ca, cv = (ctile[par * 4 + i] for i in range(4))
                ka, kv, kca, kcv = (("ct", mi, par, i) for i in range(4))
                P.copy(Ua[:, :N], P.psb[pa][:, :N], [("ps", pa)], [ka], eng="act")
                P.copy(Uv[:, :N], P.psb[pv][:, :N], [("ps", pv)], [kv], eng="dve")
                for (U, ku, cc, kc_, ch) in ((Ua, ka, ca, kca, c), (Uv, kv, cv, kcv, FC + c)):
                    P.act(cc[:, :w], U[:, 1:w + 1], AF.Identity, [ku, "cw"], [kc_], scale=cw[:, ch, 1:2], bias=cw[:, ch, 3:4])
                    P.stt(cc[:, :w], U[:, 0:w], cw[:, ch, 0:1], cc[:, :w], ALU.mult, ALU.add, [ku, "cw", kc_], [kc_])
                    P.stt(cc[:, :w], U[:, 2:w + 2], cw[:, ch, 2:3], cc[:, :w], ALU.mult, ALU.add, [ku, "cw", kc_], [kc_])
                P.act(ca[:, :w], ca[:, :w], AF.Silu, [kca], [kca])
                P.tt(hid[:, c, vo:vo + w], ca[:, :w], cv[:, :w], ALU.mult, [kca, kcv], [("hid", vo)], eng="pool")
                vo += w
        P.barrier()
        def load_dn(nt):
            b = dit[0] % 2
            dit[0] += 1
            P.dma(dst_[b], wdn[nt], [], [("dst", mi, b)])
            P.copy(dwb[b], dst_[b], [("dst", mi, b)], [("dwb", mi, b)], eng="pool")
            return b
        nxt_dn = load_dn(0)
        for nt in range(FT):
            b = nxt_dn
            if nt + 1 < FT:
                nxt_dn = load_dn(nt + 1)
            vo = 0
            for (h0, w, o0) in mb:
                pbk = P.bank()
                for kc in range(FC):
                    P.mm(P.psb[pbk][:, :w], dwb[b][:, kc, :], hid[:, kc, vo:vo + w], kc == 0, kc == FC - 1,
                         [("dwb", mi, b), ("hid", vo)], [("ps", pbk)])
                e = eit[0] % 3
                eit[0] += 1
                P.copy(etile[e][:, :w], P.psb[pbk][:, :w], [("ps", pbk)], [("et", mi, e)], eng="act" if e % 2 else "dve")
                P.dma(fT[nt][:, o0:o0 + w], etile[e][:, :w], [("et", mi, e)], [("fT", nt, o0)])
                vo += w
        P.barrier()


def halo_h2(cfg, hs, b, half):
    NC, NL = cfg.NC, cfg.NL
    me, other = hs[2 * b + half], hs[2 * b + 1 - half]
    z = np.zeros(me.shape[:2] + (1,), me.dtype)
    parts = []
    if NC:
        parts += [z, me[:, :, :NC], z]
    left = z if half == 0 else other[:, :, NT_last(cfg)]
    right = other[:, :, NC:NC + 1] if half == 0 else z
    parts += [left, me[:, :, NC:], right]
    return np.ascontiguousarray(np.concatenate(parts, 2))


def NT_last(cfg):
    return slice(cfg.NT - 1, cfg.NT)


def run_F(cfg, inp, l, hs):
    FT, FC = cfg.FT, cfg.FC
    wup = tile_w(inp["ffn_w_up"][l], FT)
    wdn = tile_w(inp["ffn_w_down"][l], FC)
    cwb = np.concatenate([inp["ffn_conv_w"][l], inp["ffn_conv_b"][l][None]], 0)
    cw = np.ascontiguousarray(cwb.T.reshape(2 * FC, 128, 4).transpose(1, 0, 2))
    maps = []
    for core in range(NCORES):
        maps.append({"h2h": halo_h2(cfg, hs, core // 2, core % 2), "w_up": wup, "cw": cw, "w_dn": wdn})
    res = launch(lambda P: body_F(P, cfg), maps, ['w_up', 'cw', 'w_dn'], VW)
    return [res[c]["fT"] for c in range(NCORES)]


def body_L4b(P, cfg):
    FT, NT = cfg.FT, cfg.NT
    xT = P.dram("xT", [128, FT, NT], F32, "ExternalInput")
    yT = P.dram("fT", [FT, 128, NT], F32, "ExternalInput")
    md0 = P.dram("mods0", [128, 6, FT, 2], F32, "ExternalInput")
    g0 = P.dram("gS0", [128, 4, FT], F32, "ExternalInput", shared=True)
    md1 = P.dram("mods1", [128, 6, FT, 2], F32, "ExternalInput")
    g1 = P.dram("gS1", [128, 4, FT], F32, "ExternalInput", shared=True)
    lbl_d = P.dram("lbl", [128, 2, 2, FT], F32, "ExternalInput", shared=True)
    wci = P.dram("c_w_in", [5 * FT, 128, FT, 128], F32, "ExternalInput", shared=True)
    xoT = P.dram("xoT", [128, FT, NT], F32, "ExternalOutput")
    qh = P.dram("qh", [FT, 128, NT], BF16, "ExternalOutput")
    fF = P.dram("fF", [FT, 128, NT], F32, "ExternalOutput")
    fB = P.dram("fB", [FT, 128, NT], F32, "ExternalOutput")
    vh = P.dram("vh", [FT, 128, NT], BF16, "ExternalOutput")
    gh = P.dram("gh", [FT, 128, NT], F32, "ExternalOutput")
    mods0 = P.sbuf([128, 6, FT, 2], F32, "mods0"); gS0 = P.sbuf([128, 4, FT], F32, "gS0")
    mods1 = P.sbuf([128, 6, FT, 2], F32, "mods1"); gS1 = P.sbuf([128, 4, FT], F32, "gS1")
    lbl = P.sbuf([128, 2, 2, FT], F32, "lbl"); lb = P.sbuf([128, 2, FT], F32, "lb"); oml = P.sbuf([128, 2, FT], F32, "oml")
    ones = P.sbuf([128, 128], F32, "ones")
    P.dma(mods0[:], md0, [], ["mods"]); P.dma(gS0[:], g0, [], ["gS"])
    P.dma(mods1[:], md1, [], ["mods"]); P.dma(gS1[:], g1, [], ["gS"])
    P.dma(lbl[:], lbl_d, [], ["lbl"])
    P.memset(ones[:], 1.0, ["ones"])
    P.tt(lb[:], lbl[:, 1], lbl[:, 0], ALU.subtract, ["lbl"], ["lb"])
    P.act(lb[:], lb[:], AF.Sigmoid, ["lb"], ["lb"])
    P.ts(oml[:], lb[:], -1.0, ALU.mult, ["lb"], ["oml"], s2=1.0, op1=ALU.add)
    hT = P.sbuf([128, FT, NT], BF16, "hT")
    resid_phase(P, cfg, xT, yT, xoT, None, mods0, gS0, 3, mods1, gS1, 0, ones, hT_res=hT)
    blks = col_blocks(cfg)
    TM = max(w for _, w, _ in blks)
    ws = WStream(P, "wci", FT)
    eo_r = Rot(P, "eo", [128, TM], F32, 3)
    eb_r = Rot(P, "eb", [128, TM], BF16, 3)

    def evac(nt, blk, pb):
        s, T, lat = blk
        kind, j = nt // FT, nt % FT
        ps = P.psb[pb][:, :T]
        if kind == 3:
            eb, ke = eb_r.next()
            P.copy(eb[:, :T], ps, [("ps", pb)], [ke], eng="act")
            P.dma(vh[j][:, s:s + T], eb[:, :T], [ke], [("o", nt, s)])
            return
        if kind == 0:
            eb, ke = eb_r.next()
            P.act(eb[:, :T], ps, AF.Silu, [("ps", pb)], [ke])
            P.dma(qh[j][:, s:s + T], eb[:, :T], [ke], [("o", nt, s)])
            return
        eo, ke = eo_r.next()
        if kind == 0:
            P.act(eo[:, :T], ps, AF.Silu, [("ps", pb)], [ke])
            dst = qh
        elif kind in (1, 2):
            d = kind - 1
            P.act(eo[:, :T], ps, AF.Sigmoid, [("ps", pb)], [ke])
            P.ts(eo[:, :T], eo[:, :T], oml[:, d, j:j + 1], ALU.mult, [ke, "oml", "lb"], [ke], s2=lb[:, d, j:j + 1], op1=ALU.add)
            dst = fF if d == 0 else fB
        else:
            P.copy(eo[:, :T], ps, [("ps", pb)], [ke])
            dst = gh
        P.dma(dst[j][:, s:s + T], eo[:, :T], [ke], [("o", nt, s)])
    gemm_ws(P, ws, wci, range(5 * FT), FT, hT, "hT", blks, evac)


def run_L4b(cfg, inp, mod, xTs, fTs):
    FT = cfg.FT
    wci = tile_w(inp["c_w_in"][0], FT)
    lbl = np.ascontiguousarray(inp["hgrn_lb_logits"].reshape(2, 2, FT, 128).transpose(3, 0, 1, 2))
    maps = []
    for core in range(NCORES):
        b = core // 2
        maps.append({"xT": xTs[core], "yT": fTs[core], "mods0": mods_for(cfg, mod, 0, b), "gS0": gS_of(cfg, inp, 0),
                     "mods1": mods_for(cfg, mod, 1, b), "gS1": gS_of(cfg, inp, 1), "lbl": lbl, "c_w_in": wci})
    res = launch(lambda P: body_L4b(P, cfg), maps, ['gS0', 'gS1', 'lbl', 'c_w_in'], VW)
    return [res[c] for c in range(NCORES)]


def body_L5(P, cfg):
    LS, NC, SEQ = cfg.LS, cfg.NC, cfg.SEQ
    HC = cfg.HH // 2
    NTL = LS // 128
    qd = [P.dram("q%s" % x, [HC, 128, LS], BF16, "ExternalInput") for x in "FB"]
    fd = [P.dram("f%s" % x, [HC, 128, LS], F32, "ExternalInput") for x in "FB"]
    Vd = [P.dram("V%s" % x, [128, NTL, HC, 128], BF16, "ExternalInput") for x in "FB"]
    od = [P.dram("o%s" % x, [HC, 128, SEQ], F32, "ExternalOutput") for x in "FB"]
    m64_d = P.dram("mask64", [128, 64], F32, "ExternalInput", shared=True)
    cm_d = P.dram("cmask", [128, 512], F32, "ExternalInput", shared=True)
    id_d = P.dram("identb", [128, 128], BF16, "ExternalInput", shared=True)
    mask64 = P.sbuf([128, 64], F32, "mask64"); cmask = P.sbuf([128, 512], F32, "cmask")
    identb = P.sbuf([128, 128], BF16, "identb")
    P.dma(mask64[:], m64_d, [], ["mask64"]); P.dma(cmask[:], cm_d, [], ["cmask"]); P.dma(identb[:], id_d, [], ["identb"])
    blks = [(0, NC, 0)] if NC else []
    s = NC
    while s < LS:
        blks.append((s, min(512, LS - s), 1))
        s += 512
    D2 = range(2)
    Vh = [P.sbuf([128, NTL, 128], BF16, "Vh%d" % d) for d in D2]
    qs = [Rot(P, "qs%d" % d, [128, 512], BF16, 2) for d in D2]
    fs = [Rot(P, "fs%d" % d, [128, 512], F32, 2) for d in D2]
    lf = [P.sbuf([128, 512], F32, "lf%d" % d) for d in D2]
    bb = [P.sbuf([128, 512], F32, "bb%d" % d) for d in D2]
    eb = [Rot(P, "eb%d" % d, [128, 512], F32, 2) for d in D2]
    enb = [P.sbuf([128, 512], F32, "enb%d" % d) for d in D2]
    Qt = [Rot(P, "Qt%d" % d, [128, 512], BF16, 2) for d in D2]
    Kt = [Rot(P, "Kt%d" % d, [128, 512], BF16, 2) for d in D2]
    KtT = [Rot(P, "KtT%d" % d, [128, 128], BF16, 2) for d in D2]
    ATm = [Rot(P, "ATm%d" % d, [128, 64], BF16, 2) for d in D2]
    S32 = [P.sbuf([128, 128], F32, "S32_%d" % d) for d in D2]
    Sb = [P.sbuf([128, 128], BF16, "Sb_%d" % d) for d in D2]
    tmpS = [Rot(P, "tmpS%d" % d, [128, 128], F32, 2) for d in D2]
    osb = [Rot(P, "osb%d" % d, [128, 512], F32, 2) for d in D2]
    for h in range(HC):
        for d in D2:
            P.dma(Vh[d][:], Vd[d][:, :, h, :], [], [("Vh", d)])
            P.memset(S32[d][:], 0.0, [("S32", d)])
            P.memset(Sb[d][:], 0.0, [("Sb", d)])
        for (s, T, lat) in blks:
            cur = []
            for d in D2:
                q_, kq = qs[d].next()
                f_, kf = fs[d].next()
                P.dma(q_[:, :T], qd[d][h][:, s:s + T], [], [kq])
                P.dma(f_[:, :T], fd[d][h][:, s:s + T], [], [kf])
                P.act(lf[d][:, :T], f_[:, :T], AF.Ln, [kf], [("lf", d)])
                P.ts(f_[:, :T], f_[:, :T], -1.0, ALU.mult, [kf], [kf], s2=1.0, op1=ALU.add, eng="pool")
                P.op("dve", lambda e, d=d, T=T: e.tensor_tensor_scan(out=bb[d][:, :T], data0=cmask[:, :T], data1=lf[d][:, :T],
                                                                    initial=0.0, op0=ALU.mult, op1=ALU.add),
                     reads=["cmask", ("lf", d)], writes=[("bb", d)])
                e_, ke = eb[d].next()
                P.act(e_[:, :T], bb[d][:, :T], AF.Exp, [("bb", d)], [ke])
                P.act(enb[d][:, :T], bb[d][:, :T], AF.Exp, [("bb", d)], [("enb", d)], scale=-1.0)
                Q_, kQ = Qt[d].next()
                K_, kK = Kt[d].next()
                P.tt(Q_[:, :T], q_[:, :T], e_[:, :T], ALU.mult, [kq, ke], [kQ])
                P.tt(K_[:, :T], f_[:, :T], enb[d][:, :T], ALU.mult, [kf, ("enb", d)], [kK], eng="pool")
                cur.append((Q_, kQ, K_, kK, e_, ke))
            for t128 in range(T // 128):
                tile = (s + t128 * 128) // 128
                ktt = []
                for d in D2:
                    Q_, kQ, K_, kK, e_, ke = cur[d]
                    pv = P.psb[6 + d][:].bitcast(BF16)
                    P.op("pe", lambda e, pv=pv, K_=K_, t128=t128: e.transpose(out=pv[:, 0:128], in_=K_[:, t128 * 128:(t128 + 1) * 128],
                                                                         identity=identb[:]),
                         reads=[kK, "identb"], writes=[("ps", 6 + d)])
                    kt_, kkt = KtT[d].next()
                    P.copy(kt_[:], pv[:, 0:128], [("ps", 6 + d)], [kkt], eng="act")
                    ktt.append((kt_, kkt))
                for c2 in range(2):
                    cc = t128 * 128 + c2 * 64
                    R = slice(64 * c2, 64 * c2 + 64)
                    for d in D2:
                        Q_, kQ, K_, kK, e_, ke = cur[d]
                        kt_, kkt = ktt[d]
                        if lat:
                            P.mm(P.psb[d][R, 0:64], K_[:, cc:cc + 64], Q_[:, cc:cc + 64], True, True, [kK, kQ], [("ps", d)])
                            am, kam = ATm[d].next()
                            P.tt(am[R, :], P.psb[d][R, 0:64], mask64[R, :], ALU.mult, [("ps", d), "mask64"], [kam])
                            P.mm(P.psb[2 + d][:, cc:cc + 64], Vh[d][R, tile, :], am[R, :], True, False, [("Vh", d), kam],
                                 [("ps", 2 + d)])
                            P.mm(P.psb[2 + d][:, cc:cc + 64], Sb[d][:], Q_[:, cc:cc + 64], False, True, [("Sb", d), kQ],
                                 [("ps", 2 + d)])
                        P.mm(P.psb[4 + d][:, 0:128], kt_[R, :], Vh[d][R, tile, :], True, True, [kkt, ("Vh", d)], [("ps", 4 + d)])
                        ts_, kts = tmpS[d].next()
                        P.tt(ts_[:], S32[d][:], P.psb[4 + d][:, 0:128], ALU.add, [("S32", d), ("ps", 4 + d)], [kts])
                        P.act(S32[d][:], ts_[:], AF.Identity, [kts, ke], [("S32", d)], scale=e_[:, cc + 63:cc + 64])
                        P.act(Sb[d][:], ts_[:], AF.Identity, [kts, ke], [("Sb", d)], scale=e_[:, cc + 63:cc + 64])
            if lat:
                for d in D2:
                    o_, ko = osb[d].next()
                    P.copy(o_[:, :T], P.psb[2 + d][:, :T], [("ps", 2 + d)], [ko], eng="dve" if d else "act")
                    P.dma(od[d][h][:, s - NC:s - NC + T], o_[:, :T], [ko], [("o", d, h, s)])


def run_L5(cfg, r4):
    NC, NL, LS, FT = cfg.NC, cfg.NL, cfg.LS, cfg.FT
    HC = cfg.HH // 2
    m64 = np.zeros((128, 64), np.float32)
    for p in range(128):
        m64[p, (p % 64):] = 1.0
    cm = np.ones((128, 512), np.float32)
    cm[:, ::64] = 0.0
    ident = np.eye(128, dtype=np.float32).astype(NPBF)
    maps = []
    for core in range(NCORES):
        b, hh = core // 2, core % 2
        ra, rb = r4[2 * b], r4[2 * b + 1]
        hs = slice(hh * HC, (hh + 1) * HC)

        def full(nm):
            return np.concatenate([ra[nm][hs][:, :, :NC], ra[nm][hs][:, :, NC:], rb[nm][hs][:, :, NC:]], 2)
        q = full("qh")
        v = full("vh")

        def vtok(vv):
            return np.ascontiguousarray(vv.transpose(2, 0, 1).reshape(LS // 128, 128, HC, 128).transpose(1, 0, 2, 3))
        maps.append({"qF": np.ascontiguousarray(q), "qB": np.ascontiguousarray(seg_rev(q, NC)),
                     "fF": np.ascontiguousarray(full("fF")), "fB": np.ascontiguousarray(seg_rev(full("fB"), NC)),
                     "VF": vtok(v), "VB": vtok(seg_rev(v, NC)), "mask64": m64, "cmask": cm, "identb": ident})
    res = launch(lambda P: body_L5(P, cfg), maps, ['mask64', 'cmask', 'identb'], 1)
    return [res[c] for c in range(NCORES)]


def body_L6a(P, cfg):
    FT, NT = cfg.FT, cfg.NT
    oFd = P.dram("oF", [FT, 128, NT], F32, "ExternalInput")
    oBd = P.dram("oB", [FT, 128, NT], F32, "ExternalInput")
    ghd = P.dram("gh", [FT, 128, NT], F32, "ExternalInput")
    hgn_d = P.dram("hgn", [128, 1], F32, "ExternalInput", shared=True)
    wo = P.dram("c_w_out", [FT, 128, FT, 128], F32, "ExternalInput", shared=True)
    yT = P.dram("yT", [FT, 128, NT], F32, "ExternalOutput")
    hgn = P.sbuf([128, 1], F32, "hgn")
    ones = P.sbuf([128, 128], F32, "ones")
    P.dma(hgn[:], hgn_d, [], ["hgn"])
    P.memset(ones[:], 1.0, ["ones"])
    mixT = P.sbuf([128, FT, NT], BF16, "mixT")
    blks = col_blocks(cfg)
    TM = max(w for _, w, _ in blks)
    oa_r = Rot(P, "oa", [128, TM], F32, 2)
    ob_r = Rot(P, "ob", [128, TM], F32, 2)
    g_r = Rot(P, "gg", [128, TM], F32, 2)
    sq_r = Rot(P, "sq", [128, TM], F32, 2)
    rs = P.sbuf([128, TM], F32, "rs")
    rstd = P.sbuf([128, TM], F32, "rstd")
    for (s, T, lat) in blks:
        for j in range(FT):
            oa, ka = oa_r.next()
            ob, kb = ob_r.next()
            gg, kg = g_r.next()
            sq, ks = sq_r.next()
            P.dma(oa[:, :T], oFd[j][:, s:s + T], [], [ka])
            P.dma(ob[:, :T], oBd[j][:, s:s + T], [], [kb])
            P.dma(gg[:, :T], ghd[j][:, s:s + T], [], [kg])
            P.tt(oa[:, :T], oa[:, :T], ob[:, :T], ALU.add, [ka, kb], [ka], eng="pool")
            P.act(sq[:, :T], oa[:, :T], AF.Square, [ka], [ks])
            pb = P.bank()
            P.mm(P.psb[pb][:, :T], ones[:], sq[:, :T], True, True, [ks, "ones"], [("ps", pb)])
            P.act(rs[:, :T], P.psb[pb][:, :T], AF.Sqrt, [("ps", pb)], ["rs"], scale=1.0 / 128, bias=EPS)
            P.op("dve", lambda e, T=T: e.reciprocal(out=rstd[:, :T], in_=rs[:, :T]), reads=["rs"], writes=["rstd"])
            P.stt(oa[:, :T], oa[:, :T], hgn[:, 0:1], rstd[:, :T], ALU.mult, ALU.mult, [ka, "hgn", "rstd"], [ka])
            P.act(gg[:, :T], gg[:, :T], AF.Silu, [kg], [kg])
            P.tt(mixT[:, j, s:s + T], oa[:, :T], gg[:, :T], ALU.mult, [ka, kg], colkeys("mixT", s, T), eng="pool")
    ws = WStream(P, "wco", FT)
    yo_r = Rot(P, "yo", [128, TM], F32, 3)

    def evac(nt, blk, pb):
        s, T, lat = blk
        yo, ko = yo_r.next()
        P.copy(yo[:, :T], P.psb[pb][:, :T], [("ps", pb)], [ko], eng="act" if (nt + s // 128) % 2 else "dve")
        P.dma(yT[nt][:, s:s + T], yo[:, :T], [ko], [("yT", nt, s)])
    gemm_ws(P, ws, wo, range(FT), FT, mixT, "mixT", blks, evac)


def run_L6a(cfgL, cfg, inp, r4, r5):
    FT, NC, NL = cfg.FT, cfg.NC, cfg.NL
    wo = tile_w(inp["c_w_out"][0], FT)
    hgn = np.ascontiguousarray(inp["hgrn_norm"][0].reshape(128, 1))
    maps = []
    for core in range(NCORES):
        b, half = core // 2, core % 2
        ra, rb = r5[2 * b], r5[2 * b + 1]
        oF = np.concatenate([ra["oF"], rb["oF"]], 0)
        oB = np.concatenate([ra["oB"], rb["oB"]], 0)[:, :, ::-1]
        sl = slice(half * NL, (half + 1) * NL)
        maps.append({"oF": np.ascontiguousarray(oF[:, :, sl]), "oB": np.ascontiguousarray(oB[:, :, sl]),
                     "gh": np.ascontiguousarray(r4[core]["gh"][:, :, NC:]), "hgn": hgn, "c_w_out": wo})
    res = launch(lambda P: body_L6a(P, cfgL), maps, ['hgn', 'c_w_out'], VW)
    return [res[c]["yT"] for c in range(NCORES)]


def fused_body(P, bodies):
    outer = P.es
    for i, b in enumerate(bodies):
        sub = ExitStack()
        P.es = sub
        b(P)
        P.flush()
        sub.close()
    P.es = outer


def run_L3aR(cfg, inp, mod, r2, xTs):
    FT = cfg.FT
    NU = cfg.S5W // 128
    gw = tile_w(inp["s5_glu_w"][0], NU)
    gb = fm(inp["s5_glu_b"][0])
    wo = tile_w(inp["ab_w_out"][0], FT)
    maps = []
    for core in range(NCORES):
        b, half = core // 2, core % 2
        ra, rb = r2[2 * b], r2[2 * b + 1]
        yFf = np.concatenate([ra["yF"], rb["yF"]], 0).reshape(NU, 128, cfg.LS)
        yBf = np.concatenate([seg_rev(ra["yB"], cfg.NC), seg_rev(rb["yB"], cfg.NC)], 0).reshape(NU, 128, cfg.LS)
        maps.append({"aT": r2[core]["aT"], "yF": np.ascontiguousarray(pair_tokens(cfg, yFf, half)),
                     "yB": np.ascontiguousarray(pair_tokens(cfg, yBf, half)), "glu_w": gw, "glu_b": gb, "w_out": wo,
                     "xT": xTs[core], "mods": mods_for(cfg, mod, 0, b), "gS": gS_of(cfg, inp, 0)})
    return launch(lambda P: fused_body(P, [lambda P: body_L3a(P, cfg), lambda P: body_R(P, cfg, 1, 2)]), maps,
                  ["glu_w", "glu_b", "w_out", "gS"], VW, internal=["yT"])


def run_FL4b(cfg, inp, mod, hs, xTs):
    FT, FC = cfg.FT, cfg.FC
    l = 0
    wup = tile_w(inp["ffn_w_up"][l], FT)
    wdn = tile_w(inp["ffn_w_down"][l], FC)
    cwb = np.concatenate([inp["ffn_conv_w"][l], inp["ffn_conv_b"][l][None]], 0)
    cw = np.ascontiguousarray(cwb.T.reshape(2 * FC, 128, 4).transpose(1, 0, 2))
    wci = tile_w(inp["c_w_in"][0], FT)
    lbl = np.ascontiguousarray(inp["hgrn_lb_logits"].reshape(2, 2, FT, 128).transpose(3, 0, 1, 2))
    maps = []
    for core in range(NCORES):
        b = core // 2
        maps.append({"h2h": halo_h2(cfg, hs, b, core % 2), "w_up": wup, "cw": cw, "w_dn": wdn,
                     "xT": xTs[core], "mods0": mods_for(cfg, mod, 0, b), "gS0": gS_of(cfg, inp, 0),
                     "mods1": mods_for(cfg, mod, 1, b), "gS1": gS_of(cfg, inp, 1), "lbl": lbl, "c_w_in": wci})
    return launch(lambda P: fused_body(P, [lambda P: body_F(P, cfg), lambda P: body_L4b(P, cfg)]), maps,
                  ["w_up", "cw", "w_dn", "gS0", "gS1", "lbl", "c_w_in"], VW, internal=["fT"])


def run_L6aR(cfgL, cfg, inp, mod, r4, r5, x1):
    FT, NC, NL = cfg.FT, cfg.NC, cfg.NL
    wo = tile_w(inp["c_w_out"][0], FT)
    hgn = np.ascontiguousarray(inp["hgrn_norm"][0].reshape(128, 1))
    maps = []
    for core in range(NCORES):
        b, half = core // 2, core % 2
        ra, rb = r5[2 * b], r5[2 * b + 1]
        oF = np.concatenate([ra["oF"], rb["oF"]], 0)
        oB = np.concatenate([ra["oB"], rb["oB"]], 0)[:, :, ::-1]
        sl = slice(half * NL, (half + 1) * NL)
        maps.append({"oF": np.ascontiguousarray(oF[:, :, sl]), "oB": np.ascontiguousarray(oB[:, :, sl]),
                     "gh": np.ascontiguousarray(r4[core]["gh"][:, :, NC:]), "hgn": hgn, "c_w_out": wo,
                     "xT": x1[core], "mods": mods_for(cfg, mod, 1, b), "gS": gS_of(cfg, inp, 1)})
    return launch(lambda P: fused_body(P, [lambda P: body_L6a(P, cfgL), lambda P: body_R(P, cfgL, 1, 2)]), maps,
                  ["hgn", "c_w_out", "gS"], VW, internal=["yT"])


def run_FR2(cfgL, inp, mod, hs, xTs):
    FT, FC = cfgL.FT, cfgL.FC
    l = 1
    wup = tile_w(inp["ffn_w_up"][l], FT)
    wdn = tile_w(inp["ffn_w_down"][l], FC)
    cwb = np.concatenate([inp["ffn_conv_w"][l], inp["ffn_conv_b"][l][None]], 0)
    cw = np.ascontiguousarray(cwb.T.reshape(2 * FC, 128, 4).transpose(1, 0, 2))
    maps = []
    for core in range(NCORES):
        b = core // 2
        maps.append({"h2h": halo_h2(cfgL, hs, b, core % 2), "w_up": wup, "cw": cw, "w_dn": wdn,
                     "xT": xTs[core], "mods": mods_for(cfgL, mod, 1, b), "gS": gS_of(cfgL, inp, 1)})
    return launch(lambda P: fused_body(P, [lambda P: body_F(P, cfgL), lambda P: body_R(P, cfgL, 3, None, "fT")]), maps,
                  ["w_up", "cw", "w_dn", "gS"], VW, internal=["fT"])


def forward(cfg, inp):
    import copy
    inp = {k: np.asarray(v) for k, v in inp.items()}
    FT, NC, NL = cfg.FT, cfg.NC, cfg.NL
    cfgL = copy.copy(cfg)
    cfgL.NC = 0
    cfgL.NT = NL
    mod = run_L0(cfg, inp)
    r1 = run_L1(cfg, inp, mod)
    r2 = run_L2(cfg, inp, r1)
    del r1
    xTs = []
    for core in range(NCORES):
        b, half = core // 2, core % 2
        X = np.concatenate([inp["ctx"][b], inp["x"][b, half * NL:(half + 1) * NL]], 0)
        xTs.append(tok_fm(X, FT))
    rr = run_L3aR(cfg, inp, mod, r2, xTs)
    del r2, xTs
    r4 = run_FL4b(cfg, inp, mod, [r["hoT"] for r in rr], [r["xoT"] for r in rr])
    del rr
    r5 = run_L5(cfg, r4)
    x1 = [np.ascontiguousarray(r["xoT"][:, :, NC:]) for r in r4]
    rr = run_L6aR(cfgL, cfg, inp, mod, r4, r5, x1)
    del r4, r5, x1
    ro = run_FR2(cfgL, inp, mod, [r["hoT"] for r in rr], [r["xoT"] for r in rr])
    out = np.zeros((cfg.BATCH, cfg.SEQ, cfg.D), np.float32)
    for core in range(NCORES):
        b, half = core // 2, core % 2
        out[b, half * NL:(half + 1) * NL] = ro[core]["xoT"].transpose(2, 1, 0).reshape(NL, cfg.D)
    return out


def kernel(**inputs):
    return forward(Cfg(), inputs)
```
